# Optimizing a Trainium2 kernel written in Bass

```python
import jax
import jax.numpy as jnp
from jax import lax
import numpy as np

D_MODEL = 1024
BATCH = 8
SEQ = 4096
DEPTH = 4

MEM_LEN = 256

NSA_HEADS = 8
NSA_GROUPS = 2
NSA_HEAD_DIM = 64
CMP_BLOCK = 32
CMP_STRIDE = 16
CMP_HIDDEN = 4 * NSA_HEAD_DIM
SEL_BLOCK = 64
N_SEL = 16
WINDOW = 512
NSA_QBLOCK = 64
FORCE_SCORE = 1e4

GLA_HEADS = 4
GLA_HEAD_DK = 64
GLA_HEAD_DV = 128
GLA_RANK = 16
GLA_TAU = 16.0
GLA_CHUNK = 64

MEM_HEADS = 4
MEM_HEAD_DIM = 128

N_BRANCH = 3
BRANCH_WIDTH = NSA_HEADS * NSA_HEAD_DIM
MLP_HIDDEN = 4 * D_MODEL
ROPE_THETA = 500000.0
ROPE_FRACTION = 4
NORM_EPS = 1e-6

IN_SIZES = (
    NSA_HEADS * NSA_HEAD_DIM,
    NSA_GROUPS * NSA_HEAD_DIM,
    NSA_GROUPS * NSA_HEAD_DIM,
    NSA_GROUPS * NSA_HEAD_DIM,
    NSA_GROUPS * NSA_HEAD_DIM,
    NSA_GROUPS * NSA_HEAD_DIM,
    NSA_GROUPS * NSA_HEAD_DIM,
    NSA_HEADS * 3,
    GLA_HEADS * GLA_HEAD_DK,
    GLA_HEADS * GLA_HEAD_DK,
    GLA_HEADS * GLA_HEAD_DV,
    GLA_HEADS * GLA_HEAD_DV,
    GLA_RANK,
    MEM_HEADS * MEM_HEAD_DIM,
    N_BRANCH * D_MODEL,
)
D_IN = sum(IN_SIZES)

kernel_name = 'nsa_gla_memory_parallel_hybrid'


def rms_norm(x, g):
    xf = x.astype(jnp.float32)
    y = xf * lax.rsqrt(jnp.mean(xf * xf, axis=-1, keepdims=True) + NORM_EPS)
    return (y * g.astype(jnp.float32)).astype(x.dtype)


def split_heads(t, n):
    B, S, _ = t.shape
    return t.reshape(B, S, n, -1).transpose(0, 2, 1, 3)


def partial_rope(x, pos):
    rot = x.shape[-1] // ROPE_FRACTION
    half = rot // 2
    inv_freq = jnp.power(ROPE_THETA, -jnp.arange(half, dtype=jnp.float32) / half)
    ang = pos.astype(jnp.float32)[:, None, :, None] * inv_freq
    cos, sin = jnp.cos(ang), jnp.sin(ang)
    xf = x.astype(jnp.float32)
    x1, x2, rest = xf[..., :half], xf[..., half:rot], xf[..., rot:]
    out = jnp.concatenate([x1 * cos - x2 * sin, x2 * cos + x1 * sin, rest], axis=-1)
    return out.astype(x.dtype)


def masked_softmax(s, mask):
    s = jnp.where(mask, s.astype(jnp.float32), -jnp.inf)
    m = jnp.max(s, axis=-1, keepdims=True)
    m = jnp.where(jnp.isfinite(m), m, 0.0)
    p = jnp.exp(s - m)
    d = jnp.sum(p, axis=-1, keepdims=True)
    return p / jnp.where(d > 0, d, 1.0)


def nsa_attention(q, k_c, v_c, k_s, v_s, k_w, v_w, gate_logits, positions,
                  q_norm, k_norm, cmp_pe, cmp_w1, cmp_w2):
    B, S, _ = q.shape
    H, G, dh, QB = NSA_HEADS, NSA_GROUPS, NSA_HEAD_DIM, NSA_QBLOCK
    R = H // G
    q = partial_rope(rms_norm(split_heads(q, H), q_norm), positions) * (dh ** -0.5)
    k_s = partial_rope(rms_norm(split_heads(k_s, G), k_norm[1]), positions)
    k_w = partial_rope(rms_norm(split_heads(k_w, G), k_norm[2]), positions)
    v_s = split_heads(v_s, G)
    v_w = split_heads(v_w, G)

    n_cmp = (S - CMP_BLOCK) // CMP_STRIDE + 1
    cmp_idx = np.arange(n_cmp)[:, None] * CMP_STRIDE + np.arange(CMP_BLOCK)[None, :]
    cmp_end_np = cmp_idx[:, -1]
    raw = jnp.stack([split_heads(k_c, G), split_heads(v_c, G)])[:, :, :, cmp_idx]
    raw = (raw + cmp_pe[:, None, None, None]).reshape(2, B, G, n_cmp, CMP_BLOCK * dh)
    hid = jax.nn.gelu(jnp.einsum('kbgnf,kfe->kbgne', raw, cmp_w1))
    comp = jnp.einsum('kbgne,ked->kbgnd', hid, cmp_w2)
    k_cmp = partial_rope(rms_norm(comp[0], k_norm[0]), positions[:, cmp_end_np])
    v_cmp = comp[1]
    cmp_end = jnp.asarray(cmp_end_np, jnp.int32)

    n_blk = S // SEL_BLOCK
    c_start = np.arange(n_cmp) * CMP_STRIDE
    b_start = np.arange(n_blk) * SEL_BLOCK
    overlap = jnp.asarray(((c_start[:, None] < b_start[None, :] + SEL_BLOCK)
                           & (b_start[None, :] < c_start[:, None] + CMP_BLOCK)).astype(np.float32))
    n_sel = min(N_SEL, n_blk)
    ks_blk = k_s.reshape(B, G, n_blk, SEL_BLOCK, dh)
    vs_blk = v_s.reshape(B, G, n_blk, SEL_BLOCK, dh)
    kw_pad = jnp.pad(k_w, ((0, 0), (0, 0), (WINDOW, 0), (0, 0)))
    vw_pad = jnp.pad(v_w, ((0, 0), (0, 0), (WINDOW, 0), (0, 0)))

    n_qb = S // QB
    q_blocks = q.reshape(B, H, n_qb, QB, dh).transpose(2, 0, 1, 3, 4).reshape(n_qb, B, G, R, QB, dh)
    gates = jax.nn.sigmoid(gate_logits.astype(jnp.float32)).astype(q.dtype)
    g_blocks = (gates.reshape(B, S, H, 3).transpose(0, 2, 1, 3)
                .reshape(B, H, n_qb, QB, 3).transpose(2, 0, 1, 3, 4).reshape(n_qb, B, G, R, QB, 3))
    gather_blocks = jax.vmap(jax.vmap(lambda blk, ix: blk[ix]))
    blk_ids = jnp.arange(n_blk)

    def query_block(args):
        qi, qb, gb = args
        t = qi * QB + jnp.arange(QB)
        s_c = jnp.einsum('bgrqd,bgnd->bgrqn', qb, k_cmp)
        p_c = masked_softmax(s_c, cmp_end[None, :] <= t[:, None])
        o_c = jnp.einsum('bgrqn,bgnd->bgrqd', p_c.astype(v_cmp.dtype), v_cmp)
        imp = jnp.einsum('bgqn,nj->bgqj', jnp.sum(p_c, axis=2), overlap)
        causal = (blk_ids * SEL_BLOCK)[None, :] <= t[:, None]
        cur = (t // SEL_BLOCK)[:, None]
        forced = causal & ((blk_ids[None, :] == 0) | (blk_ids[None, :] == cur)
                           | (blk_ids[None, :] == cur - 1))
        score = jnp.where(forced, FORCE_SCORE, jnp.where(causal, imp, -FORCE_SCORE))
        _, idx = lax.top_k(score, n_sel)
        k_g = gather_blocks(ks_blk, idx)
        v_g = gather_blocks(vs_blk, idx)
        kpos = idx[..., None] * SEL_BLOCK + jnp.arange(SEL_BLOCK)
        mask_s = (kpos <= t[:, None, None])[:, :, None].reshape(B, G, 1, QB, n_sel * SEL_BLOCK)
        s_s = jnp.einsum('bgrqd,bgqnld->bgrqnl', qb, k_g).reshape(B, G, R, QB, n_sel * SEL_BLOCK)
        p_s = masked_softmax(s_s, mask_s).reshape(B, G, R, QB, n_sel, SEL_BLOCK)
        o_s = jnp.einsum('bgrqnl,bgqnld->bgrqd', p_s.astype(v_g.dtype), v_g)
        kwb = lax.dynamic_slice_in_dim(kw_pad, qi * QB, WINDOW + QB, axis=2)
        vwb = lax.dynamic_slice_in_dim(vw_pad, qi * QB, WINDOW + QB, axis=2)
        kp = qi * QB - WINDOW + jnp.arange(WINDOW + QB)
        mask_w = ((kp[None, :] <= t[:, None]) & (kp[None, :] > t[:, None] - WINDOW)
                  & (kp[None, :] >= 0))
        p_w = masked_softmax(jnp.einsum('bgrqd,bgkd->bgrqk', qb, kwb), mask_w)
        o_w = jnp.einsum('bgrqk,bgkd->bgrqd', p_w.astype(vwb.dtype), vwb)
        return gb[..., 0:1] * o_c + gb[..., 1:2] * o_s + gb[..., 2:3] * o_w

    out = lax.map(query_block, (jnp.arange(n_qb), q_blocks, g_blocks))
    return out.reshape(n_qb, B, H, QB, dh).transpose(1, 0, 3, 2, 4).reshape(B, S, H * dh)


def gated_linear_attention(q, k, v, r, g_low, w_gate, b_gate, norm_g):
    B, S, _ = q.shape
    H, C = GLA_HEADS, GLA_CHUNK
    n_c = S // C
    log_a = jax.nn.log_sigmoid((g_low @ w_gate + b_gate).astype(jnp.float32)) / GLA_TAU

    def chunks(t):
        return t.astype(jnp.float32).reshape(B, n_c, C, H, -1).transpose(1, 0, 3, 2, 4)

    qc = chunks(q) * (GLA_HEAD_DK ** -0.5)
    kc, vc, ac = chunks(k), chunks(v), chunks(log_a)
    tril = jnp.tril(jnp.ones((C, C), dtype=bool))

    def step(state, xs):
        qb, kb, vb, ab = xs
        b = jnp.cumsum(ab, axis=2)
        decay = jnp.exp(jnp.where(tril[:, :, None], b[:, :, :, None, :] - b[:, :, None, :, :], -jnp.inf))
        attn = jnp.einsum('bhid,bhjd,bhijd->bhij', qb, kb, decay)
        o = attn @ vb + jnp.einsum('bhid,bhdv->bhiv', qb * jnp.exp(b), state)
        b_last = b[:, :, -1:, :]
        state = (jnp.exp(b_last)[:, :, 0, :, None] * state
                 + jnp.einsum('bhjd,bhjv->bhdv', kb * jnp.exp(b_last - b), vb))
        return state, o

    state0 = jnp.zeros((B, H, GLA_HEAD_DK, GLA_HEAD_DV), jnp.float32)
    _, o = lax.scan(step, state0, (qc, kc, vc, ac))
    o = o.transpose(1, 0, 3, 2, 4).reshape(B, S, H, GLA_HEAD_DV)
    o = rms_norm(o, norm_g).reshape(B, S, H * GLA_HEAD_DV)
    return (o * jax.nn.silu(r.astype(jnp.float32))).astype(q.dtype)


def memory_attention(q, mem, mem_norm, w_kv, q_norm, k_norm):
    B, S, _ = q.shape
    H, dh = MEM_HEADS, MEM_HEAD_DIM
    k, v = jnp.split(rms_norm(mem, mem_norm) @ w_kv, 2, axis=-1)
    q = rms_norm(split_heads(q, H), q_norm) * (dh ** -0.5)
    k = rms_norm(split_heads(k, H), k_norm)
    v = split_heads(v, H)
    p = jax.nn.softmax(jnp.einsum('bhsd,bhmd->bhsm', q, k).astype(jnp.float32), axis=-1)
    o = jnp.einsum('bhsm,bhmd->bhsd', p.astype(v.dtype), v)
    return o.transpose(0, 2, 1, 3).reshape(B, S, H * dh)


def hybrid_layer(x, mem, positions, ln_mix, w_in, b_merge, nsa_q_norm, nsa_k_norm,
                 cmp_pe, cmp_w1, cmp_w2, gla_w_gate, gla_b_gate, gla_norm,
                 mem_norm, mem_w_kv, mem_q_norm, mem_k_norm, w_branch, w_out,
                 ln_mlp, w_up, w_down):
    B, S, D = x.shape
    h = rms_norm(x, ln_mix)
    z = h @ w_in
    offsets = np.cumsum(IN_SIZES)[:-1].tolist()
    (n_q, n_kc, n_vc, n_ks, n_vs, n_kw, n_vw, n_g,
     g_q, g_k, g_v, g_r, g_low, m_q, merge) = jnp.split(z, offsets, axis=-1)
    o_nsa = nsa_attention(n_q, n_kc, n_vc, n_ks, n_vs, n_kw, n_vw, n_g, positions,
                          nsa_q_norm, nsa_k_norm, cmp_pe, cmp_w1, cmp_w2)
    o_gla = gated_linear_attention(g_q, g_k, g_v, g_r, g_low, gla_w_gate, gla_b_gate, gla_norm)
    o_mem = memory_attention(m_q, mem, mem_norm, mem_w_kv, mem_q_norm, mem_k_norm)
    gates = jax.nn.sigmoid((merge.reshape(B, S, N_BRANCH, D) + b_merge).astype(jnp.float32)).astype(x.dtype)
    merged = (gates[:, :, 0] * (o_nsa @ w_branch[0])
              + gates[:, :, 1] * (o_gla @ w_branch[1])
              + gates[:, :, 2] * (o_mem @ w_branch[2]))
    x = x + merged @ w_out
    h = rms_norm(x, ln_mlp)
    return x + jnp.square(jax.nn.relu(h @ w_up)) @ w_down


def setup_inputs(seed: int = 0) -> dict:
    key = jax.random.key(seed)
    ks = jax.random.split(key, 24)
    f32 = jnp.float32
    dh = NSA_HEAD_DIM
    resid = (2 * DEPTH) ** -0.5

    def nrm(k, shape, scale):
        return jax.random.normal(k, shape, f32) * scale

    def gain(k, shape):
        return 1.0 + 0.02 * jax.random.normal(k, shape, f32)

    positions = (jnp.arange(SEQ, dtype=jnp.int32)[None, :]
                 + jax.random.randint(ks[2], (BATCH, 1), 0, SEQ, dtype=jnp.int32))
    return {
        'x': nrm(ks[0], (BATCH, SEQ, D_MODEL), 1.0),
        'mem': nrm(ks[1], (BATCH, MEM_LEN, D_MODEL), 1.0),
        'positions': positions,
        'ln_mix': gain(ks[3], (DEPTH, D_MODEL)),
        'w_in': nrm(ks[4], (DEPTH, D_MODEL, D_IN), D_MODEL ** -0.5),
        'b_merge': nrm(ks[5], (DEPTH, N_BRANCH, D_MODEL), 0.02),
        'nsa_q_norm': gain(ks[6], (DEPTH, dh)),
        'nsa_k_norm': gain(ks[7], (DEPTH, 3, dh)),
        'cmp_pe': nrm(ks[8], (DEPTH, 2, CMP_BLOCK, dh), 0.02),
        'cmp_w1': nrm(ks[9], (DEPTH, 2, CMP_BLOCK * dh, CMP_HIDDEN), (CMP_BLOCK * dh) ** -0.5),
        'cmp_w2': nrm(ks[10], (DEPTH, 2, CMP_HIDDEN, dh), CMP_HIDDEN ** -0.5),
        'gla_w_gate': nrm(ks[11], (DEPTH, GLA_RANK, GLA_HEADS * GLA_HEAD_DK), GLA_RANK ** -0.5),
        'gla_b_gate': nrm(ks[12], (DEPTH, GLA_HEADS * GLA_HEAD_DK), 0.02),
        'gla_norm': gain(ks[13], (DEPTH, GLA_HEAD_DV)),
        'mem_norm': gain(ks[14], (DEPTH, D_MODEL)),
        'mem_w_kv': nrm(ks[15], (DEPTH, D_MODEL, 2 * MEM_HEADS * MEM_HEAD_DIM), D_MODEL ** -0.5),
        'mem_q_norm': gain(ks[16], (DEPTH, MEM_HEAD_DIM)),
        'mem_k_norm': gain(ks[17], (DEPTH, MEM_HEAD_DIM)),
        'w_branch': nrm(ks[18], (DEPTH, N_BRANCH, BRANCH_WIDTH, D_MODEL), BRANCH_WIDTH ** -0.5),
        'w_out': nrm(ks[19], (DEPTH, D_MODEL, D_MODEL), D_MODEL ** -0.5 * resid),
        'ln_mlp': gain(ks[20], (DEPTH, D_MODEL)),
        'w_up': nrm(ks[21], (DEPTH, D_MODEL, MLP_HIDDEN), D_MODEL ** -0.5),
        'w_down': nrm(ks[22], (DEPTH, MLP_HIDDEN, D_MODEL), MLP_HIDDEN ** -0.5 * resid),
    }


def reference(x, mem, positions, ln_mix, w_in, b_merge, nsa_q_norm, nsa_k_norm,
              cmp_pe, cmp_w1, cmp_w2, gla_w_gate, gla_b_gate, gla_norm,
              mem_norm, mem_w_kv, mem_q_norm, mem_k_norm, w_branch, w_out,
              ln_mlp, w_up, w_down):
    for l in range(DEPTH):
        x = hybrid_layer(x, mem, positions, ln_mix[l], w_in[l], b_merge[l],
                         nsa_q_norm[l], nsa_k_norm[l], cmp_pe[l], cmp_w1[l], cmp_w2[l],
                         gla_w_gate[l], gla_b_gate[l], gla_norm[l],
                         mem_norm[l], mem_w_kv[l], mem_q_norm[l], mem_k_norm[l],
                         w_branch[l], w_out[l], ln_mlp[l], w_up[l], w_down[l])
    return x
```

```python
import numpy as np
import ml_dtypes
from contextlib import ExitStack
import concourse.bass as bass
import concourse.mybir as mybir
from concourse.bass_utils import run_bass_kernel_spmd

F32 = mybir.dt.float32
BF16 = mybir.dt.bfloat16
I32 = mybir.dt.int32
AF = mybir.ActivationFunctionType
ALU = mybir.AluOpType
AX = mybir.AxisListType

D = 1024
S = 4096
DEPTH = 4
MEM = 256
D_IN = 6440
EPS = 1e-6
NEG = -30000.0
O_NQ, O_KC, O_VC, O_KS, O_VS, O_KW, O_VW, O_NG = 0, 512, 640, 768, 896, 1024, 1152, 1280
O_GQ, O_GK, O_GV, O_GR, O_GLOW, O_MQ, O_MERGE = 1304, 1560, 1816, 2328, 2840, 2856, 3368


class Tok:
    __slots__ = ("w", "r")

    def __init__(self):
        self.w = None
        self.r = {}


class FW:
    EPOCH = 16000
    NDS = 12

    def __init__(self, nc, es):
        self.nc = nc
        self.es = es
        self.engs = {"pe": nc.tensor, "act": nc.scalar, "dve": nc.vector, "pool": nc.gpsimd, "sp": nc.sync}
        self.nsem = 0
        self.sems = {k: [self._newsem()] for k in self.engs}
        self.cnt = {k: 0 for k in self.engs}
        self.waited = {k: {} for k in self.engs}
        self.dq = ("sp", "pool", "act")
        self.dsems = {q: [self._newsem() for _ in range(self.NDS)] for q in self.dq}
        self.dtarget = {q: [0] * self.NDS for q in self.dq}
        self.dnext = {q: 0 for q in self.dq}
        self.uid = 0
        self.ninst = 0

    def _newsem(self):
        self.nsem += 1
        return self.es.enter_context(self.nc.semaphore(f"s{self.nsem}"))

    def name(self, p="t"):
        self.uid += 1
        return f"{p}{self.uid}"

    def _wait(self, k, st):
        if st is None:
            return
        if st[0] == "e":
            _, src, ep, c = st
            key = ("e", src)
            cur = self.waited[k].get(key, (-1, -1))
            if (ep, c) <= cur:
                return
            self.waited[k][key] = (ep, c)
            self.engs[k].wait_ge(self.sems[src][ep], c)
        else:
            _, q, idx, gen, tg = st
            key = ("d", q, idx, gen)
            if tg <= self.waited[k].get(key, 0):
                return
            self.waited[k][key] = tg
            self.engs[k].wait_ge(self.dsem_hist[(q, idx, gen)], tg)
        self.ninst += 1

    def _deps(self, k, reads, writes):
        for t in reads:
            self._wait(k, t.w)
        for t in writes:
            if not (k == "pe" and t.w is not None and t.w[0] == "e" and t.w[1] == "pe"):
                self._wait(k, t.w)
            for st in t.r.values():
                self._wait(k, st)

    def _mark(self, st, reads, writes):
        for t in writes:
            t.w = st
            t.r = {}
        for t in reads:
            t.r[st[1] if st[0] == "e" else ("d", st[1], st[2])] = st

    def op(self, k, fn, reads=(), writes=()):
        self._deps(k, reads, writes)
        ins = fn(self.engs[k])
        if self.cnt[k] >= self.EPOCH:
            self.sems[k].append(self._newsem())
            self.cnt[k] = 0
        self.cnt[k] += 1
        ep = len(self.sems[k]) - 1
        ins.then_inc(self.sems[k][ep], 1)
        self.ninst += 1
        self._mark(("e", k, ep, self.cnt[k]), reads, writes)

    def dma(self, k, out, in_, reads=(), writes=()):
        self._deps(k, reads, writes)
        if not hasattr(self, "dsem_hist"):
            self.dsem_hist = {}
            self.dgen = {q: [0] * self.NDS for q in self.dq}
            for q in self.dq:
                for i in range(self.NDS):
                    self.dsem_hist[(q, i, 0)] = self.dsems[q][i]
        idx = self.dnext[k]
        self.dnext[k] = (idx + 1) % self.NDS
        if self.dtarget[k][idx] >= self.EPOCH:
            self._wait(k, ("d", k, idx, self.dgen[k][idx], self.dtarget[k][idx]))
            self.dgen[k][idx] += 1
            self.dsems[k][idx] = self._newsem()
            self.dsem_hist[(k, idx, self.dgen[k][idx])] = self.dsems[k][idx]
            self.dtarget[k][idx] = 0
        gen = self.dgen[k][idx]
        if self.dtarget[k][idx]:
            self._wait(k, ("d", k, idx, gen, self.dtarget[k][idx]))
        self.dtarget[k][idx] += 16
        self.engs[k].dma_start(out=out, in_=in_).then_inc(self.dsems[k][idx], 16)
        self.ninst += 1
        self._mark(("d", k, idx, gen, self.dtarget[k][idx]), reads, writes)

    def barrier(self):
        for k in self.engs:
            for j in self.engs:
                if j != k and (self.cnt[j] or len(self.sems[j]) > 1):
                    self._wait(k, ("e", j, len(self.sems[j]) - 1, self.cnt[j]))
            if hasattr(self, "dsem_hist"):
                for q in self.dq:
                    for i in range(self.NDS):
                        if self.dtarget[q][i]:
                            self._wait(k, ("d", q, i, self.dgen[q][i], self.dtarget[q][i]))


def _consts():
    bf = ml_dtypes.bfloat16
    c = {}
    c["ident"] = np.eye(128, dtype=np.float32).astype(bf)
    c["identf"] = np.eye(128, dtype=np.float32)
    bo = np.zeros((128, 128), np.float32)
    bo[:64, :64] = 1
    bo[64:, 64:] = 1
    c["bones64"] = bo.astype(bf)
    c["ones128"] = np.ones((128, 128), np.float32).astype(bf)
    rm = np.zeros((128, 128), np.float32)
    for base in (0, 64):
        for i in range(8):
            rm[base + i + 8, base + i] = -1.0
            rm[base + i, base + i + 8] = 1.0
    c["rotm"] = rm.astype(bf)
    half = 8
    inv_freq = np.power(np.float32(500000.0), -np.arange(half, dtype=np.float32) / half).astype(np.float32)
    invf = np.zeros((128, 1), np.float32)
    for p in range(128):
        pp = p % 64
        if pp < 16:
            invf[p, 0] = inv_freq[pp % 8]
    c["invf"] = invf
    k = np.arange(128)[:, None]
    q = np.arange(128)[None, :]
    c["causb4"] = np.tile(np.where(k <= q, 0.0, NEG), (1, 4)).astype(bf)
    c["winb4"] = np.tile(np.where(k > q, 0.0, NEG), (1, 4)).astype(bf)
    dl = np.arange(17)[None, :, None]
    cm = np.where(16 * k[:, :, None] + 31 - q[:, None, :] <= 128 * dl, 0.0, NEG)
    c["cmaskb4"] = np.tile(cm, (1, 1, 4)).astype(bf)
    kk = np.arange(4096)[None, :]
    jb = np.arange(64)[:, None]
    c["eblk"] = np.where(kk // 64 == jb, NEG, 0.0).astype(bf)
    n = np.arange(256)
    cst = n * 16
    bst = np.arange(64) * 64
    ov = ((cst[:, None] < bst[None, :] + 64) & (bst[None, :] < cst[:, None] + 32)).astype(np.float32)
    ov[255] = 0
    c["ovl"] = np.ascontiguousarray(ov.reshape(2, 128, 64).transpose(1, 0, 2)).astype(bf)
    j1 = np.tile(np.arange(64, dtype=np.float32)[None, :], (128, 1))
    j1[:, 0] = 1e9
    c["jidx1"] = j1
    c["curm2"] = (2 * np.arange(32)[None, :] + (np.arange(128)[:, None] // 64) - 2).astype(np.float32)
    sm = np.ones((128, 512), np.float32)
    sm[:, ::128] = 0.0
    c["segmask"] = sm
    tp = np.arange(128)[:, None]
    tc = np.arange(128)[None, :]
    c["trim"] = np.where(tp > tc, -1.0, 0.0).astype(np.float32)
    c["tril"] = (tp <= tc).astype(np.float32)
    return c


def build(nlayers=DEPTH, dbg_out=(), dbg_in=(), phases=None):
    nc = bass.Bass("TRN2", target_bir_lowering=False)
    es = ExitStack()
    fw = FW(nc, es)
    C = _consts()

    def din(name, shape, dt=F32):
        return nc.dram_tensor(name, list(shape), dt, kind="ExternalInput").ap()

    def dscr(name, shape, dt=BF16):
        kind = "ExternalOutput" if name in dbg_out else ("ExternalInput" if name in dbg_in else "Internal")
        return nc.dram_tensor(name, list(shape), dt, kind=kind).ap()

    x_in = din("x", [S, D])
    mem_in = din("mem", [MEM, D])
    pos_in = din("positions", [1, S], I32)
    L = nlayers
    W = dict(
        ln_mixT=din("ln_mixT", [L, 128, 8]), w_in=din("w_in", [L, D, D_IN]), b_mergeT=din("b_mergeT", [L, 128, 24]),
        gq=din("gq", [L, 128, 1]), gk=din("gk", [L, 128, 3]),
        cmp_peT=din("cmp_peT", [L, 2, 128, 32]), cmp_w1=din("cmp_w1", [L, 2, 2048, 256]),
        cmp_w2=din("cmp_w2", [L, 2, 256, 64]),
        gla_wg=din("gla_wg", [L, 17, 256]), gla_negb=din("gla_negb", [L, 128, 2]), gla_norm=din("gla_norm", [L, 1, 128]),
        mem_normT=din("mem_normT", [L, 128, 8]), mem_w_kv=din("mem_w_kv", [L, D, 1024]),
        mem_qn=din("mem_qn", [L, 128, 1]), mem_kn=din("mem_kn", [L, 128, 1]),
        w_branch=din("w_branch", [L, 3, 512, D]), w_out=din("w_out", [L, D, D]),
        ln_mlpT=din("ln_mlpT", [L, 128, 8]), w_up=din("w_up", [L, D, 4 * D]), w_down=din("w_down", [L, 4 * D, D]),
    )
    CT = {k: din("c_" + k, v.shape, BF16 if v.dtype == ml_dtypes.bfloat16 else F32) for k, v in C.items()}
    y_out = nc.dram_tensor("y", [S, D], F32, kind="ExternalOutput").ap()

    hT_d = dscr("hT_d", [8, 128, S])
    QT_d = dscr("QT_d", [128, 4, S])
    KsT_d = dscr("KsT_d", [128, S])
    KwT_d = dscr("KwT_d", [128, S])
    mqT_d = dscr("mqT_d", [128, 4, S])
    KcT_d = dscr("KcT_d", [128, 256])
    qeT_d = dscr("qeT_d", [2, 128, S])
    keT_d = dscr("keT_d", [2, 128, S])
    kl_d = dscr("kl_d", [S, 256])
    gv_d = dscr("gv_d", [S, 512])
    GR_d = dscr("GR_d", [S, 512])
    edec_d = dscr("edec_d", [128, 2, 32], F32)
    Vc_d = dscr("Vc_d", [256, 128])
    Vsw_d = dscr("Vsw_d", [S, 256])
    ng_d = dscr("ng_d", [S, 24], F32)
    onsaT_d = dscr("onsaT_d", [4, 128, S])
    oglaT_d = dscr("oglaT_d", [4, 128, S])
    omemT_d = dscr("omemT_d", [4, 128, S])
    x1_d = dscr("x1_d", [S, D], F32)
    xa_d = dscr("xa_d", [S, D], F32)
    cos_d = dscr("cos_d", [128, S], F32)
    sin_d = dscr("sin_d", [128, S], F32)

    def sb(pes, shape, dt=F32, name="t"):
        return pes.enter_context(nc.sbuf_tensor(fw.name(name), list(shape), dt))

    def psum(pes, shape, dt=F32, name="ps"):
        return pes.enter_context(nc.psum_tensor(fw.name(name), list(shape), dt))

    ident = sb(es, [128, 128], BF16)
    bones64 = sb(es, [128, 128], BF16)
    ones128 = sb(es, [128, 128], BF16)
    rotm = sb(es, [128, 128], BF16)
    t_const = Tok()
    for tl, nm in ((ident, "ident"), (bones64, "bones64"), (ones128, "ones128"), (rotm, "rotm")):
        fw.dma("sp", tl[:], CT[nm], writes=[t_const])

    def run(ph):
        return phases is None or ph in phases

    if run("P0"):
        with ExitStack() as pes:
            posi = sb(pes, [128, S], I32)
            ang = sb(pes, [128, S])
            tmp = sb(pes, [128, S])
            kk = sb(pes, [128, S])
            invf = sb(pes, [128, 1])
            tk = Tok()
            fw.dma("sp", posi[:], pos_in.partition_broadcast(128), writes=[tk])
            fw.dma("sp", invf[:], CT["invf"], writes=[tk])
            fw.op("dve", lambda e: e.tensor_copy(out=tmp[:], in_=posi[:]), reads=[tk], writes=[tk])
            fw.op("dve", lambda e: e.tensor_scalar(out=ang[:], in0=tmp[:], scalar1=invf[:, 0:1], scalar2=None, op0=ALU.mult),
                  reads=[tk], writes=[tk])
            MAGIC = 12582912.0
            C1 = 6.28125
            C2 = float(np.float32(2 * np.pi - 6.28125))
            for shift, dst in ((0.0, sin_d), (float(np.pi / 2), cos_d)):
                fw.op("dve", lambda e: e.tensor_scalar(out=tmp[:], in0=ang[:], scalar1=shift, scalar2=float(1 / (2 * np.pi)),
                                                       op0=ALU.add, op1=ALU.mult), reads=[tk], writes=[tk])
                fw.op("dve", lambda e: e.tensor_scalar(out=kk[:], in0=tmp[:], scalar1=MAGIC, scalar2=None, op0=ALU.add),
                      reads=[tk], writes=[tk])
                fw.op("dve", lambda e: e.tensor_scalar(out=kk[:], in0=kk[:], scalar1=-MAGIC, scalar2=None, op0=ALU.add),
                      reads=[tk], writes=[tk])
                fw.op("dve", lambda e: e.scalar_tensor_tensor(out=tmp[:], in0=kk[:], scalar=-C1, in1=ang[:], op0=ALU.mult, op1=ALU.add),
                      reads=[tk], writes=[tk])
                fw.op("dve", lambda e: e.tensor_scalar(out=tmp[:], in0=tmp[:], scalar1=shift, scalar2=None, op0=ALU.add),
                      reads=[tk], writes=[tk])
                fw.op("dve", lambda e: e.scalar_tensor_tensor(out=tmp[:], in0=kk[:], scalar=-C2, in1=tmp[:], op0=ALU.mult, op1=ALU.add),
                      reads=[tk], writes=[tk])
                fw.op("dve", lambda e: e.tensor_scalar(out=tmp[:], in0=tmp[:], scalar1=3.14159, scalar2=-3.14159, op0=ALU.min, op1=ALU.max),
                      reads=[tk], writes=[tk])
                fw.op("act", lambda e: e.activation(out=kk[:], in_=tmp[:], func=AF.Sin), reads=[tk], writes=[tk])
                fw.dma("sp", dst, kk[:], reads=[tk])
            fw.barrier()

    def norm_to_hT(pes, src, ntiles, gT, t_g, hT, hT_dram=None):
        xin = [sb(pes, [128, D]) for _ in range(2)]
        junk = sb(pes, [128, D], BF16)
        hn = [sb(pes, [128, D], BF16) for _ in range(2)]
        ss = [sb(pes, [128, 1]) for _ in range(2)]
        rs = [sb(pes, [128, 1]) for _ in range(2)]
        pst = [psum(pes, [128, 8, 128], BF16) for _ in range(2)]
        t_x = [Tok(), Tok()]
        t_j = Tok()
        t_s = [Tok(), Tok()]
        t_h = [Tok(), Tok()]
        t_p = [Tok(), Tok()]
        t_hT = Tok()
        def front(tt):
            b = tt % 2
            fw.dma("sp", xin[b][:], src[tt * 128:(tt + 1) * 128, :], writes=[t_x[b]])
            fw.op("act", lambda e: e.activation(out=junk[:], in_=xin[b][:], func=AF.Square, accum_out=ss[b][:, 0:1]),
                  reads=[t_x[b]], writes=[t_j, t_s[b]])
            fw.op("act", lambda e: e.activation(out=rs[b][:], in_=ss[b][:], func=AF.Sqrt, scale=1.0 / D, bias=EPS),
                  reads=[t_s[b]], writes=[t_s[b]])
            fw.op("dve", lambda e: e.reciprocal(out=rs[b][:], in_=rs[b][:]), reads=[t_s[b]], writes=[t_s[b]])
            fw.op("dve",
                  lambda e: e.tensor_scalar(out=hn[b][:], in0=xin[b][:], scalar1=rs[b][:, 0:1], scalar2=None, op0=ALU.mult),
                  reads=[t_x[b], t_s[b]], writes=[t_h[b]])

        def back(tt):
            b = tt % 2
            for c in range(8):
                fw.op("pe", lambda e: e.transpose(out=pst[b][:, c, :], in_=hn[b][:, c * 128:(c + 1) * 128], identity=ident[:]),
                      reads=[t_h[b], t_const], writes=[t_p[b]])
            fw.op("dve", lambda e: e.tensor_tensor(out=hT[:, :, tt * 128:(tt + 1) * 128], in0=pst[b][:],
                                                   in1=gT[:].unsqueeze(2).to_broadcast([128, 8, 128]), op=ALU.mult),
                  reads=[t_p[b], t_g], writes=[t_hT])

        front(0)
        for tt in range(ntiles):
            if tt + 1 < ntiles:
                front(tt + 1)
            back(tt)
        if hT_dram is not None:
            for c in range(8):
                fw.dma("sp", hT_dram[c], hT[:, c, :], reads=[t_hT])
        return t_hT

    for l in range(nlayers):
        x_src = x_in if l == 0 else xa_d
        x_dst = y_out if l == nlayers - 1 else xa_d
        if run("P2"):
            with ExitStack() as pes:
                gT = sb(pes, [128, 8])
                t_g = Tok()
                fw.dma("sp", gT[:], W["ln_mixT"][l], writes=[t_g])
                hT = sb(pes, [128, 8, S], BF16)
                def norm_cb(x_src=x_src, gT=gT, t_g=t_g, hT=hT):
                    with ExitStack() as pes1:
                        t = norm_to_hT(pes1, x_src, S // 128, gT, t_g, hT, hT_d)
                        fw.barrier()
                    return t

                p2_projections(nc, fw, pes, l, W, hT, norm_cb, dict(
                    QT_d=QT_d, KsT_d=KsT_d, KwT_d=KwT_d, mqT_d=mqT_d, Vsw_d=Vsw_d, ng_d=ng_d, cos_d=cos_d, sin_d=sin_d, KcT_d=KcT_d, Vc_d=Vc_d, qeT_d=qeT_d, keT_d=keT_d, kl_d=kl_d, gv_d=gv_d, GR_d=GR_d, edec_d=edec_d, CT=CT),
                    dict(ident=ident, bones64=bones64, ones128=ones128, rotm=rotm, t_const=t_const), sb, psum)
                fw.barrier()
        if run("P3"):
            with ExitStack() as pes:
                p3_nsa(nc, fw, pes, l, dict(QT_d=QT_d, KsT_d=KsT_d, KwT_d=KwT_d, KcT_d=KcT_d, Vc_d=Vc_d, Vsw_d=Vsw_d, ng_d=ng_d,
                                            onsaT_d=onsaT_d), CT, dict(ident=ident, t_const=t_const), sb, psum)
                fw.barrier()
        if run("P4"):
            with ExitStack() as pes:
                p4_gla(nc, fw, pes, l, dict(qeT_d=qeT_d, keT_d=keT_d, kl_d=kl_d, gv_d=gv_d, GR_d=GR_d, edec_d=edec_d, oglaT_d=oglaT_d),
                       CT, dict(ident=ident, t_const=t_const), sb, psum)
                fw.barrier()
        with ExitStack() as pes56:
            w6, w6_issue = (p6_load(nc, fw, pes56, l, W, sb) if run("P6") else (None, None))
            if run("P5"):
                with ExitStack() as pes:
                    p5_mem(nc, fw, pes, l, W, mem_in, mqT_d, omemT_d,
                           dict(ident=ident, ones128=ones128, t_const=t_const), sb, psum, norm_to_hT, after_w=w6_issue)
                    fw.barrier()
            elif w6_issue is not None:
                w6_issue()
            if run("P6"):
                with ExitStack() as pes:
                    p6_merge(nc, fw, pes, l, W, hT_d, onsaT_d, oglaT_d, omemT_d, x_src, x1_d, sb, psum, w6)
                    fw.barrier()
        if run("P7"):
            with ExitStack() as pes:
                p7_mlp(nc, fw, pes, l, W, x1_d, x_dst, dict(ident=ident, t_const=t_const), sb, psum)
                fw.barrier()
    fw.barrier()
    es.close()
    return nc, fw


class Skew:
    def __init__(self):
        self.pipe = []

    def push(self, stages):
        self.pipe.insert(0, list(stages))
        for d, job in enumerate(list(self.pipe)):
            if d < len(job):
                job[d]()
        while self.pipe and len(self.pipe[-1]) <= len(self.pipe) - 1:
            self.pipe.pop()

    def drain(self):
        while self.pipe:
            self.push([])


def wload(fw, wt, src, tok):
    fw.dma("pool", wt, src, writes=[tok])


def p2_projections(nc, fw, pes, l, W, hT, norm_cb, DR, CS, sb, psum):
    w_in = W["w_in"][l].rearrange("(c p) n -> p c n", p=128)
    ident, bones64, ones128, rotm, t_const = CS["ident"], CS["bones64"], CS["ones128"], CS["rotm"], CS["t_const"]
    cosT = sb(pes, [128, S])
    sinT = sb(pes, [128, S])
    t_cs = Tok()
    fw.dma("sp", cosT[:], DR["cos_d"], writes=[t_cs])
    fw.dma("sp", sinT[:], DR["sin_d"], writes=[t_cs])
    gq = sb(pes, [128, 1])
    gk = sb(pes, [128, 3])
    mqn = sb(pes, [128, 1])
    t_gn = Tok()
    fw.dma("sp", gq[:], W["gq"][l], writes=[t_gn])
    fw.dma("sp", gk[:], W["gk"][l], writes=[t_gn])
    fw.dma("sp", mqn[:], W["mem_qn"][l], writes=[t_gn])
    fw.op("dve", lambda e: e.tensor_scalar(out=gq[:], in0=gq[:], scalar1=0.125, scalar2=None, op0=ALU.mult), reads=[t_gn], writes=[t_gn])
    fw.op("dve", lambda e: e.tensor_scalar(out=mqn[:], in0=mqn[:], scalar1=float(128 ** -0.5), scalar2=None, op0=ALU.mult),
          reads=[t_gn], writes=[t_gn])

    wt = [sb(pes, [128, 8, 128], BF16) for _ in range(3)]
    t_w = [Tok(), Tok(), Tok()]
    pm = [psum(pes, [128, 512]) for _ in range(2)]
    t_pm = [Tok(), Tok()]
    pa = [psum(pes, [128, 512]) for _ in range(2)]
    t_pa = [Tok(), Tok()]
    sq = [sb(pes, [128, 512], BF16) for _ in range(2)]
    t_sq = [Tok(), Tok()]
    rstd = [sb(pes, [128, 512]) for _ in range(2)]
    t_rs = [Tok(), Tok()]
    qn = [sb(pes, [128, 512], BF16) for _ in range(2)]
    t_qn = [Tok(), Tok()]
    t1 = [sb(pes, [128, 512]) for _ in range(2)]
    t_t1 = [Tok(), Tok()]
    t2 = [sb(pes, [128, 512]) for _ in range(2)]
    t_t2 = [Tok(), Tok()]
    ob = [sb(pes, [128, 512], BF16) for _ in range(3)]
    t_ob = [Tok(), Tok(), Tok()]
    t_ob2 = [Tok(), Tok(), Tok()]
    state = dict(w=0, i=0, o=0, ip=0)

    pipe = []

    def push(stages):
        pipe.insert(0, list(stages))
        for d, job in enumerate(list(pipe)):
            if d < len(job):
                job[d]()
        while pipe and len(pipe[-1]) <= len(pipe) - 1:
            pipe.pop()

    def drain():
        while pipe:
            push([])

    loaded = set()

    def load_w(ci, wsrc):
        if ci in loaded:
            return
        loaded.add(ci)
        wb = ci % 3
        if isinstance(wsrc, (list, tuple)):
            for gi, ws_ in enumerate(wsrc):
                wload(fw, wt[wb][:, :, gi * 64:(gi + 1) * 64], ws_, t_w[wb])
        else:
            wload(fw, wt[wb][:], wsrc, t_w[wb])

    def fm_chunk(wsrc, gain, blk, rope, dst_fn, preload_only=False, nxt=None):
        wb = state["w"] % 3
        state["w"] += 1
        load_w(state["w"] - 1, wsrc)
        if preload_only:
            state["w"] -= 1
            return
        if nxt is not None:
            load_w(state["w"], nxt)
        for _ in range(3):
            if state.get("defer"):
                state["defer"].pop(0)()
        for T in range(S // 512):
            i = state["i"] % 2
            state["i"] += 1
            ip = state["ip"] % 3
            state["ip"] += 1
            o = state["o"] % 3
            state["o"] += 1
            tsl = slice(T * 512, (T + 1) * 512)
            t_hT = state["t_hT"]

            def st0(i=i, ip=ip, tsl=tsl, wb=wb):
                for k in range(8):
                    fw.op("pe", lambda e: e.matmul(pm_fm[ip][:], lhsT=wt[wb][:, k, :], rhs=hT[:, k, tsl], start=(k == 0), stop=(k == 7)),
                          reads=[t_w[wb], t_hT], writes=[t_pm_fm[ip]])
                if gain is None:
                    sdst, t_sd = dst_fn(tsl)
                    fw.op("act", lambda e: e.activation(out=sdst, in_=pm_fm[ip][:], func=AF.Copy), reads=[t_pm_fm[ip]], writes=[t_sd])
                else:
                    fw.op("act", lambda e: e.activation(out=sq[i][:], in_=pm_fm[ip][:], func=AF.Square), reads=[t_pm_fm[ip]], writes=[t_sq[i]])

            def st1(i=i, ip=ip, o=o, tsl=tsl):
                om = bones64 if blk == 64 else ones128
                fw.op("pe", lambda e: e.matmul(pa[i][:], lhsT=om[:], rhs=sq[i][:], start=True, stop=True),
                      reads=[t_sq[i], t_const], writes=[t_pa[i]])
                fw.op("act", lambda e: e.activation(out=rstd[i][:], in_=pa[i][:], func=AF.Ln, scale=1.0 / blk, bias=EPS),
                      reads=[t_pa[i]], writes=[t_rs[i]])
                fw.op("act", lambda e: e.activation(out=rstd[i][:], in_=rstd[i][:], func=AF.Exp, scale=-0.5), reads=[t_rs[i]], writes=[t_rs[i]])
                dst = qn[i] if rope else ob[o]
                t_dst = t_qn[i] if rope else t_ob[o]
                fw.op("dve", lambda e: e.scalar_tensor_tensor(out=dst[:], in0=pm_fm[ip][:], scalar=gain, in1=rstd[i][:],
                                                              op0=ALU.mult, op1=ALU.mult),
                      reads=[t_pm_fm[ip], t_rs[i], t_gn], writes=[t_dst])
                if not rope:
                    fw.dma("sp", dst_fn(tsl), ob[o][:], reads=[t_ob[o]])

            def st2(i=i, o=o, tsl=tsl):
                fw.op("pe", lambda e: e.matmul(pr[i][:], lhsT=rotm[:], rhs=qn[i][:], start=True, stop=True),
                      reads=[t_qn[i], t_const], writes=[t_pr[i]])
                fw.op("dve", lambda e: e.tensor_tensor(out=t1[i][:], in0=pr[i][:], in1=sinT[:, tsl], op=ALU.mult),
                      reads=[t_pr[i], t_cs], writes=[t_t1[i]])
                fw.op("pool", lambda e: e.tensor_tensor(out=t2[i][:], in0=qn[i][:], in1=cosT[:, tsl], op=ALU.mult),
                      reads=[t_qn[i], t_cs], writes=[t_t2[i]])
                fw.op("dve", lambda e: e.tensor_tensor(out=ob[o][:, 0:256], in0=t1[i][:, 0:256], in1=t2[i][:, 0:256], op=ALU.add),
                      reads=[t_t1[i], t_t2[i]], writes=[t_ob[o]])
                fw.op("pool", lambda e: e.tensor_tensor(out=ob[o][:, 256:512], in0=t1[i][:, 256:512], in1=t2[i][:, 256:512], op=ALU.add),
                      reads=[t_t1[i], t_t2[i]], writes=[t_ob2[o]])
                fw.dma("sp", dst_fn(tsl), ob[o][:], reads=[t_ob[o], t_ob2[o]])

            if gain is None:
                push([st0])
            elif rope:
                push([st0, st1, st2])
            else:
                push([st0, st1])

    QT_d, KsT_d, KwT_d, mqT_d = DR["QT_d"], DR["KsT_d"], DR["KwT_d"], DR["mqT_d"]
    import os
    sel = os.environ.get("P2SEL", "q,ks,kw,mq,cmp,tm,gla").split(",")
    wq = w_in[:, :, O_NQ:O_NQ + 512].rearrange("p c (g r d) -> p c r g d", g=2, r=4, d=64)
    chunks = []
    if "q" in sel:
        for r in range(4):
            chunks.append(([wq[:, :, r, 0, :], wq[:, :, r, 1, :]], gq[:, 0:1], 64, True, lambda tsl, r=r: QT_d[:, r, tsl]))
    if "ks" in sel:
        chunks.append((w_in[:, :, O_KS:O_KS + 128], gk[:, 1:2], 64, True, lambda tsl: KsT_d[:, tsl]))
    if "kw" in sel:
        chunks.append((w_in[:, :, O_KW:O_KW + 128], gk[:, 2:3], 64, True, lambda tsl: KwT_d[:, tsl]))
    if "mq" in sel:
        for h in range(4):
            chunks.append((w_in[:, :, O_MQ + h * 128:O_MQ + (h + 1) * 128], mqn[:, 0:1], 128, False, lambda tsl, h=h: mqT_d[:, h, tsl]))
    ncore = len(chunks)
    if chunks:
        fm_chunk(chunks[0][0], None, 0, False, None, preload_only=True)
    state["t_hT"] = t_hT = norm_cb()
    pmx = ExitStack()
    pm_fm = pm + [psum(pmx, [128, 512])]
    t_pm_fm = t_pm + [Tok()]
    pr = [psum(pmx, [128, 512]) for _ in range(2)]
    t_pr = [Tok(), Tok()]
    cs0 = ExitStack()
    if "cmp" in sel:
        kcvc = sb(cs0, [128, 2, S], BF16)
        t_kcvc = Tok()
        cmp_w, cmp_defer = p2_compress_load(nc, fw, cs0, l, W, sb)
        state["defer"] = cmp_defer
        chunks.append((w_in[:, :, O_KC:O_KC + 128], None, 0, False, lambda tsl: (kcvc[:, 0, tsl], t_kcvc)))
        chunks.append((w_in[:, :, O_VC:O_VC + 128], None, 0, False, lambda tsl: (kcvc[:, 1, tsl], t_kcvc)))
    for ci in range(ncore):
        nxt = chunks[ci + 1][0] if ci + 1 < len(chunks) else None
        fm_chunk(*chunks[ci], nxt=nxt)
    if "cmp" in sel:
        while state.get("defer"):
            state["defer"].pop(0)()
        fm_chunk(*chunks[ncore], nxt=chunks[ncore + 1][0])
        fm_chunk(*chunks[ncore + 1])
        drain()
        fw.barrier()
        pmx.close()
        with ExitStack() as cs_:
            p2_compress(nc, fw, cs_, l, W, kcvc, t_kcvc, cosT, sinT, t_cs, gk, t_gn, DR, CS, sb, psum, cmp_w)
            fw.barrier()
    drain()
    fw.barrier()
    pmx.close()
    cs0.close()
    drain()
    if "gla" in sel:
        with ExitStack() as cs_:
            p2_gla(nc, fw, cs_, l, W, w_in, hT, t_hT, pm, t_pm, pa, t_pa, wt, t_w, DR, sb, psum)
            fw.barrier()
    if "tm" in sel:
        with ExitStack() as cs_:
            wtm = sb(cs_, [128, 8, 280], BF16)
            t_wtm = Tok()
            wload(fw, wtm[:, :, 0:128], w_in[:, :, O_VS:O_VS + 128], t_wtm)
            wload(fw, wtm[:, :, 128:256], w_in[:, :, O_VW:O_VW + 128], t_wtm)
            wload(fw, wtm[:, :, 256:280], w_in[:, :, O_NG:O_NG + 24], t_wtm)
            vst = sb(cs_, [128, 32, 256], BF16)
            ngst = sb(cs_, [128, 32, 24])
            t_vst = Tok()
            for tt in range(S // 128):
                i = tt % 2
                for k in range(8):
                    fw.op("pe", lambda e: e.matmul(pm[i][:, 0:280], lhsT=hT[:, k, tt * 128:(tt + 1) * 128], rhs=wtm[:, k, :],
                                                   start=(k == 0), stop=(k == 7)), reads=[t_wtm, t_hT], writes=[t_pm[i]])
                fw.op("act", lambda e: e.activation(out=vst[:, tt, :], in_=pm[i][:, 0:256], func=AF.Copy), reads=[t_pm[i]], writes=[t_vst])
                fw.op("act", lambda e: e.activation(out=ngst[:, tt, :], in_=pm[i][:, 256:280], func=AF.Sigmoid), reads=[t_pm[i]], writes=[t_vst])
            fw.dma("sp", DR["Vsw_d"].rearrange("(t p) c -> p t c", p=128), vst[:], reads=[t_vst])
            fw.dma("sp", DR["ng_d"].rearrange("(t p) c -> p t c", p=128), ngst[:], reads=[t_vst])
            fw.barrier()


def p2_compress_load(nc, fw, pes, l, W, sb):
    w1r = [sb(pes, [128, 32, 256], BF16) for _ in range(2)]
    t_w1 = Tok()
    defer = []
    for kv in range(2):
        src = W["cmp_w1"][l][kv].rearrange("(l d) e -> d l e", d=64)
        for g in range(2):
            for lh in range(0, 32, 8):
                defer.append(lambda kv=kv, g=g, lh=lh, src=src: wload(fw, w1r[kv][g * 64:(g + 1) * 64, lh:lh + 8, :], src[:, lh:lh + 8, :], t_w1))
    peb = sb(pes, [128, 2, 32], BF16)
    for kv in range(2):
        defer.append(lambda kv=kv: wload(fw, peb[:, kv, :], W["cmp_peT"][l][kv], t_w1))
    w2p = sb(pes, [128, 8, 128], BF16)
    t_w2 = Tok()
    fw.op("pool", lambda e: e.memset(w2p[:], 0.0), writes=[t_w2])
    for kv in range(2):
        for g in range(2):
            i0 = kv * 4 + g * 2
            defer.append(lambda kv=kv, g=g, i0=i0: wload(fw, w2p[:, i0:i0 + 2, g * 64:(g + 1) * 64],
                                                          W["cmp_w2"][l][kv].rearrange("(ec e) d -> e ec d", e=128), t_w2))
    return (w1r, t_w1, peb, w2p, t_w2), defer


def p2_compress(nc, fw, pes, l, W, kcvc, t_kcvc, cosT, sinT, t_cs, gk, t_gn, DR, CS, sb, psum, wts):
    bones64, rotm, t_const = CS["bones64"], CS["rotm"], CS["t_const"]
    NC_ = 255
    w1r, t_w1, peb, w2p, t_w2 = wts
    bias = sb(pes, [128, 4])
    t_b = Tok()
    hidT = sb(pes, [128, 8, 256], BF16)
    t_hid = Tok()
    fw.op("pool", lambda e: e.memset(hidT[:], 0.0), writes=[t_hid])
    ph = [psum(pes, [128, 256]) for _ in range(2)]
    t_ph = [Tok(), Tok()]
    pb = psum(pes, [128, 256])
    t_pb = Tok()
    pc = psum(pes, [128, 256])
    t_pc = Tok()
    u = [sb(pes, [128, NC_]) for _ in range(2)]
    t_u = [Tok(), Tok()]
    v = [sb(pes, [128, NC_]) for _ in range(2)]
    t_v = [Tok(), Tok()]
    for kv in range(2):
        for ec in range(2):
            for l_ in range(32):
                fw.op("pe", lambda e: e.matmul(pb[:, kv * 2 + ec:kv * 2 + ec + 1], lhsT=w1r[kv][0:64, l_, ec * 128:(ec + 1) * 128],
                                               rhs=peb[0:64, kv, l_:l_ + 1], start=(l_ == 0), stop=(l_ == 31)),
                      reads=[t_w1], writes=[t_pb])
    fw.op("dve", lambda e: e.tensor_copy(out=bias[:], in_=pb[:, 0:4]), reads=[t_pb], writes=[t_b])
    cnt = 0
    for kv in range(2):
        for g in range(2):
            for ec in range(2):
                i = cnt % 2
                cnt += 1
                for l_ in range(32):
                    fw.op("pe", lambda e: e.matmul(ph[i][:, 0:NC_], lhsT=w1r[kv][g * 64:(g + 1) * 64, l_, ec * 128:(ec + 1) * 128],
                                                   rhs=kcvc[g * 64:(g + 1) * 64, kv, l_:l_ + 16 * (NC_ - 1) + 1:16],
                                                   start=(l_ == 0), stop=(l_ == 31)), reads=[t_w1, t_kcvc], writes=[t_ph[i]])
                hi = kv * 4 + g * 2 + ec
                fw.op("act", lambda e: e.activation(out=u[i][:], in_=ph[i][:, 0:NC_], func=AF.Identity, bias=bias[:, kv * 2 + ec:kv * 2 + ec + 1]),
                      reads=[t_ph[i], t_b], writes=[t_u[i]])
                fw.op("dve", lambda e: e.tensor_tensor(out=v[i][:], in0=u[i][:], in1=u[i][:], op=ALU.mult), reads=[t_u[i]], writes=[t_v[i]])
                fw.op("dve", lambda e: e.tensor_scalar(out=v[i][:], in0=v[i][:], scalar1=0.044715, scalar2=1.0, op0=ALU.mult, op1=ALU.add),
                      reads=[t_v[i]], writes=[t_v[i]])
                fw.op("dve", lambda e: e.tensor_tensor(out=v[i][:], in0=v[i][:], in1=u[i][:], op=ALU.mult), reads=[t_v[i], t_u[i]], writes=[t_v[i]])
                fw.op("act", lambda e: e.activation(out=v[i][:], in_=v[i][:], func=AF.Sigmoid, scale=float(2 * np.sqrt(2 / np.pi))),
                      reads=[t_v[i]], writes=[t_v[i]])
                fw.op("dve", lambda e: e.tensor_tensor(out=hidT[:, hi, 0:NC_], in0=v[i][:], in1=u[i][:], op=ALU.mult),
                      reads=[t_v[i], t_u[i]], writes=[t_hid])
    n_acc = 0
    for g in range(2):
        for ec in range(2):
            fw.op("pe", lambda e: e.matmul(pc[:], lhsT=w2p[:, g * 2 + ec, :], rhs=hidT[:, g * 2 + ec, :], start=(n_acc == 0), stop=(n_acc == 3)),
                  reads=[t_w2, t_hid], writes=[t_pc])
            n_acc += 1
    sq = sb(pes, [128, 256], BF16)
    rstd = sb(pes, [128, 256])
    qn = sb(pes, [128, 256], BF16)
    t1 = sb(pes, [128, NC_])
    t2 = sb(pes, [128, NC_])
    kc_o = sb(pes, [128, 256], BF16)
    t_k = Tok()
    fw.op("pool", lambda e: e.memset(kc_o[:], 0.0), writes=[t_k])
    fw.op("act", lambda e: e.activation(out=sq[:], in_=pc[:], func=AF.Square), reads=[t_pc], writes=[t_k])
    fw.op("pe", lambda e: e.matmul(ph[0][:], lhsT=bones64[:], rhs=sq[:], start=True, stop=True), reads=[t_k, t_const], writes=[t_ph[0]])
    fw.op("act", lambda e: e.activation(out=rstd[:], in_=ph[0][:], func=AF.Sqrt, scale=1.0 / 64, bias=EPS), reads=[t_ph[0]], writes=[t_k])
    fw.op("dve", lambda e: e.reciprocal(out=rstd[:], in_=rstd[:]), reads=[t_k], writes=[t_k])
    fw.op("dve", lambda e: e.scalar_tensor_tensor(out=qn[:], in0=pc[:], scalar=gk[:, 0:1], in1=rstd[:], op0=ALU.mult, op1=ALU.mult),
          reads=[t_pc, t_k, t_gn], writes=[t_k])
    fw.op("pe", lambda e: e.matmul(ph[1][:], lhsT=rotm[:], rhs=qn[:], start=True, stop=True), reads=[t_k, t_const], writes=[t_ph[1]])
    cend = slice(31, 31 + 16 * (NC_ - 1) + 1, 16)
    fw.op("dve", lambda e: e.tensor_tensor(out=t1[:], in0=ph[1][:, 0:NC_], in1=sinT[:, cend], op=ALU.mult), reads=[t_ph[1], t_cs], writes=[t_k])
    fw.op("dve", lambda e: e.tensor_tensor(out=t2[:], in0=qn[:, 0:NC_], in1=cosT[:, cend], op=ALU.mult), reads=[t_k, t_cs], writes=[t_k])
    fw.op("dve", lambda e: e.tensor_tensor(out=kc_o[:, 0:NC_], in0=t1[:], in1=t2[:], op=ALU.add), reads=[t_k], writes=[t_k])
    fw.dma("sp", DR["KcT_d"], kc_o[:], reads=[t_k])
    vc_o = sb(pes, [128, 2, 128], BF16)
    t_vo = Tok()
    fw.op("pool", lambda e: e.memset(vc_o[:], 0.0), writes=[t_vo])
    for nt in range(2):
        nn = 128 if nt == 0 else NC_ - 128
        n_acc = 0
        for g in range(2):
            for ec in range(2):
                fw.op("pe", lambda e: e.matmul(ph[nt][0:nn, 0:128], lhsT=hidT[:, 4 + g * 2 + ec, nt * 128:nt * 128 + nn],
                                               rhs=w2p[:, 4 + g * 2 + ec, :], start=(n_acc == 0), stop=(n_acc == 3)),
                      reads=[t_w2, t_hid], writes=[t_ph[nt]])
                n_acc += 1
        fw.op("act", lambda e: e.activation(out=vc_o[0:nn, nt, :], in_=ph[nt][0:nn, 0:128], func=AF.Copy), reads=[t_ph[nt]], writes=[t_vo])
    fw.dma("sp", DR["Vc_d"].rearrange("(t p) c -> p t c", p=128), vc_o[:], reads=[t_vo])


def p3_nsa(nc, fw, pes, l, DR, CT, CS, sb, psum):
    ident, t_const = CS["ident"], CS["t_const"]
    mm = lambda e, *a, **k: e.matmul(*a, skip_group_check=True, **k)
    QT = sb(pes, [128, 4, S], BF16)
    t_QT = Tok()
    for r in range(4):
        fw.dma("sp", QT[:, r, :], DR["QT_d"][:, r, :], writes=[t_QT])
    Kpad = sb(pes, [128, 4, S], BF16)
    t_K = Tok()
    for i in (2, 3):
        fw.op("dve", lambda e: e.memset(Kpad[:, i, :], 0.0), writes=[t_K])
    for ws_, src in ((0, DR["KsT_d"]), (1, DR["KwT_d"])):
        for g in range(2):
            fw.dma("sp", Kpad[g * 64:(g + 1) * 64, ws_ * 2 + g, :], src[g * 64:(g + 1) * 64, :], writes=[t_K])
    for g in range(2):
        fw.dma("sp", Kpad[(1 - g) * 64:(2 - g) * 64, g, :], CT["eblk"], writes=[t_K])
    Kc = sb(pes, [128, 2, 256], BF16)
    t_Kc = Tok()
    fw.op("dve", lambda e: e.memset(Kc[:], 0.0), writes=[t_Kc])
    for g in range(2):
        fw.dma("sp", Kc[g * 64:(g + 1) * 64, g, :], DR["KcT_d"][g * 64:(g + 1) * 64, :], writes=[t_Kc])
    Vs = sb(pes, [128, 32, 2, 65], BF16)
    Vw = sb(pes, [128, 32, 2, 65], BF16)
    Vc = sb(pes, [128, 2, 2, 65], BF16)
    t_V = Tok()
    for vi, v_ in enumerate((Vs, Vw, Vc)):
        fw.op("pool" if vi == 1 else "dve", lambda e: e.memset(v_[:], 1.0), writes=[t_V])
    vsw_v = DR["Vsw_d"].rearrange("(t p) c -> p t c", p=128)
    vc_v = DR["Vc_d"].rearrange("(t p) c -> p t c", p=128)
    for g in range(2):
        fw.dma("sp", Vs[:, :, g, 0:64], vsw_v[:, :, g * 64:(g + 1) * 64], writes=[t_V])
        fw.dma("sp", Vw[:, :, g, 0:64], vsw_v[:, :, 128 + g * 64:128 + (g + 1) * 64], writes=[t_V])
        fw.dma("sp", Vc[:, :, g, 0:64], vc_v[:, :, g * 64:(g + 1) * 64], writes=[t_V])
    t_tab = Tok()
    ovl = sb(pes, [128, 2, 64], BF16)
    ng = sb(pes, [128, 32, 24])
    cmaskb4 = sb(pes, [128, 17, 512], BF16)
    causb4 = sb(pes, [128, 512], BF16)
    winb4 = sb(pes, [128, 512], BF16)
    identf = sb(pes, [128, 128])
    jidx1 = sb(pes, [128, 64])
    curm2 = sb(pes, [128, 32])
    fw.dma("sp", ovl[:], CT["ovl"], writes=[t_tab])
    fw.dma("sp", ng[:], DR["ng_d"].rearrange("(t p) c -> p t c", p=128), writes=[t_tab])
    fw.dma("sp", cmaskb4[:], CT["cmaskb4"], writes=[t_tab])
    fw.dma("sp", causb4[:], CT["causb4"], writes=[t_tab])
    fw.dma("sp", winb4[:], CT["winb4"], writes=[t_tab])
    fw.dma("sp", jidx1[:], CT["jidx1"], writes=[t_tab])
    fw.dma("sp", identf[:], CT["identf"], writes=[t_tab])
    fw.dma("sp", curm2[:], CT["curm2"], writes=[t_tab])

    NPS = 3
    pS = [psum(pes, [128, 512]) for _ in range(NPS)]
    t_pS = [Tok() for _ in range(NPS)]
    pOc = psum(pes, [128, 4, 65])
    t_pOc = Tok()
    pImp = psum(pes, [128, 4, 64])
    t_pImp = Tok()
    pOs = psum(pes, [128, 4, 65])
    t_pOs = Tok()
    pOwb = psum(pes, [128, 512])
    pOw = pOwb[:, 0:260].rearrange("p (r c) -> p r c", r=4)
    pNsT = pOwb[:, 384:512]
    t_pOw = Tok()
    pTT = psum(pes, [128, 4, 128], BF16)
    t_pT = Tok()
    t_pT = Tok()
    NPT = 4
    PT = [sb(pes, [128, 512], BF16) for _ in range(NPT)]
    t_PT = [Tok() for _ in range(NPT)]
    elig = [sb(pes, [128, 64]) for _ in range(2)]
    t_el = [Tok(), Tok()]
    rd = [[sb(pes, [128, 4, 3]) for _ in range(2)] for _ in range(2)]
    t_rd = [[Tok(), Tok()], [Tok(), Tok()]]
    accc = [[sb(pes, [128, 4, 64]) for _ in range(2)] for _ in range(2)]
    t_ac = [[Tok(), Tok()], [Tok(), Tok()]]
    imp = sb(pes, [128, 64])
    sc1 = sb(pes, [128, 64])
    sc2 = sb(pes, [128, 64])
    m8 = sb(pes, [128, 16])
    t_tk = Tok()
    ns_both = [sb(pes, [128, 128]) for _ in range(2)]
    t_ns = [Tok(), Tok()]
    Qaug = [[sb(pes, [128, 4, 128], BF16) for _ in range(2)] for _ in range(2)]
    t_qa = [[Tok(), Tok()], [Tok(), Tok()]]
    coef = sb(pes, [128, 12])
    bs_ = sb(pes, [128, 4, 64])
    bw_ = sb(pes, [128, 4, 64])
    bc_ = sb(pes, [128, 4, 64])
    t_bs, t_bw, t_bc = Tok(), Tok(), Tok()
    acc = sb(pes, [128, 64])
    t_cb = Tok()
    otok = [sb(pes, [128, 512], BF16) for _ in range(2)]
    t_ot = [Tok(), Tok()]
    oTs = [sb(pes, [128, 4, 512], BF16) for _ in range(2)]
    t_oTs = [Tok(), Tok()]
    onsa_v = DR["onsaT_d"].rearrange("c p t -> p c t")
    cnt = dict(s=0, p=0)
    pend = []

    def flush():
        while pend:
            pend.pop(0)()

    def tile(lhsT, lhs_reads, qsl, extra, pv, rhs=None, rhs_reads=None):
        i = cnt["s"] % NPS
        cnt["s"] += 1
        j = cnt["p"] % NPT
        cnt["p"] += 1
        rhs_ = QT[:, :, qsl] if rhs is None else rhs
        fw.op("pe", lambda e: mm(e, pS[i][:].rearrange("p (r q) -> p r q", r=4), lhsT=lhsT, rhs=rhs_, start=True, stop=(len(extra) == 0)),
              reads=lhs_reads + ([t_QT] if rhs is None else rhs_reads), writes=[t_pS[i]])
        for xi, (xl, xr_, xreads) in enumerate(extra):
            fw.op("pe", lambda e: mm(e, pS[i][:], lhsT=xl, rhs=xr_, start=False, stop=(xi == len(extra) - 1)),
                  reads=xreads, writes=[t_pS[i]])
        fw.op("act", lambda e: e.activation(out=PT[j][:], in_=pS[i][:], func=AF.Exp), reads=[t_pS[i]], writes=[t_PT[j]])
        while len(pend) > 1:
            pend.pop(0)()
        pend.append(lambda: pv(j))

    def stage_a(qt):
        p = qt % 2
        qsl = slice(qt * 128, (qt + 1) * 128)
        fw.op("dve", lambda e: e.tensor_scalar(out=elig[p][:], in0=jidx1[:], scalar1=curm2[:, qt:qt + 1], scalar2=None, op0=ALU.is_le),
              reads=[t_tab], writes=[t_el[p]])
        for g in range(2):
            nts = [nt for nt in range(2) if qt - 16 * nt >= 0]
            for ni, nt in enumerate(nts):
                dlt = qt - 16 * nt
                extra = [(ident[:], cmaskb4[:, dlt, :], [t_const, t_tab])] if dlt <= 16 else []

                def pv_c(j, ni=ni, nt=nt, g=g, last=(ni == len(nts) - 1)):
                    for r in range(4):
                        fw.op("pe", lambda e: mm(e, pOc[:, r, :], lhsT=PT[j][:, r * 128:(r + 1) * 128], rhs=Vc[:, nt, g, :],
                                                 start=(ni == 0 and r == 0), stop=last), reads=[t_PT[j], t_V], writes=[t_pOc])
                        fw.op("pe", lambda e: mm(e, pImp[:, r, :], lhsT=PT[j][:, r * 128:(r + 1) * 128], rhs=ovl[:, nt, :],
                                                 start=(ni == 0 and r == 0), stop=last), reads=[t_PT[j], t_tab], writes=[t_pImp])
                if ni == len(nts) - 1:
                    tile(Kc[:, g, nt * 128:(nt + 1) * 128], [t_Kc], qsl, extra,
                         lambda j, pv_c=pv_c, g=g: (pv_c(j), chain_a(qt, g)))
                else:
                    tile(Kc[:, g, nt * 128:(nt + 1) * 128], [t_Kc], qsl, extra, pv_c)

    def chain_a(qt, g):
        if True:
            p = qt % 2
            rdg, t_rdg = rd[p][g], t_rd[p][g]
            fw.op("dve", lambda e: e.tensor_scalar(out=rdg[:, :, 0], in0=pOc[:, :, 64], scalar1=1e-30, scalar2=None, op0=ALU.max),
                  reads=[t_pOc], writes=[t_rdg])
            fw.op("dve", lambda e: e.reciprocal(out=rdg[:, :, 0], in_=rdg[:, :, 0]), reads=[t_rdg], writes=[t_rdg])
            fw.op("dve", lambda e: e.tensor_scalar(out=imp[:], in0=pImp[:, 0, :], scalar1=rdg[:, 0, 0:1], scalar2=None, op0=ALU.mult),
                  reads=[t_pImp, t_rdg], writes=[t_tk])
            for r in range(1, 4):
                fw.op("dve", lambda e: e.scalar_tensor_tensor(out=imp[:], in0=pImp[:, r, :], scalar=rdg[:, r, 0:1], in1=imp[:],
                                                              op0=ALU.mult, op1=ALU.add), reads=[t_pImp, t_rdg, t_tk], writes=[t_tk])
            for r in range(4):
                fw.op("dve", lambda e: e.tensor_scalar(out=accc[p][g][:, r, :], in0=pOc[:, r, 0:64], scalar1=rdg[:, r, 0:1], scalar2=None, op0=ALU.mult),
                      reads=[t_pOc, t_rdg], writes=[t_ac[p][g]])
            fw.op("dve", lambda e: e.scalar_tensor_tensor(out=sc1[:], in0=imp[:], scalar=1.0, in1=elig[p][:], op0=ALU.add, op1=ALU.mult),
                  reads=[t_tk, t_el[p]], writes=[t_tk])
            fw.op("dve", lambda e: e.tensor_scalar(out=sc1[:], in0=sc1[:], scalar1=-1.0, scalar2=None, op0=ALU.add), reads=[t_tk], writes=[t_tk])
            fw.op("dve", lambda e: e.max(out=m8[:, 0:8], in_=sc1[:]), reads=[t_tk], writes=[t_tk])
            fw.op("dve", lambda e: e.match_replace(out=sc2[:], in_to_replace=m8[:, 0:8], in_values=sc1[:], imm_value=-2.0),
                  reads=[t_tk], writes=[t_tk])
            fw.op("dve", lambda e: e.max(out=m8[:, 8:16], in_=sc2[:]), reads=[t_tk], writes=[t_tk])
            fw.op("dve", lambda e: e.scalar_tensor_tensor(out=ns_both[p][:, (1 - g) * 64:(2 - g) * 64], in0=sc1[:], scalar=m8[:, 12:13], in1=elig[p][:],
                                                          op0=ALU.is_lt, op1=ALU.mult), reads=[t_tk, t_el[p]], writes=[t_ns[p]])
            fw.op("dve", lambda e: e.tensor_scalar(out=rdg[:, :, 0], in0=rdg[:, :, 0], scalar1=0.0, scalar2=1.0, op0=ALU.mult, op1=ALU.add),
                  reads=[t_rdg], writes=[t_rdg])

    def stage_b(qt, mid=None):
        p = qt % 2
        qsl = slice(qt * 128, (qt + 1) * 128)
        fw.op("pe", lambda e: e.transpose(out=pNsT, in_=ns_both[p][:], identity=identf[:]), reads=[t_ns[p], t_tab], writes=[t_pOw])
        for g in range(2):
            own = slice(g * 64, (g + 1) * 64)
            spare = slice((1 - g) * 64, (2 - g) * 64)
            fw.op("dve", lambda e: e.tensor_copy(out=Qaug[p][g][spare, :, :], in_=pNsT[spare, :].unsqueeze(1).to_broadcast([64, 4, 128])),
                  reads=[t_pOw], writes=[t_qa[p][g]])
            fw.op("pool", lambda e: e.tensor_copy(out=Qaug[p][g][own, :, :], in_=QT[own, :, qsl]), reads=[t_QT], writes=[t_qa[p][g]])
        for g in range(2):
            if g == 1 and mid is not None:
                mid()
            kts = list(range(max(0, qt - 4), qt + 1))
            for ki, kt in enumerate(kts):
                extra = []
                if kt == qt:
                    extra.append((ident[:], causb4[:], [t_const, t_tab]))
                elif kt == qt - 4:
                    extra.append((ident[:], winb4[:], [t_const, t_tab]))

                def pv_w(j, ki=ki, kt=kt, g=g, last=(ki == len(kts) - 1)):
                    for r in range(4):
                        fw.op("pe", lambda e: mm(e, pOw[:, r, :], lhsT=PT[j][:, r * 128:(r + 1) * 128], rhs=Vw[:, kt, g, :],
                                                 start=(ki == 0 and r == 0), stop=last), reads=[t_PT[j], t_V], writes=[t_pOw])
                tile(Kpad[:, 2 + g, kt * 128:(kt + 1) * 128], [t_K], qsl, extra, pv_w)
            for kt in range(qt + 1):
                extra = []
                if kt == qt:
                    extra.append((ident[:], causb4[:], [t_const, t_tab]))

                def pv_s(j, kt=kt, g=g, last=(kt == qt)):
                    for r in range(4):
                        fw.op("pe", lambda e: mm(e, pOs[:, r, :], lhsT=PT[j][:, r * 128:(r + 1) * 128], rhs=Vs[:, kt, g, :],
                                                 start=(kt == 0 and r == 0), stop=last), reads=[t_PT[j], t_V], writes=[t_pOs])
                if kt == qt:
                    tile(Kpad[:, g, kt * 128:(kt + 1) * 128], [t_K], qsl, extra,
                         lambda j, pv_s=pv_s, g=g: (pv_s(j), combine_b(qt, g)), rhs=Qaug[p][g][:], rhs_reads=[t_qa[p][g]])
                else:
                    tile(Kpad[:, g, kt * 128:(kt + 1) * 128], [t_K], qsl, extra, pv_s, rhs=Qaug[p][g][:], rhs_reads=[t_qa[p][g]])

    def combine_b(qt, g):
        if True:
            p = qt % 2
            rdg, t_rdg = rd[p][g], t_rd[p][g]
            fw.op("dve", lambda e: e.reciprocal(out=rdg[:, :, 1], in_=pOs[:, :, 64]), reads=[t_pOs], writes=[t_rdg])
            fw.op("dve", lambda e: e.reciprocal(out=rdg[:, :, 2], in_=pOw[:, :, 64]), reads=[t_pOw], writes=[t_rdg])
            fw.op("dve", lambda e: e.tensor_tensor(out=coef[:], in0=ng[:, qt, g * 12:(g + 1) * 12], in1=rdg[:].rearrange("p r b -> p (r b)"),
                                                   op=ALU.mult), reads=[t_tab, t_rdg], writes=[t_cb])
            cf = coef[:].rearrange("p (r b) -> p r b", b=3)
            fw.op("dve", lambda e: e.tensor_tensor(out=bs_[:], in0=pOs[:, :, 0:64], in1=cf[:, :, 1:2].to_broadcast([128, 4, 64]), op=ALU.mult),
                  reads=[t_pOs, t_cb], writes=[t_bs])
            fw.op("dve", lambda e: e.tensor_tensor(out=bw_[:], in0=pOw[:, :, 0:64], in1=cf[:, :, 2:3].to_broadcast([128, 4, 64]), op=ALU.mult),
                  reads=[t_pOw, t_cb], writes=[t_bw])
            fw.op("pool", lambda e: e.tensor_tensor(out=bc_[:], in0=accc[p][g][:], in1=cf[:, :, 0:1].to_broadcast([128, 4, 64]), op=ALU.mult),
                  reads=[t_ac[p][g], t_cb], writes=[t_bc])
            fw.op("pool", lambda e: e.tensor_tensor(out=bc_[:], in0=bc_[:], in1=bs_[:], op=ALU.add), reads=[t_bs, t_bc], writes=[t_bc])
            fw.op("pool", lambda e: e.tensor_tensor(out=otok[p][:, g * 256:(g + 1) * 256].rearrange("p (r d) -> p r d", r=4), in0=bc_[:], in1=bw_[:],
                                                    op=ALU.add), reads=[t_bc, t_bw], writes=[t_ot[p]])

    def stage_t(qt):
        p = qt % 2
        sb_ = (qt // 4) % 2
        for c in range(4):
            fw.op("pe", lambda e: e.transpose(out=pTT[:, c, :], in_=otok[p][:, c * 128:(c + 1) * 128], identity=ident[:]),
                  reads=[t_ot[p], t_const], writes=[t_pT])
        fw.op("dve", lambda e: e.tensor_copy(out=oTs[sb_][:, :, (qt % 4) * 128:(qt % 4 + 1) * 128], in_=pTT[:]),
              reads=[t_pT], writes=[t_oTs[sb_]])
        if qt % 4 == 3:
            fw.dma("sp", onsa_v[:, :, (qt // 4) * 512:(qt // 4 + 1) * 512], oTs[sb_][:], reads=[t_oTs[sb_]])

    NQ = S // 128
    stage_a(0)
    for qt in range(NQ):
        if qt + 1 < NQ:
            stage_a(qt + 1)
        else:
            flush()
        stage_b(qt, (lambda q=qt - 1: stage_t(q)) if qt >= 1 else None)
    flush()
    stage_t(NQ - 1)


def p2_gla(nc, fw, pes, l, W, w_in, hT, t_hT, pm, t_pm, pa, t_pa, wt, t_w, DR, sb, psum):
    CT = DR["CT"]
    t_c = Tok()
    segmask = sb(pes, [128, 512])
    trim = sb(pes, [128, 128])
    wg17 = sb(pes, [17, 256], BF16)
    ngb = sb(pes, [128, 128])
    fw.dma("sp", segmask[:], CT["segmask"], writes=[t_c])
    fw.dma("sp", trim[:], CT["trim"], writes=[t_c])
    wload(fw, wg17[:], W["gla_wg"][l], t_c)
    fw.dma("sp", ngb[:], W["gla_norm"][l].partition_broadcast(128), writes=[t_c])
    glowT = sb(pes, [17, S], BF16)
    t_gl = Tok()
    fw.op("pool", lambda e: e.memset(glowT[:], 1.0), writes=[t_gl])
    wlow = sb(pes, [128, 8, 16], BF16)
    wload(fw, wlow[:], w_in[:, :, O_GLOW:O_GLOW + 16], t_c)
    for T in range(S // 512):
        i = T % 2
        tsl = slice(T * 512, (T + 1) * 512)
        for k in range(8):
            fw.op("pe", lambda e: e.matmul(pm[i][0:16, :], lhsT=wlow[:, k, :], rhs=hT[:, k, tsl], start=(k == 0), stop=(k == 7)),
                  reads=[t_c, t_hT], writes=[t_pm[i]])
        fw.op("act", lambda e: e.activation(out=glowT[0:16, tsl], in_=pm[i][0:16, :], func=AF.Copy), reads=[t_pm[i]], writes=[t_gl])
    la = [sb(pes, [128, 512]) for _ in range(2)]
    t_la = [Tok(), Tok()]
    cs = [sb(pes, [128, 512]) for _ in range(2)]
    t_cs_ = [Tok(), Tok()]
    eb = [sb(pes, [128, 512]) for _ in range(2)]
    t_eb = [Tok(), Tok()]
    enb = [sb(pes, [128, 512]) for _ in range(2)]
    t_enb = [Tok(), Tok()]
    qo = [sb(pes, [128, 512], BF16) for _ in range(2)]
    t_qo = [Tok(), Tok()]
    ko = [sb(pes, [128, 512], BF16) for _ in range(2)]
    t_ko = [Tok(), Tok()]
    edec = sb(pes, [128, 2, 32])
    t_ed = Tok()
    px = [psum(pes, [128, 512]) for _ in range(2)]
    t_px = [Tok(), Tok()]
    wq2 = sb(pes, [128, 8, 128], BF16)
    t_wq2 = Tok()
    it = 0
    for hc in range(2):
        wload(fw, wt[0][:], w_in[:, :, O_GQ + hc * 128:O_GQ + (hc + 1) * 128], t_w[0])
        wload(fw, wq2[:], w_in[:, :, O_GK + hc * 128:O_GK + (hc + 1) * 128], t_wq2)
        for T in range(S // 512):
            i = it % 2
            it += 1
            tsl = slice(T * 512, (T + 1) * 512)
            fw.op("pe", lambda e: e.matmul(px[i][:], lhsT=wg17[:, hc * 128:(hc + 1) * 128], rhs=glowT[:, tsl], start=True, stop=True),
                  reads=[t_c, t_gl], writes=[t_px[i]])
            fw.op("act", lambda e: e.activation(out=la[i][:], in_=px[i][:], func=AF.Exp, scale=-1.0), reads=[t_px[i]], writes=[t_la[i]])
            fw.op("act", lambda e: e.activation(out=la[i][:], in_=la[i][:], func=AF.Ln, bias=1.0), reads=[t_la[i]], writes=[t_la[i]])
            fw.op("dve", lambda e: e.tensor_tensor_scan(out=cs[i][:], data0=segmask[:], data1=la[i][:], initial=0.0, op0=ALU.mult, op1=ALU.add),
                  reads=[t_la[i], t_c], writes=[t_cs_[i]])
            fw.op("act", lambda e: e.activation(out=eb[i][:], in_=cs[i][:], func=AF.Exp, scale=-1.0 / 16), reads=[t_cs_[i]], writes=[t_eb[i]])
            fw.op("act", lambda e: e.activation(out=enb[i][:], in_=cs[i][:], func=AF.Exp, scale=1.0 / 16), reads=[t_cs_[i]], writes=[t_enb[i]])
            fw.op("dve", lambda e: e.tensor_copy(out=edec[:, hc, T * 4:(T + 1) * 4], in_=eb[i][:, 127:512:128]), reads=[t_eb[i]], writes=[t_ed])
            for k in range(8):
                fw.op("pe", lambda e: e.matmul(pm[i][:], lhsT=wt[0][:, k, :], rhs=hT[:, k, tsl], start=(k == 0), stop=(k == 7)),
                      reads=[t_w[0], t_hT], writes=[t_pm[i]])
            fw.op("dve", lambda e: e.scalar_tensor_tensor(out=qo[i][:], in0=pm[i][:], scalar=0.125, in1=eb[i][:], op0=ALU.mult, op1=ALU.mult),
                  reads=[t_pm[i], t_eb[i]], writes=[t_qo[i]])
            fw.dma("sp", DR["qeT_d"][hc][:, tsl], qo[i][:], reads=[t_qo[i]])
            for k in range(8):
                fw.op("pe", lambda e: e.matmul(pa[i][:], lhsT=wq2[:, k, :], rhs=hT[:, k, tsl], start=(k == 0), stop=(k == 7)),
                      reads=[t_wq2, t_hT], writes=[t_pa[i]])
            fw.op("dve", lambda e: e.tensor_tensor(out=ko[i][:], in0=pa[i][:], in1=enb[i][:], op=ALU.mult),
                  reads=[t_pa[i], t_enb[i]], writes=[t_ko[i]])
            fw.dma("sp", DR["keT_d"][hc][:, tsl], ko[i][:], reads=[t_ko[i]])
    fw.dma("sp", DR["edec_d"], edec[:], reads=[t_ed])
    wk = sb(pes, [128, 8, 256], BF16)
    wv = sb(pes, [128, 8, 512], BF16)
    wr = sb(pes, [128, 8, 512], BF16)
    t_wtm = Tok()
    wload(fw, wk[:], w_in[:, :, O_GK:O_GK + 256], t_wtm)
    for k in range(0, 8, 4):
        wload(fw, wv[:, k:k + 4, :], w_in[:, k:k + 4, O_GV:O_GV + 512], t_wtm)
        wload(fw, wr[:, k:k + 4, :], w_in[:, k:k + 4, O_GR:O_GR + 512], t_wtm)
    pd_ = psum(pes, [128, 256])
    t_pd = Tok()
    lat = [sb(pes, [128, 256]) for _ in range(2)]
    t_lat = [Tok(), Tok()]
    dex = [sb(pes, [128, 256]) for _ in range(2)]
    t_dex = [Tok(), Tok()]
    sg = [sb(pes, [128, 512]) for _ in range(2)]
    t_sg = [Tok(), Tok()]
    NST = 2
    klst = [sb(pes, [128, NST, 256], BF16) for _ in range(2)]
    vst = [sb(pes, [128, NST, 512], BF16) for _ in range(2)]
    grst = [sb(pes, [128, NST, 512], BF16) for _ in range(2)]
    t_st = [Tok(), Tok()]
    kl_v = DR["kl_d"].rearrange("(t p) c -> p t c", p=128)
    gv_v = DR["gv_d"].rearrange("(t p) c -> p t c", p=128)
    gr_v = DR["GR_d"].rearrange("(t p) c -> p t c", p=128)
    def gate_front(tt):
        i = tt % 2
        tok = slice(tt * 128, (tt + 1) * 128)
        fw.op("pe", lambda e: e.matmul(px[i][:, 0:256], lhsT=glowT[:, tok], rhs=wg17[:], start=True, stop=True),
              reads=[t_c, t_gl], writes=[t_px[i]])
        fw.op("act", lambda e: e.activation(out=lat[i][:], in_=px[i][:, 0:256], func=AF.Exp, scale=-1.0), reads=[t_px[i]], writes=[t_lat[i]])
        fw.op("act", lambda e: e.activation(out=lat[i][:], in_=lat[i][:], func=AF.Ln, bias=1.0), reads=[t_lat[i]], writes=[t_lat[i]])

    gate_front(0)
    for tt in range(S // 128):
        i = tt % 2
        st = (tt // NST) % 2
        si = tt % NST
        tok = slice(tt * 128, (tt + 1) * 128)
        for k in range(8):
            fw.op("pe", lambda e: e.matmul(pa[i][:, 0:256], lhsT=hT[:, k, tok], rhs=wk[:, k, :], start=(k == 0), stop=(k == 7)),
                  reads=[t_wtm, t_hT], writes=[t_pa[i]])
        if tt + 1 < S // 128:
            gate_front(tt + 1)
        fw.op("pe", lambda e: e.matmul(pd_[:], lhsT=trim[:], rhs=lat[i][:], start=True, stop=True), reads=[t_c, t_lat[i]], writes=[t_pd])
        fw.op("act", lambda e: e.activation(out=dex[i][:], in_=pd_[:], func=AF.Exp, scale=1.0 / 16), reads=[t_pd], writes=[t_dex[i]])
        fw.op("dve", lambda e: e.tensor_tensor(out=klst[st][:, si, :], in0=pa[i][:, 0:256], in1=dex[i][:], op=ALU.mult),
              reads=[t_pa[i], t_dex[i]], writes=[t_st[st]])
        for k in range(8):
            fw.op("pe", lambda e: e.matmul(pm[i][:], lhsT=hT[:, k, tok], rhs=wv[:, k, :], start=(k == 0), stop=(k == 7)),
                  reads=[t_wtm, t_hT], writes=[t_pm[i]])
        fw.op("act", lambda e: e.activation(out=vst[st][:, si, :], in_=pm[i][:], func=AF.Copy), reads=[t_pm[i]], writes=[t_st[st]])
        if si == NST - 1:
            g0 = tt - (NST - 1)
            fw.dma("sp", kl_v[:, g0:g0 + NST, :], klst[st][:], reads=[t_st[st]])
            fw.dma("sp", gv_v[:, g0:g0 + NST, :], vst[st][:], reads=[t_st[st]])
    t_st2 = [Tok(), Tok()]
    for tt in range(S // 128):
        i = tt % 2
        st = (tt // NST) % 2
        si = tt % NST
        tok = slice(tt * 128, (tt + 1) * 128)
        for k in range(8):
            fw.op("pe", lambda e: e.matmul(px[i][:], lhsT=hT[:, k, tok], rhs=wr[:, k, :], start=(k == 0), stop=(k == 7)),
                  reads=[t_wtm, t_hT], writes=[t_px[i]])
        fw.op("act", lambda e: e.activation(out=sg[i][:], in_=px[i][:], func=AF.Sigmoid), reads=[t_px[i]], writes=[t_sg[i]])
        fw.op("dve", lambda e: e.tensor_tensor(out=sg[i][:], in0=px[i][:], in1=sg[i][:], op=ALU.mult), reads=[t_px[i], t_sg[i]], writes=[t_sg[i]])
        fw.op("pool", lambda e: e.tensor_tensor(out=grst[st][:, si, :].rearrange("p (h v) -> p h v", h=4), in0=sg[i][:].rearrange("p (h v) -> p h v", h=4),
                                                in1=ngb[:].unsqueeze(1).to_broadcast([128, 4, 128]), op=ALU.mult),
              reads=[t_sg[i], t_c], writes=[t_st2[st]])
        if si == NST - 1:
            g0 = tt - (NST - 1)
            fw.dma("sp", gr_v[:, g0:g0 + NST, :], grst[st][:], reads=[t_st2[st]])


def p4_gla(nc, fw, pes, l, DR, CT, CS, sb, psum):
    ident, t_const = CS["ident"], CS["t_const"]
    mm = lambda e, *a, **k: e.matmul(*a, skip_group_check=True, **k)
    NB = 4
    t_blk = [Tok() for _ in range(NB)]
    t_misc = Tok()
    qeP = sb(pes, [128, 4, S], BF16)
    keP = sb(pes, [128, 4, S], BF16)
    klP = sb(pes, [128, 32, 4, 128], BF16)
    gv = sb(pes, [128, 32, 512], BF16)
    GR = sb(pes, [128, 32, 512], BF16)
    edec = sb(pes, [128, 2, 32])
    tril = sb(pes, [128, 128])
    fw.dma("sp", edec[:], DR["edec_d"], writes=[t_misc])
    fw.dma("sp", tril[:], CT["tril"], writes=[t_misc])
    kl_v = DR["kl_d"].rearrange("(t p) c -> p t c", p=128)
    gv_v = DR["gv_d"].rearrange("(t p) c -> p t c", p=128)
    gr_v = DR["GR_d"].rearrange("(t p) c -> p t c", p=128)
    for bk in range(NB):
        tk = slice(bk * 1024, (bk + 1) * 1024)
        t0 = bk * 8
        tb = t_blk[bk]
        for i in range(4):
            fw.op("dve", lambda e: e.memset(qeP[:, i, tk], 0.0), writes=[tb])
            fw.op("pool", lambda e: e.memset(keP[:, i, tk], 0.0), writes=[tb])
        fw.op("dve", lambda e: e.memset(klP[:, t0:t0 + 8, :, :], 0.0), writes=[tb])
        for hc in range(2):
            for hh in range(2):
                hr = slice(hh * 64, (hh + 1) * 64)
                fw.dma("sp", qeP[hr, hh * 2 + hc, tk], DR["qeT_d"][hc][hr, tk], writes=[tb])
                fw.dma("sp", keP[hr, hh * 2 + hc, tk], DR["keT_d"][hc][hr, tk], writes=[tb])
        for h in range(4):
            fw.dma("sp", klP[:, t0:t0 + 8, h, (h % 2) * 64:(h % 2 + 1) * 64], kl_v[:, t0:t0 + 8, h * 64:(h + 1) * 64], writes=[tb])
        fw.dma("sp", gv[:, t0:t0 + 8, :], gv_v[:, t0:t0 + 8, :], writes=[tb])
        fw.dma("sp", GR[:, t0:t0 + 8, :], gr_v[:, t0:t0 + 8, :], writes=[tb])
    stf = sb(pes, [128, 2, 128])
    stb = [sb(pes, [128, 2, 128], BF16) for _ in range(2)]
    t_stf = Tok()
    t_stb = [Tok(), Tok()]
    fw.op("pool", lambda e: e.memset(stf[:], 0.0), writes=[t_stf])
    fw.op("pool", lambda e: e.memset(stb[1][:], 0.0), writes=[t_stb[1]])
    pA = [psum(pes, [128, 4, 128]) for _ in range(2)]
    t_pA = [Tok(), Tok()]
    pO = [psum(pes, [128, 512]) for _ in range(2)]
    t_pO = [Tok(), Tok()]
    pSt = psum(pes, [128, 2, 128])
    t_pSt = Tok()
    pTo = psum(pes, [128, 4, 128], BF16)
    t_pTo = Tok()
    AT = [sb(pes, [128, 4, 128], BF16) for _ in range(2)]
    t_AT = [Tok(), Tok()]
    junk = sb(pes, [128, 128], BF16)
    t_j = Tok()
    ss = [sb(pes, [128, 4]) for _ in range(2)]
    t_ss = [Tok(), Tok()]
    tg = [sb(pes, [128, 512]) for _ in range(2)]
    t_tg = [Tok(), Tok()]
    otok = [sb(pes, [128, 512], BF16) for _ in range(2)]
    t_ot = [Tok(), Tok()]
    oTs = [sb(pes, [128, 4, 512], BF16) for _ in range(2)]
    t_oTs = [Tok(), Tok()]
    ogla_v = DR["oglaT_d"].rearrange("c p t -> p c t")
    sk = Skew()
    for c in range(S // 128):
        csl = slice(c * 128, (c + 1) * 128)
        a = c % 2
        t_in = t_blk[c // 8]

        def s0(c=c, csl=csl, a=a, t_in=t_in):
            for h in range(4):
                hc, hh = h // 2, h % 2
                fw.op("pe", lambda e: mm(e, pA[a][:, h, :], lhsT=keP[:, hh * 2 + hc, csl], rhs=qeP[:, hh * 2 + hc, csl], start=(h == 0), stop=True),
                      reads=[t_in], writes=[t_pA[a]])
            fw.op("dve", lambda e: e.tensor_tensor(out=AT[a][:], in0=pA[a][:], in1=tril[:].unsqueeze(1).to_broadcast([128, 4, 128]), op=ALU.mult),
                  reads=[t_pA[a], t_misc], writes=[t_AT[a]])

        def s1(c=c, csl=csl, a=a, t_in=t_in):
            for h in range(4):
                hc, hh = h // 2, h % 2
                fw.op("pe", lambda e: mm(e, pSt[:, hc, :], lhsT=klP[:, c, h, :], rhs=gv[:, c, h * 128:(h + 1) * 128],
                                         start=(h == 0), stop=(hh == 1)), reads=[t_in], writes=[t_pSt])
            sprev, t_sprev = stb[(c - 1) % 2], t_stb[(c - 1) % 2]
            for h in range(4):
                fw.op("pe", lambda e: mm(e, pO[a][:, h * 128:(h + 1) * 128], lhsT=AT[a][:, h, :], rhs=gv[:, c, h * 128:(h + 1) * 128],
                                         start=(h == 0), stop=False), reads=[t_AT[a], t_in], writes=[t_pO[a]])
            for h in range(4):
                hc, hh = h // 2, h % 2
                fw.op("pe", lambda e: mm(e, pO[a][:, h * 128:(h + 1) * 128], lhsT=qeP[:, hh * 2 + hc, csl], rhs=sprev[:, hc, :],
                                         start=False, stop=True), reads=[t_in, t_sprev], writes=[t_pO[a]])
            for hc in range(2):
                fw.op("dve", lambda e: e.scalar_tensor_tensor(out=stf[:, hc, :], in0=stf[:, hc, :], scalar=edec[:, hc, c:c + 1], in1=pSt[:, hc, :],
                                                              op0=ALU.mult, op1=ALU.add), reads=[t_pSt, t_misc, t_stf], writes=[t_stf])
            fw.op("dve", lambda e: e.tensor_copy(out=stb[c % 2][:], in_=stf[:]), reads=[t_stf], writes=[t_stb[c % 2]])

        def s2(c=c, a=a, t_in=t_in):
            for h in range(4):
                fw.op("act", lambda e: e.activation(out=junk[:], in_=pO[a][:, h * 128:(h + 1) * 128], func=AF.Square, accum_out=ss[a][:, h:h + 1]),
                      reads=[t_pO[a]], writes=[t_j, t_ss[a]])
            fw.op("act", lambda e: e.activation(out=ss[a][:], in_=ss[a][:], func=AF.Sqrt, scale=1.0 / 128, bias=EPS), reads=[t_ss[a]], writes=[t_ss[a]])
            fw.op("dve", lambda e: e.reciprocal(out=ss[a][:], in_=ss[a][:]), reads=[t_ss[a]], writes=[t_ss[a]])
            fw.op("dve", lambda e: e.tensor_tensor(out=tg[a][:], in0=pO[a][:], in1=GR[:, c, :], op=ALU.mult), reads=[t_pO[a], t_in], writes=[t_tg[a]])
            fw.op("pool", lambda e: e.tensor_tensor(out=otok[a][:].rearrange("p (h v) -> p h v", h=4), in0=tg[a][:].rearrange("p (h v) -> p h v", h=4),
                                                    in1=ss[a][:].unsqueeze(2).to_broadcast([128, 4, 128]), op=ALU.mult),
                  reads=[t_tg[a], t_ss[a]], writes=[t_ot[a]])

        def s3(c=c, a=a):
            sb_ = (c // 4) % 2
            for cc in range(4):
                fw.op("pe", lambda e: e.transpose(out=pTo[:, cc, :], in_=otok[a][:, cc * 128:(cc + 1) * 128], identity=ident[:]),
                      reads=[t_ot[a], t_const], writes=[t_pTo])
            fw.op("act", lambda e: e.activation(out=oTs[sb_][:, :, (c % 4) * 128:(c % 4 + 1) * 128], in_=pTo[:], func=AF.Copy),
                  reads=[t_pTo], writes=[t_oTs[sb_]])
            if c % 4 == 3:
                fw.dma("sp", ogla_v[:, :, (c // 4) * 512:(c // 4 + 1) * 512], oTs[sb_][:], reads=[t_oTs[sb_]])

        sk.push([s0, s1, s2, s3])
    sk.drain()


def p5_mem(nc, fw, pes, l, W, mem_in, mqT_d, omemT_d, CS, sb, psum, norm_to_hT, after_w=None):
    ident, ones128, t_const = CS["ident"], CS["ones128"], CS["t_const"]
    gT = sb(pes, [128, 8])
    kn = sb(pes, [128, 1])
    t_g = Tok()
    fw.dma("sp", gT[:], W["mem_normT"][l], writes=[t_g])
    fw.dma("sp", kn[:], W["mem_kn"][l], writes=[t_g])
    mT = sb(pes, [128, 8, MEM], BF16)
    with ExitStack() as p1:
        t_mT = norm_to_hT(p1, mem_in, MEM // 128, gT, t_g, mT)
        fw.barrier()
    wkv = W["mem_w_kv"][l].rearrange("(c p) n -> p c n", p=128)
    kT = sb(pes, [128, 4, MEM], BF16)
    vaug = sb(pes, [128, 2, 4, 130], BF16)
    t_kT = Tok()
    t_v = Tok()
    fw.op("pool", lambda e: e.memset(vaug[:], 1.0), writes=[t_v])
    with ExitStack() as p2:
        wt = [sb(p2, [128, 8, 128], BF16) for _ in range(2)]
        t_w = [Tok(), Tok()]
        wv = sb(p2, [128, 8, 512], BF16)
        t_wv = Tok()
        pk = [psum(p2, [128, MEM]) for _ in range(2)]
        t_pk = [Tok(), Tok()]
        pa = [psum(p2, [128, MEM]) for _ in range(2)]
        t_pa = [Tok(), Tok()]
        pv = [psum(p2, [128, 512]) for _ in range(2)]
        t_pv = [Tok(), Tok()]
        sq = [sb(p2, [128, MEM], BF16) for _ in range(2)]
        t_sq = [Tok(), Tok()]
        rstd = [sb(p2, [128, MEM]) for _ in range(2)]
        t_rs = [Tok(), Tok()]
        wload(fw, wv[:], wkv[:, :, 512:1024], t_wv)
        for h in range(4):
            b = h % 2
            wload(fw, wt[b][:], wkv[:, :, h * 128:(h + 1) * 128], t_w[b])
            for k in range(8):
                fw.op("pe", lambda e: e.matmul(pk[b][:], lhsT=wt[b][:, k, :], rhs=mT[:, k, :], start=(k == 0), stop=(k == 7)),
                      reads=[t_w[b], t_mT], writes=[t_pk[b]])
            fw.op("act", lambda e: e.activation(out=sq[b][:], in_=pk[b][:], func=AF.Square), reads=[t_pk[b]], writes=[t_sq[b]])
            fw.op("pe", lambda e: e.matmul(pa[b][:], lhsT=ones128[:], rhs=sq[b][:], start=True, stop=True),
                  reads=[t_sq[b], t_const], writes=[t_pa[b]])
            fw.op("act", lambda e: e.activation(out=rstd[b][:], in_=pa[b][:], func=AF.Sqrt, scale=1.0 / 128, bias=EPS),
                  reads=[t_pa[b]], writes=[t_rs[b]])
            fw.op("dve", lambda e: e.reciprocal(out=rstd[b][:], in_=rstd[b][:]), reads=[t_rs[b]], writes=[t_rs[b]])
            fw.op("dve", lambda e: e.scalar_tensor_tensor(out=kT[:, h, :], in0=pk[b][:], scalar=kn[:, 0:1], in1=rstd[b][:],
                                                          op0=ALU.mult, op1=ALU.mult),
                  reads=[t_pk[b], t_rs[b], t_g], writes=[t_kT])
        for mt in range(2):
            for k in range(8):
                fw.op("pe", lambda e: e.matmul(pv[mt][:], lhsT=mT[:, k, mt * 128:(mt + 1) * 128], rhs=wv[:, k, :],
                                               start=(k == 0), stop=(k == 7)),
                      reads=[t_wv, t_mT], writes=[t_pv[mt]])
            fw.op("act", lambda e: e.activation(out=vaug[:, mt, :, 0:128], in_=pv[mt][:].rearrange("p (h d) -> p h d", h=4),
                                                func=AF.Copy), reads=[t_pv[mt]], writes=[t_v])
        fw.barrier()
    if after_w is not None:
        after_w()
    qt = [sb(pes, [128, 4, 512], BF16) for _ in range(2)]
    t_q = [Tok(), Tok()]
    ps_s = [psum(pes, [128, 2, 512]) for _ in range(2)]
    t_ps = [Tok(), Tok()]
    pT = [sb(pes, [128, 2, 512], BF16) for _ in range(2)]
    t_pT = [Tok(), Tok()]
    ps_o = [psum(pes, [128, 130]) for _ in range(2)]
    t_po = [Tok(), Tok()]
    rden = [sb(pes, [128, 1]) for _ in range(2)]
    t_rd = [Tok(), Tok()]
    otok = [[sb(pes, [128, 512], BF16) for _ in range(4)] for _ in range(2)]
    t_ot = [[Tok() for _ in range(4)] for _ in range(2)]
    ps_t = [psum(pes, [128, 4, 128], BF16) for _ in range(2)]
    t_pt = [Tok(), Tok()]
    oT = [sb(pes, [128, 4, 512], BF16) for _ in range(2)]
    t_oT = [Tok(), Tok()]
    omem_v = omemT_d.rearrange("c p t -> p c t")
    cnt = dict(o=0, j=0)
    sk = Skew()
    NT = S // 512
    fw.dma("sp", qt[0][:], mqT_d[:, :, 0:512], writes=[t_q[0]])
    for T in range(NT):
        b = T % 2
        tsl = slice(T * 512, (T + 1) * 512)
        if T + 1 < NT:
            fw.dma("sp", qt[1 - b][:], mqT_d[:, :, (T + 1) * 512:(T + 2) * 512], writes=[t_q[1 - b]])
        for h in range(4):
            j = cnt["j"] % 2
            cnt["j"] += 1

            def s0(h=h, j=j, b=b):
                for mt in range(2):
                    fw.op("pe", lambda e: e.matmul(ps_s[j][:, mt, :], lhsT=kT[:, h, mt * 128:(mt + 1) * 128], rhs=qt[b][:, h, :],
                                                   start=True, stop=True), reads=[t_kT, t_q[b]], writes=[t_ps[j]])
                fw.op("act", lambda e: e.activation(out=pT[j][:], in_=ps_s[j][:], func=AF.Exp), reads=[t_ps[j]], writes=[t_pT[j]])

            def s1(h=h, j=j, b=b):
                for qs in range(4):
                    o2 = cnt["o"] % 2
                    cnt["o"] += 1
                    for mt in range(2):
                        fw.op("pe", lambda e: e.matmul(ps_o[o2][:], lhsT=pT[j][:, mt, qs * 128:(qs + 1) * 128], rhs=vaug[:, mt, h, :],
                                                       start=(mt == 0), stop=(mt == 1)), reads=[t_pT[j], t_v], writes=[t_po[o2]])
                    fw.op("dve", lambda e: e.reciprocal(out=rden[o2][:], in_=ps_o[o2][:, 128:129]), reads=[t_po[o2]], writes=[t_rd[o2]])
                    fw.op("dve", lambda e: e.tensor_scalar(out=otok[b][qs][:, h * 128:(h + 1) * 128], in0=ps_o[o2][:, 0:128],
                                                           scalar1=rden[o2][:, 0:1], scalar2=None, op0=ALU.mult),
                          reads=[t_po[o2], t_rd[o2]], writes=[t_ot[b][qs]])

            def s2(b=b, tsl=tsl):
                for qs in range(4):
                    jj = qs % 2
                    for c in range(4):
                        fw.op("pe", lambda e: e.transpose(out=ps_t[jj][:, c, :], in_=otok[b][qs][:, c * 128:(c + 1) * 128], identity=ident[:]),
                              reads=[t_ot[b][qs], t_const], writes=[t_pt[jj]])
                    fw.op("act", lambda e: e.activation(out=oT[b][:, :, qs * 128:(qs + 1) * 128], in_=ps_t[jj][:], func=AF.Copy),
                          reads=[t_pt[jj]], writes=[t_oT[b]])
                fw.dma("sp", omem_v[:, :, tsl], oT[b][:], reads=[t_oT[b]])

            sk.push([s0, s1, s2] if h == 3 else [s0, s1])
    sk.drain()


def p6_load(nc, fw, pes, l, W, sb):
    w_in = W["w_in"][l].rearrange("(c p) n -> p c n", p=128)
    Wb = sb(pes, [128, 3, 4, D], BF16)
    Wm = sb(pes, [128, 8, 3 * D], BF16)
    Wo = sb(pes, [128, 8, D], BF16)
    bm = sb(pes, [128, 24])
    t_w = Tok()
    t_wm = [Tok() for _ in range(8)]
    t_wo = Tok()

    def issue():
        fw.dma("sp", bm[:], W["b_mergeT"][l], writes=[t_w])
        wbv = W["w_branch"][l].rearrange("b (k p) n -> p b k n", p=128)
        for d2 in range(4):
            for br in range(3):
                wload(fw, Wb[:, br, :, d2 * 256:(d2 + 1) * 256], wbv[:, br, :, d2 * 256:(d2 + 1) * 256], t_wm[d2])
                c0 = br * D + d2 * 256
                wload(fw, Wm[:, :, c0:c0 + 256], w_in[:, :, O_MERGE + c0:O_MERGE + c0 + 256], t_wm[d2])
        wov = W["w_out"][l].rearrange("(k p) n -> p k n", p=128)
        for k in range(0, 8, 4):
            wload(fw, Wo[:, k:k + 4, :], wov[:, k:k + 4, :], t_wo)

    return (Wb, Wm, Wo, bm, t_w, t_wm, t_wo), issue


def p6_merge(nc, fw, pes, l, W, hT_d, onsaT_d, oglaT_d, omemT_d, x_src, x1_d, sb, psum, w6):
    Wb, Wm, Wo, bm, t_w, t_wm, t_wo = w6
    hTt = [sb(pes, [128, 8, 512], BF16) for _ in range(2)]
    oTt = [sb(pes, [128, 3, 4, 512], BF16) for _ in range(2)]
    xt = [sb(pes, [128, 4, D]) for _ in range(2)]
    t_in = [Tok(), Tok()]
    t_x = [Tok(), Tok()]
    psB = [psum(pes, [128, 512]) for _ in range(3)]
    psG = [psum(pes, [128, 512]) for _ in range(3)]
    t_pB = [Tok() for _ in range(3)]
    t_pG = [Tok() for _ in range(3)]
    pso = [psum(pes, [128, 512]) for _ in range(2)]
    t_po = [Tok(), Tok()]
    gate = [sb(pes, [128, 512]) for _ in range(3)]
    t_gt = [Tok() for _ in range(3)]
    mg = [sb(pes, [128, 512]) for _ in range(3)]
    t_mg = [Tok() for _ in range(3)]
    mT = [sb(pes, [128, 8, 512], BF16) for _ in range(2)]
    t_mT = [Tok(), Tok()]
    xo = [sb(pes, [128, 512]) for _ in range(2)]
    t_xo = [Tok(), Tok()]
    hv = hT_d.rearrange("c p t -> p c t")
    ovs = [o.rearrange("c p t -> p c t") for o in (onsaT_d, oglaT_d, omemT_d)]
    oc = 0
    def load_in(T):
        b = T % 2
        tsl = slice(T * 512, (T + 1) * 512)
        fw.dma("sp", hTt[b][:], hv[:, :, tsl], writes=[t_in[b]])
        for br in range(3):
            fw.dma("sp", oTt[b][:, br, :, :], ovs[br][:, :, tsl], writes=[t_in[b]])
        fw.dma("sp", xt[b][:], x_src[tsl, :].rearrange("(s p) d -> p s d", p=128), writes=[t_x[b]])

    load_in(0)
    for T in range(S // 512):
        b = T % 2
        tsl = slice(T * 512, (T + 1) * 512)
        if T + 1 < S // 512:
            load_in(T + 1)
        for dc in range(8):
            dsl = slice(dc * 128, (dc + 1) * 128)
            for br in range(3):
                for k in range(4):
                    fw.op("pe", lambda e: e.matmul(psB[br][:], lhsT=Wb[:, br, k, dsl], rhs=oTt[b][:, br, k, :], start=(k == 0), stop=(k == 3)),
                          reads=[t_wm[dc // 2], t_in[b]], writes=[t_pB[br]])
                for k in range(8):
                    fw.op("pe", lambda e: e.matmul(psG[br][:], lhsT=Wm[:, k, br * D + dc * 128:br * D + (dc + 1) * 128], rhs=hTt[b][:, k, :],
                                                   start=(k == 0), stop=(k == 7)), reads=[t_wm[dc // 2], t_in[b]], writes=[t_pG[br]])
                fw.op("act", lambda e: e.activation(out=gate[br][:], in_=psG[br][:], func=AF.Sigmoid, bias=bm[:, br * 8 + dc:br * 8 + dc + 1]),
                      reads=[t_pG[br], t_w], writes=[t_gt[br]])
                fw.op("dve", lambda e: e.tensor_tensor(out=mg[br][:], in0=psB[br][:], in1=gate[br][:], op=ALU.mult),
                      reads=[t_pB[br], t_gt[br]], writes=[t_mg[br]])
            fw.op("pool", lambda e: e.tensor_tensor(out=mg[0][:], in0=mg[0][:], in1=mg[1][:], op=ALU.add),
                  reads=[t_mg[1]], writes=[t_mg[0]])
            fw.op("pool", lambda e: e.tensor_tensor(out=mT[b][:, dc, :], in0=mg[0][:], in1=mg[2][:], op=ALU.add),
                  reads=[t_mg[0], t_mg[2]], writes=[t_mT[b]])
        for ts_ in range(4):
            for dh in range(2):
                o2 = oc % 2
                oc += 1
                for k in range(8):
                    fw.op("pe", lambda e: e.matmul(pso[o2][:], lhsT=mT[b][:, k, ts_ * 128:(ts_ + 1) * 128], rhs=Wo[:, k, dh * 512:(dh + 1) * 512],
                                                   start=(k == 0), stop=(k == 7)), reads=[t_mT[b], t_wo], writes=[t_po[o2]])
                fw.op("dve", lambda e: e.tensor_tensor(out=xo[o2][:], in0=pso[o2][:], in1=xt[b][:, ts_, dh * 512:(dh + 1) * 512], op=ALU.add),
                      reads=[t_po[o2], t_x[b]], writes=[t_xo[o2]])
                fw.dma("sp", x1_d[T * 512 + ts_ * 128:T * 512 + (ts_ + 1) * 128, dh * 512:(dh + 1) * 512], xo[o2][:], reads=[t_xo[o2]])


def p7_mlp(nc, fw, pes, l, W, x1_d, x_dst, CS, sb, psum):
    ident, t_const = CS["ident"], CS["t_const"]
    Wu = sb(pes, [128, 8, 4 * D], BF16)
    Wd = sb(pes, [128, 32, D], BF16)
    gT = sb(pes, [128, 8])
    t_w = Tok()
    t_wu = [Tok() for _ in range(4)]
    t_wd = [Tok() for _ in range(8)]
    wuv = W["w_up"][l].rearrange("(k p) n -> p k n", p=128)
    wdv = W["w_down"][l].rearrange("(k p) n -> p k n", p=128)
    fw.dma("sp", gT[:], W["ln_mlpT"][l], writes=[t_w])
    for cb in range(4):
        for kh in range(0, 8, 4):
            wload(fw, Wu[:, kh:kh + 4, cb * 1024:(cb + 1) * 1024], wuv[:, kh:kh + 4, cb * 1024:(cb + 1) * 1024], t_wu[cb])
    for kb in range(8):
        wload(fw, Wd[:, kb * 4:kb * 4 + 4, :], wdv[:, kb * 4:kb * 4 + 4, :], t_wd[kb])
    xs = [sb(pes, [128, D]) for _ in range(2)]
    t_x = [Tok(), Tok()]
    xr = [sb(pes, [128, 512]) for _ in range(2)]
    t_xr = [Tok(), Tok()]
    junk = sb(pes, [128, D], BF16)
    t_j = Tok()
    ss = [sb(pes, [128, 1]) for _ in range(2)]
    t_s = [Tok(), Tok()]
    hn = [sb(pes, [128, D], BF16) for _ in range(2)]
    t_h = [Tok(), Tok()]
    pst = [psum(pes, [128, 8, 128], BF16) for _ in range(2)]
    t_p = [Tok(), Tok()]
    h2T = [sb(pes, [128, 8, 512], BF16) for _ in range(2)]
    t_h2 = [Tok(), Tok()]
    pu = [psum(pes, [128, 512]) for _ in range(2)]
    t_pu = [Tok(), Tok()]
    rl = [sb(pes, [128, 512]) for _ in range(2)]
    t_rl = [Tok(), Tok()]
    aT = sb(pes, [128, 32, 512], BF16)
    t_a = Tok()
    pd = [psum(pes, [128, 512]) for _ in range(2)]
    t_pd = [Tok(), Tok()]
    xo = [sb(pes, [128, 512]) for _ in range(2)]
    t_xo = [Tok(), Tok()]
    def norm_front(T, s_):
        i = (T * 4 + s_) % 2
        fw.dma("sp", xs[i][:], x1_d[T * 512 + s_ * 128:T * 512 + (s_ + 1) * 128, :], writes=[t_x[i]])
        fw.op("act", lambda e: e.activation(out=junk[:], in_=xs[i][:], func=AF.Square, accum_out=ss[i][:, 0:1]),
              reads=[t_x[i]], writes=[t_j, t_s[i]])
        fw.op("act", lambda e: e.activation(out=ss[i][:], in_=ss[i][:], func=AF.Sqrt, scale=1.0 / D, bias=EPS),
              reads=[t_s[i]], writes=[t_s[i]])
        fw.op("dve", lambda e: e.reciprocal(out=ss[i][:], in_=ss[i][:]), reads=[t_s[i]], writes=[t_s[i]])
        fw.op("dve", lambda e: e.tensor_scalar(out=hn[i][:], in0=xs[i][:], scalar1=ss[i][:, 0:1], scalar2=None, op0=ALU.mult),
              reads=[t_x[i], t_s[i]], writes=[t_h[i]])

    def norm_back(T, s_):
        i = (T * 4 + s_) % 2
        hb = T % 2
        for c in range(8):
            fw.op("pe", lambda e: e.transpose(out=pst[i][:, c, :], in_=hn[i][:, c * 128:(c + 1) * 128], identity=ident[:]),
                  reads=[t_h[i], t_const], writes=[t_p[i]])
        fw.op("dve", lambda e: e.tensor_tensor(out=h2T[hb][:, :, s_ * 128:(s_ + 1) * 128], in0=pst[i][:],
                                               in1=gT[:].unsqueeze(2).to_broadcast([128, 8, 128]), op=ALU.mult),
              reads=[t_p[i], t_w], writes=[t_h2[hb]])

    NT = S // 512
    for s_ in range(4):
        norm_front(0, s_)
        norm_back(0, s_)
    for T in range(NT):
        hb = T % 2
        for hc in range(32):
            i = hc % 2
            if T + 1 < NT and hc >= 6 and (hc - 6) % 6 == 0 and (hc - 6) // 6 < 4:
                norm_front(T + 1, (hc - 6) // 6)
            if T + 1 < NT and hc >= 9 and (hc - 9) % 6 == 0 and (hc - 9) // 6 < 4:
                norm_back(T + 1, (hc - 9) // 6)
            for k in range(8):
                fw.op("pe", lambda e: e.matmul(pu[i][:], lhsT=Wu[:, k, hc * 128:(hc + 1) * 128], rhs=h2T[hb][:, k, :], start=(k == 0), stop=(k == 7)),
                      reads=[t_wu[hc // 8], t_h2[hb]], writes=[t_pu[i]])
            fw.op("act", lambda e: e.activation(out=rl[i][:], in_=pu[i][:], func=AF.Relu), reads=[t_pu[i]], writes=[t_rl[i]])
            eng = "dve" if hc % 2 == 0 else "pool"
            fw.op(eng, lambda e: e.tensor_tensor(out=aT[:, hc, :], in0=rl[i][:], in1=rl[i][:], op=ALU.mult),
                  reads=[t_rl[i]], writes=[t_a])
        for s_ in range(4):
            for dh in range(2):
                i = (s_ * 2 + dh) % 2
                fw.dma("sp", xr[i][:], x1_d[T * 512 + s_ * 128:T * 512 + (s_ + 1) * 128, dh * 512:(dh + 1) * 512], writes=[t_xr[i]])
                for k in range(32):
                    fw.op("pe", lambda e: e.matmul(pd[i][:], lhsT=aT[:, k, s_ * 128:(s_ + 1) * 128], rhs=Wd[:, k, dh * 512:(dh + 1) * 512],
                                                   start=(k == 0), stop=(k == 31)), reads=[t_a, t_wd[k // 4]], writes=[t_pd[i]])
                fw.op("dve", lambda e: e.tensor_tensor(out=xo[i][:], in0=pd[i][:], in1=xr[i][:], op=ALU.add),
                      reads=[t_pd[i], t_xr[i]], writes=[t_xo[i]])
                fw.dma("sp", x_dst[T * 512 + s_ * 128:T * 512 + (s_ + 1) * 128, dh * 512:(dh + 1) * 512], xo[i][:], reads=[t_xo[i]])


def host_inputs(inp, b, nlayers=DEPTH, l0=0, x=None):
    f = np.float32
    L = nlayers
    sl = slice(l0, l0 + L)
    P = {k: np.asarray(v)[sl] for k, v in inp.items() if k not in ("x", "mem", "positions")}
    m = {
        "x": np.ascontiguousarray(inp["x"][b] if x is None else x, dtype=f),
        "mem": np.ascontiguousarray(inp["mem"][b], dtype=f),
        "positions": np.ascontiguousarray(np.asarray(inp["positions"][b]).reshape(1, S).astype(np.int32)),
        "ln_mixT": np.ascontiguousarray(P["ln_mix"].reshape(L, 8, 128).transpose(0, 2, 1), dtype=f),
        "w_in": np.ascontiguousarray(P["w_in"], dtype=f),
        "b_mergeT": np.ascontiguousarray(P["b_merge"].reshape(L, 3, 8, 128).transpose(0, 3, 1, 2).reshape(L, 128, 24), dtype=f),
        "gq": np.ascontiguousarray(np.tile(P["nsa_q_norm"], (1, 2)).reshape(L, 128, 1), dtype=f),
        "gk": np.ascontiguousarray(np.tile(P["nsa_k_norm"], (1, 1, 2)).transpose(0, 2, 1), dtype=f),
        "cmp_peT": np.ascontiguousarray(np.tile(P["cmp_pe"].transpose(0, 1, 3, 2), (1, 1, 2, 1)), dtype=f),
        "cmp_w1": np.ascontiguousarray(P["cmp_w1"], dtype=f),
        "cmp_w2": np.ascontiguousarray(P["cmp_w2"], dtype=f),
        "gla_wg": np.ascontiguousarray(np.concatenate([P["gla_w_gate"], P["gla_b_gate"][:, None, :]], axis=1), dtype=f),
        "gla_negb": np.ascontiguousarray(P["gla_b_gate"].reshape(L, 2, 128).transpose(0, 2, 1), dtype=f),
        "gla_norm": np.ascontiguousarray(P["gla_norm"].reshape(L, 1, 128), dtype=f),
        "mem_normT": np.ascontiguousarray(P["mem_norm"].reshape(L, 8, 128).transpose(0, 2, 1), dtype=f),
        "mem_w_kv": np.ascontiguousarray(P["mem_w_kv"], dtype=f),
        "mem_qn": np.ascontiguousarray(P["mem_q_norm"].reshape(L, 128, 1), dtype=f),
        "mem_kn": np.ascontiguousarray(P["mem_k_norm"].reshape(L, 128, 1), dtype=f),
        "w_branch": np.ascontiguousarray(P["w_branch"], dtype=f),
        "w_out": np.ascontiguousarray(P["w_out"], dtype=f),
        "ln_mlpT": np.ascontiguousarray(P["ln_mlp"].reshape(L, 8, 128).transpose(0, 2, 1), dtype=f),
        "w_up": np.ascontiguousarray(P["w_up"], dtype=f),
        "w_down": np.ascontiguousarray(P["w_down"], dtype=f),
    }
    for k, v in _consts().items():
        m["c_" + k] = v
    return m


FUSED = True
_CACHE = {}


def kernel(**inputs):
    nb = 8
    if FUSED:
        if "nc" not in _CACHE:
            _CACHE["nc"] = build(DEPTH)[0]
        nc = _CACHE["nc"]
        in_maps = [host_inputs(inputs, b) for b in range(nb)]
        res = run_bass_kernel_spmd(nc, in_maps, core_ids=list(range(nb)))
        return np.stack([np.asarray(r["y"], dtype=np.float32) for r in res.results], axis=0)
    if "nc1" not in _CACHE:
        _CACHE["nc1"] = build(1)[0]
    nc = _CACHE["nc1"]
    xs = [None] * nb
    for l in range(DEPTH):
        in_maps = [host_inputs(inputs, b, 1, l, xs[b]) for b in range(nb)]
        res = run_bass_kernel_spmd(nc, in_maps, core_ids=list(range(nb)))
        xs = [np.asarray(r["y"], dtype=np.float32) for r in res.results]
    return np.stack(xs, axis=0)
```

```python
import numpy as np
import ml_dtypes
from contextlib import ExitStack
import concourse.bass as bass
import concourse.mybir as mybir
from concourse.bass_utils import run_bass_kernel_spmd

F32 = mybir.dt.float32
BF16 = mybir.dt.bfloat16
I32 = mybir.dt.int32
AF = mybir.ActivationFunctionType
ALU = mybir.AluOpType
AX = mybir.AxisListType

D = 1024
S = 4096
DEPTH = 4
MEM = 256
D_IN = 6440
EPS = 1e-6
NEG = -30000.0
O_NQ, O_KC, O_VC, O_KS, O_VS, O_KW, O_VW, O_NG = 0, 512, 640, 768, 896, 1024, 1152, 1280
O_GQ, O_GK, O_GV, O_GR, O_GLOW, O_MQ, O_MERGE = 1304, 1560, 1816, 2328, 2840, 2856, 3368


class Tok:
    __slots__ = ("w", "r")

    def __init__(self):
        self.w = None
        self.r = {}


class FW:
    EPOCH = 16000
    NDS = 24

    def __init__(self, nc, es):
        self.nc = nc
        self.es = es
        self.engs = {"pe": nc.tensor, "act": nc.scalar, "dve": nc.vector, "pool": nc.gpsimd, "sp": nc.sync}
        self.nsem = 0
        self.sems = {k: [self._newsem()] for k in self.engs}
        self.cnt = {k: 0 for k in self.engs}
        self.waited = {k: {} for k in self.engs}
        self.dq = ("sp", "pool")
        self.dsems = {q: [self._newsem() for _ in range(self.NDS)] for q in self.dq}
        self.dtarget = {q: [0] * self.NDS for q in self.dq}
        self.dnext = {q: 0 for q in self.dq}
        self.uid = 0
        self.ninst = 0

    def _newsem(self):
        self.nsem += 1
        return self.es.enter_context(self.nc.semaphore(f"s{self.nsem}"))

    def name(self, p="t"):
        self.uid += 1
        return f"{p}{self.uid}"

    def _wait(self, k, st):
        if st is None:
            return
        if st[0] == "e":
            _, src, ep, c = st
            key = ("e", src)
            cur = self.waited[k].get(key, (-1, -1))
            if (ep, c) <= cur:
                return
            self.waited[k][key] = (ep, c)
            self.engs[k].wait_ge(self.sems[src][ep], c)
        else:
            _, q, idx, gen, tg = st
            key = ("d", q, idx, gen)
            if tg <= self.waited[k].get(key, 0):
                return
            self.waited[k][key] = tg
            self.engs[k].wait_ge(self.dsem_hist[(q, idx, gen)], tg)
        self.ninst += 1

    def _deps(self, k, reads, writes):
        for t in reads:
            self._wait(k, t.w)
        for t in writes:
            if not (k == "pe" and t.w is not None and t.w[0] == "e" and t.w[1] == "pe"):
                self._wait(k, t.w)
            for st in t.r.values():
                self._wait(k, st)

    def _mark(self, st, reads, writes):
        for t in writes:
            t.w = st
            t.r = {}
        for t in reads:
            t.r[st[1] if st[0] == "e" else ("d", st[1], st[2])] = st

    def op(self, k, fn, reads=(), writes=()):
        self._deps(k, reads, writes)
        ins = fn(self.engs[k])
        if self.cnt[k] >= self.EPOCH:
            self.sems[k].append(self._newsem())
            self.cnt[k] = 0
        self.cnt[k] += 1
        ep = len(self.sems[k]) - 1
        ins.then_inc(self.sems[k][ep], 1)
        self.ninst += 1
        self._mark(("e", k, ep, self.cnt[k]), reads, writes)

    def dma(self, k, out, in_, reads=(), writes=()):
        self._deps(k, reads, writes)
        if not hasattr(self, "dsem_hist"):
            self.dsem_hist = {}
            self.dgen = {q: [0] * self.NDS for q in self.dq}
            for q in self.dq:
                for i in range(self.NDS):
                    self.dsem_hist[(q, i, 0)] = self.dsems[q][i]
        idx = self.dnext[k]
        self.dnext[k] = (idx + 1) % self.NDS
        if self.dtarget[k][idx] >= self.EPOCH:
            self._wait(k, ("d", k, idx, self.dgen[k][idx], self.dtarget[k][idx]))
            self.dgen[k][idx] += 1
            self.dsems[k][idx] = self._newsem()
            self.dsem_hist[(k, idx, self.dgen[k][idx])] = self.dsems[k][idx]
            self.dtarget[k][idx] = 0
        gen = self.dgen[k][idx]
        if self.dtarget[k][idx]:
            self._wait(k, ("d", k, idx, gen, self.dtarget[k][idx]))
        self.dtarget[k][idx] += 16
        self.engs[k].dma_start(out=out, in_=in_).then_inc(self.dsems[k][idx], 16)
        self.ninst += 1
        self._mark(("d", k, idx, gen, self.dtarget[k][idx]), reads, writes)

    def barrier(self):
        for k in self.engs:
            for j in self.engs:
                if j != k and (self.cnt[j] or len(self.sems[j]) > 1):
                    self._wait(k, ("e", j, len(self.sems[j]) - 1, self.cnt[j]))
            if hasattr(self, "dsem_hist"):
                for q in self.dq:
                    for i in range(self.NDS):
                        if self.dtarget[q][i]:
                            self._wait(k, ("d", q, i, self.dgen[q][i], self.dtarget[q][i]))


def _consts():
    bf = ml_dtypes.bfloat16
    c = {}
    c["ident"] = np.eye(128, dtype=np.float32).astype(bf)
    c["identf"] = np.eye(128, dtype=np.float32)
    bo = np.zeros((128, 128), np.float32)
    bo[:64, :64] = 1
    bo[64:, 64:] = 1
    c["bones64"] = bo.astype(bf)
    c["ones128"] = np.ones((128, 128), np.float32).astype(bf)
    rm = np.zeros((128, 128), np.float32)
    for base in (0, 64):
        for i in range(8):
            rm[base + i + 8, base + i] = -1.0
            rm[base + i, base + i + 8] = 1.0
    c["rotm"] = rm.astype(bf)
    half = 8
    inv_freq = np.power(np.float32(500000.0), -np.arange(half, dtype=np.float32) / half).astype(np.float32)
    invf = np.zeros((128, 1), np.float32)
    for p in range(128):
        pp = p % 64
        if pp < 16:
            invf[p, 0] = inv_freq[pp % 8]
    c["invf"] = invf
    k = np.arange(128)[:, None]
    q = np.arange(128)[None, :]
    c["causb4"] = np.tile(np.where(k <= q, 0.0, NEG), (1, 4)).astype(bf)
    c["winb4"] = np.tile(np.where(k > q, 0.0, NEG), (1, 4)).astype(bf)
    dl = np.arange(17)[None, :, None]
    cm = np.where(16 * k[:, :, None] + 31 - q[:, None, :] <= 128 * dl, 0.0, NEG)
    c["cmaskb4"] = np.tile(cm, (1, 1, 4)).astype(bf)
    kk = np.arange(4096)[None, :]
    jb = np.arange(64)[:, None]
    c["eblk"] = np.where(kk // 64 == jb, NEG, 0.0).astype(bf)
    n = np.arange(256)
    cst = n * 16
    bst = np.arange(64) * 64
    ov = ((cst[:, None] < bst[None, :] + 64) & (bst[None, :] < cst[:, None] + 32)).astype(np.float32)
    ov[255] = 0
    c["ovl"] = np.ascontiguousarray(ov.reshape(2, 128, 64).transpose(1, 0, 2)).astype(bf)
    j1 = np.tile(np.arange(64, dtype=np.float32)[None, :], (128, 1))
    j1[:, 0] = 1e9
    c["jidx1"] = j1
    c["curm2"] = (2 * np.arange(32)[None, :] + (np.arange(128)[:, None] // 64) - 2).astype(np.float32)
    sm = np.ones((128, 512), np.float32)
    sm[:, ::128] = 0.0
    c["segmask"] = sm
    tp = np.arange(128)[:, None]
    tc = np.arange(128)[None, :]
    c["trim"] = np.where(tp > tc, -1.0, 0.0).astype(np.float32)
    c["tril"] = (tp <= tc).astype(np.float32)
    return c


def build(nlayers=DEPTH, dbg_out=(), dbg_in=(), phases=None):
    nc = bass.Bass("TRN2", target_bir_lowering=False)
    es = ExitStack()
    fw = FW(nc, es)
    C = _consts()

    def din(name, shape, dt=F32):
        return nc.dram_tensor(name, list(shape), dt, kind="ExternalInput").ap()

    def dscr(name, shape, dt=BF16):
        kind = "ExternalOutput" if name in dbg_out else ("ExternalInput" if name in dbg_in else "Internal")
        return nc.dram_tensor(name, list(shape), dt, kind=kind).ap()

    x_in = din("x", [S, D])
    mem_in = din("mem", [MEM, D])
    pos_in = din("positions", [1, S], I32)
    L = nlayers
    W = dict(
        ln_mixT=din("ln_mixT", [L, 128, 8]), w_in=din("w_in", [L, D, D_IN]), b_mergeT=din("b_mergeT", [L, 128, 24]),
        gq=din("gq", [L, 128, 1]), gk=din("gk", [L, 128, 3]),
        cmp_peT=din("cmp_peT", [L, 2, 128, 32]), cmp_w1=din("cmp_w1", [L, 2, 2048, 256]),
        cmp_w2=din("cmp_w2", [L, 2, 256, 64]),
        gla_wg=din("gla_wg", [L, 17, 256]), gla_negb=din("gla_negb", [L, 128, 2]), gla_norm=din("gla_norm", [L, 1, 128]),
        mem_normT=din("mem_normT", [L, 128, 8]), mem_w_kv=din("mem_w_kv", [L, D, 1024]),
        mem_qn=din("mem_qn", [L, 128, 1]), mem_kn=din("mem_kn", [L, 128, 1]),
        w_branch=din("w_branch", [L, 3, 512, D]), w_out=din("w_out", [L, D, D]),
        ln_mlpT=din("ln_mlpT", [L, 128, 8]), w_up=din("w_up", [L, D, 4 * D]), w_down=din("w_down", [L, 4 * D, D]),
    )
    CT = {k: din("c_" + k, v.shape, BF16 if v.dtype == ml_dtypes.bfloat16 else F32) for k, v in C.items()}
    y_out = nc.dram_tensor("y", [S, D], F32, kind="ExternalOutput").ap()

    hT_d = dscr("hT_d", [8, 128, S])
    QT_d = dscr("QT_d", [128, 4, S])
    KsT_d = dscr("KsT_d", [128, S])
    KwT_d = dscr("KwT_d", [128, S])
    mqT_d = dscr("mqT_d", [128, 4, S])
    KcT_d = dscr("KcT_d", [128, 256])
    qeT_d = dscr("qeT_d", [2, 128, S])
    keT_d = dscr("keT_d", [2, 128, S])
    kl_d = dscr("kl_d", [S, 256])
    gv_d = dscr("gv_d", [S, 512])
    GR_d = dscr("GR_d", [S, 512])
    edec_d = dscr("edec_d", [128, 2, 32], F32)
    Vc_d = dscr("Vc_d", [256, 128])
    Vsw_d = dscr("Vsw_d", [S, 256])
    ng_d = dscr("ng_d", [S, 24], F32)
    onsaT_d = dscr("onsaT_d", [4, 128, S])
    oglaT_d = dscr("oglaT_d", [4, 128, S])
    omemT_d = dscr("omemT_d", [4, 128, S])
    x1_d = dscr("x1_d", [S, D], F32)
    xa_d = dscr("xa_d", [S, D], F32)
    cos_d = dscr("cos_d", [128, S], F32)
    sin_d = dscr("sin_d", [128, S], F32)

    def sb(pes, shape, dt=F32, name="t"):
        return pes.enter_context(nc.sbuf_tensor(fw.name(name), list(shape), dt))

    def psum(pes, shape, dt=F32, name="ps"):
        return pes.enter_context(nc.psum_tensor(fw.name(name), list(shape), dt))

    ident = sb(es, [128, 128], BF16)
    bones64 = sb(es, [128, 128], BF16)
    ones128 = sb(es, [128, 128], BF16)
    rotm = sb(es, [128, 128], BF16)
    t_const = Tok()
    for tl, nm in ((ident, "ident"), (bones64, "bones64"), (ones128, "ones128"), (rotm, "rotm")):
        fw.dma("sp", tl[:], CT[nm], writes=[t_const])

    def run(ph):
        return phases is None or ph in phases

    if run("P0"):
        with ExitStack() as pes:
            posi = sb(pes, [128, S], I32)
            ang = sb(pes, [128, S])
            tmp = sb(pes, [128, S])
            kk = sb(pes, [128, S])
            invf = sb(pes, [128, 1])
            tk = Tok()
            fw.dma("sp", posi[:], pos_in.partition_broadcast(128), writes=[tk])
            fw.dma("sp", invf[:], CT["invf"], writes=[tk])
            fw.op("dve", lambda e: e.tensor_copy(out=tmp[:], in_=posi[:]), reads=[tk], writes=[tk])
            fw.op("dve", lambda e: e.tensor_scalar(out=ang[:], in0=tmp[:], scalar1=invf[:, 0:1], scalar2=None, op0=ALU.mult),
                  reads=[tk], writes=[tk])
            MAGIC = 12582912.0
            C1 = 6.28125
            C2 = float(np.float32(2 * np.pi - 6.28125))
            for shift, dst in ((0.0, sin_d), (float(np.pi / 2), cos_d)):
                fw.op("dve", lambda e: e.tensor_scalar(out=tmp[:], in0=ang[:], scalar1=shift, scalar2=float(1 / (2 * np.pi)),
                                                       op0=ALU.add, op1=ALU.mult), reads=[tk], writes=[tk])
                fw.op("dve", lambda e: e.tensor_scalar(out=kk[:], in0=tmp[:], scalar1=MAGIC, scalar2=None, op0=ALU.add),
                      reads=[tk], writes=[tk])
                fw.op("dve", lambda e: e.tensor_scalar(out=kk[:], in0=kk[:], scalar1=-MAGIC, scalar2=None, op0=ALU.add),
                      reads=[tk], writes=[tk])
                fw.op("dve", lambda e: e.scalar_tensor_tensor(out=tmp[:], in0=kk[:], scalar=-C1, in1=ang[:], op0=ALU.mult, op1=ALU.add),
                      reads=[tk], writes=[tk])
                fw.op("dve", lambda e: e.tensor_scalar(out=tmp[:], in0=tmp[:], scalar1=shift, scalar2=None, op0=ALU.add),
                      reads=[tk], writes=[tk])
                fw.op("dve", lambda e: e.scalar_tensor_tensor(out=tmp[:], in0=kk[:], scalar=-C2, in1=tmp[:], op0=ALU.mult, op1=ALU.add),
                      reads=[tk], writes=[tk])
                fw.op("dve", lambda e: e.tensor_scalar(out=tmp[:], in0=tmp[:], scalar1=3.14159, scalar2=-3.14159, op0=ALU.min, op1=ALU.max),
                      reads=[tk], writes=[tk])
                fw.op("act", lambda e: e.activation(out=kk[:], in_=tmp[:], func=AF.Sin), reads=[tk], writes=[tk])
                fw.dma("sp", dst, kk[:], reads=[tk])
            fw.barrier()

    def norm_to_hT(pes, src, ntiles, gT, t_g, hT, hT_dram=None):
        xin = [sb(pes, [128, D]) for _ in range(2)]
        junk = sb(pes, [128, D], BF16)
        hn = [sb(pes, [128, D], BF16) for _ in range(2)]
        ss = [sb(pes, [128, 1]) for _ in range(2)]
        rs = [sb(pes, [128, 1]) for _ in range(2)]
        pst = [psum(pes, [128, 8, 128], BF16) for _ in range(2)]
        t_x = [Tok(), Tok()]
        t_j = Tok()
        t_s = [Tok(), Tok()]
        t_h = [Tok(), Tok()]
        t_p = [Tok(), Tok()]
        t_hT = Tok()
        def front(tt):
            b = tt % 2
            fw.dma("sp", xin[b][:], src[tt * 128:(tt + 1) * 128, :], writes=[t_x[b]])
            fw.op("act", lambda e: e.activation(out=junk[:], in_=xin[b][:], func=AF.Square, accum_out=ss[b][:, 0:1]),
                  reads=[t_x[b]], writes=[t_j, t_s[b]])
            fw.op("act", lambda e: e.activation(out=rs[b][:], in_=ss[b][:], func=AF.Sqrt, scale=1.0 / D, bias=EPS),
                  reads=[t_s[b]], writes=[t_s[b]])
            fw.op("dve", lambda e: e.reciprocal(out=rs[b][:], in_=rs[b][:]), reads=[t_s[b]], writes=[t_s[b]])
            fw.op("dve",
                  lambda e: e.tensor_scalar(out=hn[b][:], in0=xin[b][:], scalar1=rs[b][:, 0:1], scalar2=None, op0=ALU.mult),
                  reads=[t_x[b], t_s[b]], writes=[t_h[b]])

        def back(tt):
            b = tt % 2
            for c in range(8):
                fw.op("pe", lambda e: e.transpose(out=pst[b][:, c, :], in_=hn[b][:, c * 128:(c + 1) * 128], identity=ident[:]),
                      reads=[t_h[b], t_const], writes=[t_p[b]])
            fw.op("dve", lambda e: e.tensor_tensor(out=hT[:, :, tt * 128:(tt + 1) * 128], in0=pst[b][:],
                                                   in1=gT[:].unsqueeze(2).to_broadcast([128, 8, 128]), op=ALU.mult),
                  reads=[t_p[b], t_g], writes=[t_hT])

        front(0)
        for tt in range(ntiles):
            if tt + 1 < ntiles:
                front(tt + 1)
            back(tt)
        if hT_dram is not None:
            for c in range(8):
                fw.dma("sp", hT_dram[c], hT[:, c, :], reads=[t_hT])
        return t_hT

    for l in range(nlayers):
        x_src = x_in if l == 0 else xa_d
        x_dst = y_out if l == nlayers - 1 else xa_d
        if run("P2"):
            with ExitStack() as pes:
                gT = sb(pes, [128, 8])
                t_g = Tok()
                fw.dma("sp", gT[:], W["ln_mixT"][l], writes=[t_g])
                hT = sb(pes, [128, 8, S], BF16)
                def norm_cb(x_src=x_src, gT=gT, t_g=t_g, hT=hT):
                    with ExitStack() as pes1:
                        t = norm_to_hT(pes1, x_src, S // 128, gT, t_g, hT, hT_d)
                        fw.barrier()
                    return t

                p2_projections(nc, fw, pes, l, W, hT, norm_cb, dict(
                    QT_d=QT_d, KsT_d=KsT_d, KwT_d=KwT_d, mqT_d=mqT_d, Vsw_d=Vsw_d, ng_d=ng_d, cos_d=cos_d, sin_d=sin_d, KcT_d=KcT_d, Vc_d=Vc_d, qeT_d=qeT_d, keT_d=keT_d, kl_d=kl_d, gv_d=gv_d, GR_d=GR_d, edec_d=edec_d, CT=CT),
                    dict(ident=ident, bones64=bones64, ones128=ones128, rotm=rotm, t_const=t_const), sb, psum)
                fw.barrier()
        if run("P3"):
            with ExitStack() as pes:
                p3_nsa(nc, fw, pes, l, dict(QT_d=QT_d, KsT_d=KsT_d, KwT_d=KwT_d, KcT_d=KcT_d, Vc_d=Vc_d, Vsw_d=Vsw_d, ng_d=ng_d,
                                            onsaT_d=onsaT_d), CT, dict(ident=ident, t_const=t_const), sb, psum)
                fw.barrier()
        if run("P4"):
            with ExitStack() as pes:
                p4_gla(nc, fw, pes, l, dict(qeT_d=qeT_d, keT_d=keT_d, kl_d=kl_d, gv_d=gv_d, GR_d=GR_d, edec_d=edec_d, oglaT_d=oglaT_d),
                       CT, dict(ident=ident, t_const=t_const), sb, psum)
                fw.barrier()
        with ExitStack() as pes56:
            w6, w6_issue = (p6_load(nc, fw, pes56, l, W, sb) if run("P6") else (None, None))
            if run("P5"):
                with ExitStack() as pes:
                    p5_mem(nc, fw, pes, l, W, mem_in, mqT_d, omemT_d,
                           dict(ident=ident, ones128=ones128, t_const=t_const), sb, psum, norm_to_hT, after_w=w6_issue)
                    fw.barrier()
            elif w6_issue is not None:
                w6_issue()
            if run("P6"):
                with ExitStack() as pes:
                    p6_merge(nc, fw, pes, l, W, hT_d, onsaT_d, oglaT_d, omemT_d, x_src, x1_d, sb, psum, w6)
                    fw.barrier()
        if run("P7"):
            with ExitStack() as pes:
                p7_mlp(nc, fw, pes, l, W, x1_d, x_dst, dict(ident=ident, t_const=t_const), sb, psum)
                fw.barrier()
    fw.barrier()
    es.close()
    return nc, fw


class Skew:
    def __init__(self):
        self.pipe = []

    def push(self, stages):
        self.pipe.insert(0, list(stages))
        for d, job in enumerate(list(self.pipe)):
            if d < len(job):
                job[d]()
        while self.pipe and len(self.pipe[-1]) <= len(self.pipe) - 1:
            self.pipe.pop()

    def drain(self):
        while self.pipe:
            self.push([])


def wload(fw, wt, src, tok):
    fw.dma("pool", wt, src, writes=[tok])


def p2_projections(nc, fw, pes, l, W, hT, norm_cb, DR, CS, sb, psum):
    w_in = W["w_in"][l].rearrange("(c p) n -> p c n", p=128)
    ident, bones64, ones128, rotm, t_const = CS["ident"], CS["bones64"], CS["ones128"], CS["rotm"], CS["t_const"]
    cosT = sb(pes, [128, S])
    sinT = sb(pes, [128, S])
    t_cs = Tok()
    fw.dma("sp", cosT[:], DR["cos_d"], writes=[t_cs])
    fw.dma("sp", sinT[:], DR["sin_d"], writes=[t_cs])
    gq = sb(pes, [128, 1])
    gk = sb(pes, [128, 3])
    mqn = sb(pes, [128, 1])
    t_gn = Tok()
    fw.dma("sp", gq[:], W["gq"][l], writes=[t_gn])
    fw.dma("sp", gk[:], W["gk"][l], writes=[t_gn])
    fw.dma("sp", mqn[:], W["mem_qn"][l], writes=[t_gn])
    fw.op("dve", lambda e: e.tensor_scalar(out=gq[:], in0=gq[:], scalar1=0.125, scalar2=None, op0=ALU.mult), reads=[t_gn], writes=[t_gn])
    fw.op("dve", lambda e: e.tensor_scalar(out=mqn[:], in0=mqn[:], scalar1=float(128 ** -0.5), scalar2=None, op0=ALU.mult),
          reads=[t_gn], writes=[t_gn])

    wt = [sb(pes, [128, 8, 128], BF16) for _ in range(3)]
    t_w = [Tok(), Tok(), Tok()]
    pm = [psum(pes, [128, 512]) for _ in range(2)]
    t_pm = [Tok(), Tok()]
    pa = [psum(pes, [128, 512]) for _ in range(2)]
    t_pa = [Tok(), Tok()]
    sq = [sb(pes, [128, 512], BF16) for _ in range(2)]
    t_sq = [Tok(), Tok()]
    rstd = [sb(pes, [128, 512]) for _ in range(2)]
    t_rs = [Tok(), Tok()]
    qn = [sb(pes, [128, 512], BF16) for _ in range(2)]
    t_qn = [Tok(), Tok()]
    t1 = [sb(pes, [128, 512]) for _ in range(2)]
    t_t1 = [Tok(), Tok()]
    t2 = [sb(pes, [128, 512]) for _ in range(2)]
    t_t2 = [Tok(), Tok()]
    ob = [sb(pes, [128, 512], BF16) for _ in range(3)]
    t_ob = [Tok(), Tok(), Tok()]
    t_ob2 = [Tok(), Tok(), Tok()]
    state = dict(w=0, i=0, o=0, ip=0)

    pipe = []

    def push(stages):
        pipe.insert(0, list(stages))
        for d, job in enumerate(list(pipe)):
            if d < len(job):
                job[d]()
        while pipe and len(pipe[-1]) <= len(pipe) - 1:
            pipe.pop()

    def drain():
        while pipe:
            push([])

    loaded = set()

    def load_w(ci, wsrc):
        if ci in loaded:
            return
        loaded.add(ci)
        wb = ci % 3
        if isinstance(wsrc, (list, tuple)):
            for gi, ws_ in enumerate(wsrc):
                wload(fw, wt[wb][:, :, gi * 64:(gi + 1) * 64], ws_, t_w[wb])
        else:
            wload(fw, wt[wb][:], wsrc, t_w[wb])

    def fm_chunk(wsrc, gain, blk, rope, dst_fn, preload_only=False, nxt=None):
        wb = state["w"] % 3
        state["w"] += 1
        load_w(state["w"] - 1, wsrc)
        if preload_only:
            state["w"] -= 1
            return
        if nxt is not None:
            load_w(state["w"], nxt)
        for _ in range(3):
            if state.get("defer"):
                state["defer"].pop(0)()
        for T in range(S // 512):
            i = state["i"] % 2
            state["i"] += 1
            ip = state["ip"] % 3
            state["ip"] += 1
            o = state["o"] % 3
            state["o"] += 1
            tsl = slice(T * 512, (T + 1) * 512)
            t_hT = state["t_hT"]

            def st0(i=i, ip=ip, tsl=tsl, wb=wb):
                for k in range(8):
                    fw.op("pe", lambda e: e.matmul(pm_fm[ip][:], lhsT=wt[wb][:, k, :], rhs=hT[:, k, tsl], start=(k == 0), stop=(k == 7)),
                          reads=[t_w[wb], t_hT], writes=[t_pm_fm[ip]])
                if gain is None:
                    sdst, t_sd = dst_fn(tsl)
                    fw.op("act", lambda e: e.activation(out=sdst, in_=pm_fm[ip][:], func=AF.Copy), reads=[t_pm_fm[ip]], writes=[t_sd])
                else:
                    fw.op("act", lambda e: e.activation(out=sq[i][:], in_=pm_fm[ip][:], func=AF.Square), reads=[t_pm_fm[ip]], writes=[t_sq[i]])

            def st1(i=i, ip=ip, o=o, tsl=tsl):
                om = bones64 if blk == 64 else ones128
                fw.op("pe", lambda e: e.matmul(pa[i][:], lhsT=om[:], rhs=sq[i][:], start=True, stop=True),
                      reads=[t_sq[i], t_const], writes=[t_pa[i]])
                fw.op("act", lambda e: e.activation(out=rstd[i][:], in_=pa[i][:], func=AF.Ln, scale=1.0 / blk, bias=EPS),
                      reads=[t_pa[i]], writes=[t_rs[i]])
                fw.op("act", lambda e: e.activation(out=rstd[i][:], in_=rstd[i][:], func=AF.Exp, scale=-0.5), reads=[t_rs[i]], writes=[t_rs[i]])
                dst = qn[i] if rope else ob[o]
                t_dst = t_qn[i] if rope else t_ob[o]
                fw.op("dve", lambda e: e.scalar_tensor_tensor(out=dst[:], in0=pm_fm[ip][:], scalar=gain, in1=rstd[i][:],
                                                              op0=ALU.mult, op1=ALU.mult),
                      reads=[t_pm_fm[ip], t_rs[i], t_gn], writes=[t_dst])
                if not rope:
                    fw.dma("sp", dst_fn(tsl), ob[o][:], reads=[t_ob[o]])

            def st2(i=i, o=o, tsl=tsl):
                fw.op("pe", lambda e: e.matmul(pr[i][:], lhsT=rotm[:], rhs=qn[i][:], start=True, stop=True),
                      reads=[t_qn[i], t_const], writes=[t_pr[i]])
                fw.op("dve", lambda e: e.tensor_tensor(out=t1[i][:], in0=pr[i][:], in1=sinT[:, tsl], op=ALU.mult),
                      reads=[t_pr[i], t_cs], writes=[t_t1[i]])
                fw.op("pool", lambda e: e.tensor_tensor(out=t2[i][:], in0=qn[i][:], in1=cosT[:, tsl], op=ALU.mult),
                      reads=[t_qn[i], t_cs], writes=[t_t2[i]])
                fw.op("dve", lambda e: e.tensor_tensor(out=ob[o][:, 0:256], in0=t1[i][:, 0:256], in1=t2[i][:, 0:256], op=ALU.add),
                      reads=[t_t1[i], t_t2[i]], writes=[t_ob[o]])
                fw.op("pool", lambda e: e.tensor_tensor(out=ob[o][:, 256:512], in0=t1[i][:, 256:512], in1=t2[i][:, 256:512], op=ALU.add),
                      reads=[t_t1[i], t_t2[i]], writes=[t_ob2[o]])
                fw.dma("sp", dst_fn(tsl), ob[o][:], reads=[t_ob[o], t_ob2[o]])

            if gain is None:
                push([st0])
            elif rope:
                push([st0, st1, st2])
            else:
                push([st0, st1])

    QT_d, KsT_d, KwT_d, mqT_d = DR["QT_d"], DR["KsT_d"], DR["KwT_d"], DR["mqT_d"]
    import os
    sel = os.environ.get("P2SEL", "q,ks,kw,mq,cmp,tm,gla").split(",")
    wq = w_in[:, :, O_NQ:O_NQ + 512].rearrange("p c (g r d) -> p c r g d", g=2, r=4, d=64)
    chunks = []
    if "q" in sel:
        for r in range(4):
            chunks.append(([wq[:, :, r, 0, :], wq[:, :, r, 1, :]], gq[:, 0:1], 64, True, lambda tsl, r=r: QT_d[:, r, tsl]))
    if "ks" in sel:
        chunks.append((w_in[:, :, O_KS:O_KS + 128], gk[:, 1:2], 64, True, lambda tsl: KsT_d[:, tsl]))
    if "kw" in sel:
        chunks.append((w_in[:, :, O_KW:O_KW + 128], gk[:, 2:3], 64, True, lambda tsl: KwT_d[:, tsl]))
    if "mq" in sel:
        for h in range(4):
            chunks.append((w_in[:, :, O_MQ + h * 128:O_MQ + (h + 1) * 128], mqn[:, 0:1], 128, False, lambda tsl, h=h: mqT_d[:, h, tsl]))
    ncore = len(chunks)
    if chunks:
        fm_chunk(chunks[0][0], None, 0, False, None, preload_only=True)
    state["t_hT"] = t_hT = norm_cb()
    pmx = ExitStack()
    pm_fm = pm + [psum(pmx, [128, 512])]
    t_pm_fm = t_pm + [Tok()]
    pr = [psum(pmx, [128, 512]) for _ in range(2)]
    t_pr = [Tok(), Tok()]
    cs0 = ExitStack()
    if "cmp" in sel:
        kcvc = sb(cs0, [128, 2, S], BF16)
        t_kcvc = Tok()
        cmp_w, cmp_defer = p2_compress_load(nc, fw, cs0, l, W, sb)
        state["defer"] = cmp_defer
        chunks.append((w_in[:, :, O_KC:O_KC + 128], None, 0, False, lambda tsl: (kcvc[:, 0, tsl], t_kcvc)))
        chunks.append((w_in[:, :, O_VC:O_VC + 128], None, 0, False, lambda tsl: (kcvc[:, 1, tsl], t_kcvc)))
    for ci in range(ncore):
        nxt = chunks[ci + 1][0] if ci + 1 < len(chunks) else None
        fm_chunk(*chunks[ci], nxt=nxt)
    if "cmp" in sel:
        while state.get("defer"):
            state["defer"].pop(0)()
        fm_chunk(*chunks[ncore], nxt=chunks[ncore + 1][0])
        fm_chunk(*chunks[ncore + 1])
        drain()
        fw.barrier()
        pmx.close()
        with ExitStack() as cs_:
            p2_compress(nc, fw, cs_, l, W, kcvc, t_kcvc, cosT, sinT, t_cs, gk, t_gn, DR, CS, sb, psum, cmp_w)
            fw.barrier()
    drain()
    fw.barrier()
    pmx.close()
    cs0.close()
    drain()
    if "gla" in sel:
        with ExitStack() as cs_:
            p2_gla(nc, fw, cs_, l, W, w_in, hT, t_hT, pm, t_pm, pa, t_pa, wt, t_w, DR, sb, psum)
            fw.barrier()
    if "tm" in sel:
        with ExitStack() as cs_:
            wtm = sb(cs_, [128, 8, 280], BF16)
            t_wtm = Tok()
            wload(fw, wtm[:, :, 0:128], w_in[:, :, O_VS:O_VS + 128], t_wtm)
            wload(fw, wtm[:, :, 128:256], w_in[:, :, O_VW:O_VW + 128], t_wtm)
            wload(fw, wtm[:, :, 256:280], w_in[:, :, O_NG:O_NG + 24], t_wtm)
            vst = sb(cs_, [128, 32, 256], BF16)
            ngst = sb(cs_, [128, 32, 24])
            t_vst = Tok()
            for tt in range(S // 128):
                i = tt % 2
                for k in range(8):
                    fw.op("pe", lambda e: e.matmul(pm[i][:, 0:280], lhsT=hT[:, k, tt * 128:(tt + 1) * 128], rhs=wtm[:, k, :],
                                                   start=(k == 0), stop=(k == 7)), reads=[t_wtm, t_hT], writes=[t_pm[i]])
                fw.op("act", lambda e: e.activation(out=vst[:, tt, :], in_=pm[i][:, 0:256], func=AF.Copy), reads=[t_pm[i]], writes=[t_vst])
                fw.op("act", lambda e: e.activation(out=ngst[:, tt, :], in_=pm[i][:, 256:280], func=AF.Sigmoid), reads=[t_pm[i]], writes=[t_vst])
            fw.dma("sp", DR["Vsw_d"].rearrange("(t p) c -> p t c", p=128), vst[:], reads=[t_vst])
            fw.dma("sp", DR["ng_d"].rearrange("(t p) c -> p t c", p=128), ngst[:], reads=[t_vst])
            fw.barrier()


def p2_compress_load(nc, fw, pes, l, W, sb):
    w1r = [sb(pes, [128, 32, 256], BF16) for _ in range(2)]
    t_w1 = Tok()
    defer = []
    for kv in range(2):
        src = W["cmp_w1"][l][kv].rearrange("(l d) e -> d l e", d=64)
        for g in range(2):
            for lh in range(0, 32, 8):
                defer.append(lambda kv=kv, g=g, lh=lh, src=src: wload(fw, w1r[kv][g * 64:(g + 1) * 64, lh:lh + 8, :], src[:, lh:lh + 8, :], t_w1))
    peb = sb(pes, [128, 2, 32], BF16)
    for kv in range(2):
        defer.append(lambda kv=kv: wload(fw, peb[:, kv, :], W["cmp_peT"][l][kv], t_w1))
    w2p = sb(pes, [128, 8, 128], BF16)
    t_w2 = Tok()
    fw.op("pool", lambda e: e.memset(w2p[:], 0.0), writes=[t_w2])
    for kv in range(2):
        for g in range(2):
            i0 = kv * 4 + g * 2
            defer.append(lambda kv=kv, g=g, i0=i0: wload(fw, w2p[:, i0:i0 + 2, g * 64:(g + 1) * 64],
                                                          W["cmp_w2"][l][kv].rearrange("(ec e) d -> e ec d", e=128), t_w2))
    return (w1r, t_w1, peb, w2p, t_w2), defer


def p2_compress(nc, fw, pes, l, W, kcvc, t_kcvc, cosT, sinT, t_cs, gk, t_gn, DR, CS, sb, psum, wts):
    bones64, rotm, t_const = CS["bones64"], CS["rotm"], CS["t_const"]
    NC_ = 255
    w1r, t_w1, peb, w2p, t_w2 = wts
    bias = sb(pes, [128, 4])
    t_b = Tok()
    hidT = sb(pes, [128, 8, 256], BF16)
    t_hid = Tok()
    fw.op("pool", lambda e: e.memset(hidT[:], 0.0), writes=[t_hid])
    ph = [psum(pes, [128, 256]) for _ in range(2)]
    t_ph = [Tok(), Tok()]
    pb = psum(pes, [128, 256])
    t_pb = Tok()
    pc = psum(pes, [128, 256])
    t_pc = Tok()
    u = [sb(pes, [128, NC_]) for _ in range(2)]
    t_u = [Tok(), Tok()]
    v = [sb(pes, [128, NC_]) for _ in range(2)]
    t_v = [Tok(), Tok()]
    for kv in range(2):
        for ec in range(2):
            for l_ in range(32):
                fw.op("pe", lambda e: e.matmul(pb[:, kv * 2 + ec:kv * 2 + ec + 1], lhsT=w1r[kv][0:64, l_, ec * 128:(ec + 1) * 128],
                                               rhs=peb[0:64, kv, l_:l_ + 1], start=(l_ == 0), stop=(l_ == 31)),
                      reads=[t_w1], writes=[t_pb])
    fw.op("dve", lambda e: e.tensor_copy(out=bias[:], in_=pb[:, 0:4]), reads=[t_pb], writes=[t_b])
    cnt = 0
    for kv in range(2):
        for g in range(2):
            for ec in range(2):
                i = cnt % 2
                cnt += 1
                for l_ in range(32):
                    fw.op("pe", lambda e: e.matmul(ph[i][:, 0:NC_], lhsT=w1r[kv][g * 64:(g + 1) * 64, l_, ec * 128:(ec + 1) * 128],
                                                   rhs=kcvc[g * 64:(g + 1) * 64, kv, l_:l_ + 16 * (NC_ - 1) + 1:16],
                                                   start=(l_ == 0), stop=(l_ == 31)), reads=[t_w1, t_kcvc], writes=[t_ph[i]])
                hi = kv * 4 + g * 2 + ec
                fw.op("act", lambda e: e.activation(out=u[i][:], in_=ph[i][:, 0:NC_], func=AF.Identity, bias=bias[:, kv * 2 + ec:kv * 2 + ec + 1]),
                      reads=[t_ph[i], t_b], writes=[t_u[i]])
                fw.op("dve", lambda e: e.tensor_tensor(out=v[i][:], in0=u[i][:], in1=u[i][:], op=ALU.mult), reads=[t_u[i]], writes=[t_v[i]])
                fw.op("dve", lambda e: e.tensor_scalar(out=v[i][:], in0=v[i][:], scalar1=0.044715, scalar2=1.0, op0=ALU.mult, op1=ALU.add),
                      reads=[t_v[i]], writes=[t_v[i]])
                fw.op("dve", lambda e: e.tensor_tensor(out=v[i][:], in0=v[i][:], in1=u[i][:], op=ALU.mult), reads=[t_v[i], t_u[i]], writes=[t_v[i]])
                fw.op("act", lambda e: e.activation(out=v[i][:], in_=v[i][:], func=AF.Sigmoid, scale=float(2 * np.sqrt(2 / np.pi))),
                      reads=[t_v[i]], writes=[t_v[i]])
                fw.op("dve", lambda e: e.tensor_tensor(out=hidT[:, hi, 0:NC_], in0=v[i][:], in1=u[i][:], op=ALU.mult),
                      reads=[t_v[i], t_u[i]], writes=[t_hid])
    n_acc = 0
    for g in range(2):
        for ec in range(2):
            fw.op("pe", lambda e: e.matmul(pc[:], lhsT=w2p[:, g * 2 + ec, :], rhs=hidT[:, g * 2 + ec, :], start=(n_acc == 0), stop=(n_acc == 3)),
                  reads=[t_w2, t_hid], writes=[t_pc])
            n_acc += 1
    sq = sb(pes, [128, 256], BF16)
    rstd = sb(pes, [128, 256])
    qn = sb(pes, [128, 256], BF16)
    t1 = sb(pes, [128, NC_])
    t2 = sb(pes, [128, NC_])
    kc_o = sb(pes, [128, 256], BF16)
    t_k = Tok()
    fw.op("pool", lambda e: e.memset(kc_o[:], 0.0), writes=[t_k])
    fw.op("act", lambda e: e.activation(out=sq[:], in_=pc[:], func=AF.Square), reads=[t_pc], writes=[t_k])
    fw.op("pe", lambda e: e.matmul(ph[0][:], lhsT=bones64[:], rhs=sq[:], start=True, stop=True), reads=[t_k, t_const], writes=[t_ph[0]])
    fw.op("act", lambda e: e.activation(out=rstd[:], in_=ph[0][:], func=AF.Sqrt, scale=1.0 / 64, bias=EPS), reads=[t_ph[0]], writes=[t_k])
    fw.op("dve", lambda e: e.reciprocal(out=rstd[:], in_=rstd[:]), reads=[t_k], writes=[t_k])
    fw.op("dve", lambda e: e.scalar_tensor_tensor(out=qn[:], in0=pc[:], scalar=gk[:, 0:1], in1=rstd[:], op0=ALU.mult, op1=ALU.mult),
          reads=[t_pc, t_k, t_gn], writes=[t_k])
    fw.op("pe", lambda e: e.matmul(ph[1][:], lhsT=rotm[:], rhs=qn[:], start=True, stop=True), reads=[t_k, t_const], writes=[t_ph[1]])
    cend = slice(31, 31 + 16 * (NC_ - 1) + 1, 16)
    fw.op("dve", lambda e: e.tensor_tensor(out=t1[:], in0=ph[1][:, 0:NC_], in1=sinT[:, cend], op=ALU.mult), reads=[t_ph[1], t_cs], writes=[t_k])
    fw.op("dve", lambda e: e.tensor_tensor(out=t2[:], in0=qn[:, 0:NC_], in1=cosT[:, cend], op=ALU.mult), reads=[t_k, t_cs], writes=[t_k])
    fw.op("dve", lambda e: e.tensor_tensor(out=kc_o[:, 0:NC_], in0=t1[:], in1=t2[:], op=ALU.add), reads=[t_k], writes=[t_k])
    fw.dma("sp", DR["KcT_d"], kc_o[:], reads=[t_k])
    vc_o = sb(pes, [128, 2, 128], BF16)
    t_vo = Tok()
    fw.op("pool", lambda e: e.memset(vc_o[:], 0.0), writes=[t_vo])
    for nt in range(2):
        nn = 128 if nt == 0 else NC_ - 128
        n_acc = 0
        for g in range(2):
            for ec in range(2):
                fw.op("pe", lambda e: e.matmul(ph[nt][0:nn, 0:128], lhsT=hidT[:, 4 + g * 2 + ec, nt * 128:nt * 128 + nn],
                                               rhs=w2p[:, 4 + g * 2 + ec, :], start=(n_acc == 0), stop=(n_acc == 3)),
                      reads=[t_w2, t_hid], writes=[t_ph[nt]])
                n_acc += 1
        fw.op("act", lambda e: e.activation(out=vc_o[0:nn, nt, :], in_=ph[nt][0:nn, 0:128], func=AF.Copy), reads=[t_ph[nt]], writes=[t_vo])
    fw.dma("sp", DR["Vc_d"].rearrange("(t p) c -> p t c", p=128), vc_o[:], reads=[t_vo])


def p3_nsa(nc, fw, pes, l, DR, CT, CS, sb, psum):
    ident, t_const = CS["ident"], CS["t_const"]
    mm = lambda e, *a, **k: e.matmul(*a, skip_group_check=True, **k)
    QT = sb(pes, [128, 4, S], BF16)
    t_QT = Tok()
    for r in range(4):
        fw.dma("sp", QT[:, r, :], DR["QT_d"][:, r, :], writes=[t_QT])
    Kpad = sb(pes, [128, 4, S], BF16)
    t_K = Tok()
    for i in (2, 3):
        fw.op("dve", lambda e: e.memset(Kpad[:, i, :], 0.0), writes=[t_K])
    for ws_, src in ((0, DR["KsT_d"]), (1, DR["KwT_d"])):
        for g in range(2):
            fw.dma("sp", Kpad[g * 64:(g + 1) * 64, ws_ * 2 + g, :], src[g * 64:(g + 1) * 64, :], writes=[t_K])
    for g in range(2):
        fw.dma("sp", Kpad[(1 - g) * 64:(2 - g) * 64, g, :], CT["eblk"], writes=[t_K])
    Kc = sb(pes, [128, 2, 256], BF16)
    t_Kc = Tok()
    fw.op("dve", lambda e: e.memset(Kc[:], 0.0), writes=[t_Kc])
    for g in range(2):
        fw.dma("sp", Kc[g * 64:(g + 1) * 64, g, :], DR["KcT_d"][g * 64:(g + 1) * 64, :], writes=[t_Kc])
    Vs = sb(pes, [128, 32, 2, 65], BF16)
    Vw = sb(pes, [128, 32, 2, 65], BF16)
    Vc = sb(pes, [128, 2, 2, 65], BF16)
    t_V = Tok()
    for vi, v_ in enumerate((Vs, Vw, Vc)):
        fw.op("pool" if vi == 1 else "dve", lambda e: e.memset(v_[:], 1.0), writes=[t_V])
    vsw_v = DR["Vsw_d"].rearrange("(t p) c -> p t c", p=128)
    vc_v = DR["Vc_d"].rearrange("(t p) c -> p t c", p=128)
    for g in range(2):
        fw.dma("sp", Vs[:, :, g, 0:64], vsw_v[:, :, g * 64:(g + 1) * 64], writes=[t_V])
        fw.dma("sp", Vw[:, :, g, 0:64], vsw_v[:, :, 128 + g * 64:128 + (g + 1) * 64], writes=[t_V])
        fw.dma("sp", Vc[:, :, g, 0:64], vc_v[:, :, g * 64:(g + 1) * 64], writes=[t_V])
    t_tab = Tok()
    ovl = sb(pes, [128, 2, 64], BF16)
    ng = sb(pes, [128, 32, 24])
    cmaskb4 = sb(pes, [128, 17, 512], BF16)
    causb4 = sb(pes, [128, 512], BF16)
    winb4 = sb(pes, [128, 512], BF16)
    identf = sb(pes, [128, 128])
    jidx1 = sb(pes, [128, 64])
    curm2 = sb(pes, [128, 32])
    fw.dma("sp", ovl[:], CT["ovl"], writes=[t_tab])
    fw.dma("sp", ng[:], DR["ng_d"].rearrange("(t p) c -> p t c", p=128), writes=[t_tab])
    fw.dma("sp", cmaskb4[:], CT["cmaskb4"], writes=[t_tab])
    fw.dma("sp", causb4[:], CT["causb4"], writes=[t_tab])
    fw.dma("sp", winb4[:], CT["winb4"], writes=[t_tab])
    fw.dma("sp", jidx1[:], CT["jidx1"], writes=[t_tab])
    fw.dma("sp", identf[:], CT["identf"], writes=[t_tab])
    fw.dma("sp", curm2[:], CT["curm2"], writes=[t_tab])

    NPS = 3
    pS = [psum(pes, [128, 512]) for _ in range(NPS)]
    t_pS = [Tok() for _ in range(NPS)]
    pOc = psum(pes, [128, 4, 65])
    t_pOc = Tok()
    pImp = psum(pes, [128, 4, 64])
    t_pImp = Tok()
    pOs = psum(pes, [128, 4, 65])
    t_pOs = Tok()
    pOwb = psum(pes, [128, 512])
    pOw = pOwb[:, 0:260].rearrange("p (r c) -> p r c", r=4)
    pNsT = pOwb[:, 384:512]
    t_pOw = Tok()
    pTT = psum(pes, [128, 4, 128], BF16)
    t_pT = Tok()
    t_pT = Tok()
    NPT = 4
    PT = [sb(pes, [128, 512], BF16) for _ in range(NPT)]
    t_PT = [Tok() for _ in range(NPT)]
    elig = [sb(pes, [128, 64]) for _ in range(2)]
    t_el = [Tok(), Tok()]
    rd = [[sb(pes, [128, 4, 3]) for _ in range(2)] for _ in range(2)]
    t_rd = [[Tok(), Tok()], [Tok(), Tok()]]
    accc = [[sb(pes, [128, 4, 64]) for _ in range(2)] for _ in range(2)]
    t_ac = [[Tok(), Tok()], [Tok(), Tok()]]
    imp = sb(pes, [128, 64])
    sc1 = sb(pes, [128, 64])
    sc2 = sb(pes, [128, 64])
    m8 = sb(pes, [128, 16])
    t_tk = Tok()
    ns_both = [sb(pes, [128, 128]) for _ in range(2)]
    t_ns = [Tok(), Tok()]
    Qaug = [[sb(pes, [128, 4, 128], BF16) for _ in range(2)] for _ in range(2)]
    t_qa = [[Tok(), Tok()], [Tok(), Tok()]]
    coef = sb(pes, [128, 12])
    bs_ = sb(pes, [128, 4, 64])
    bw_ = sb(pes, [128, 4, 64])
    bc_ = sb(pes, [128, 4, 64])
    t_bs, t_bw, t_bc = Tok(), Tok(), Tok()
    acc = sb(pes, [128, 64])
    t_cb = Tok()
    otok = [sb(pes, [128, 512], BF16) for _ in range(2)]
    t_ot = [Tok(), Tok()]
    oTs = [sb(pes, [128, 4, 512], BF16) for _ in range(2)]
    t_oTs = [Tok(), Tok()]
    onsa_v = DR["onsaT_d"].rearrange("c p t -> p c t")
    cnt = dict(s=0, p=0)
    pend = []

    def flush():
        while pend:
            pend.pop(0)()

    def tile(lhsT, lhs_reads, qsl, extra, pv, rhs=None, rhs_reads=None):
        i = cnt["s"] % NPS
        cnt["s"] += 1
        j = cnt["p"] % NPT
        cnt["p"] += 1
        rhs_ = QT[:, :, qsl] if rhs is None else rhs
        fw.op("pe", lambda e: mm(e, pS[i][:].rearrange("p (r q) -> p r q", r=4), lhsT=lhsT, rhs=rhs_, start=True, stop=(len(extra) == 0)),
              reads=lhs_reads + ([t_QT] if rhs is None else rhs_reads), writes=[t_pS[i]])
        for xi, (xl, xr_, xreads) in enumerate(extra):
            fw.op("pe", lambda e: mm(e, pS[i][:], lhsT=xl, rhs=xr_, start=False, stop=(xi == len(extra) - 1)),
                  reads=xreads, writes=[t_pS[i]])
        fw.op("act", lambda e: e.activation(out=PT[j][:], in_=pS[i][:], func=AF.Exp), reads=[t_pS[i]], writes=[t_PT[j]])
        while len(pend) > 1:
            pend.pop(0)()
        pend.append(lambda: pv(j))

    def stage_a(qt):
        p = qt % 2
        qsl = slice(qt * 128, (qt + 1) * 128)
        fw.op("dve", lambda e: e.tensor_scalar(out=elig[p][:], in0=jidx1[:], scalar1=curm2[:, qt:qt + 1], scalar2=None, op0=ALU.is_le),
              reads=[t_tab], writes=[t_el[p]])
        for g in range(2):
            nts = [nt for nt in range(2) if qt - 16 * nt >= 0]
            for ni, nt in enumerate(nts):
                dlt = qt - 16 * nt
                extra = [(ident[:], cmaskb4[:, dlt, :], [t_const, t_tab])] if dlt <= 16 else []

                def pv_c(j, ni=ni, nt=nt, g=g, last=(ni == len(nts) - 1)):
                    for r in range(4):
                        fw.op("pe", lambda e: mm(e, pOc[:, r, :], lhsT=PT[j][:, r * 128:(r + 1) * 128], rhs=Vc[:, nt, g, :],
                                                 start=(ni == 0 and r == 0), stop=last), reads=[t_PT[j], t_V], writes=[t_pOc])
                        fw.op("pe", lambda e: mm(e, pImp[:, r, :], lhsT=PT[j][:, r * 128:(r + 1) * 128], rhs=ovl[:, nt, :],
                                                 start=(ni == 0 and r == 0), stop=last), reads=[t_PT[j], t_tab], writes=[t_pImp])
                if ni == len(nts) - 1:
                    tile(Kc[:, g, nt * 128:(nt + 1) * 128], [t_Kc], qsl, extra,
                         lambda j, pv_c=pv_c, g=g: (pv_c(j), chain_a(qt, g)))
                else:
                    tile(Kc[:, g, nt * 128:(nt + 1) * 128], [t_Kc], qsl, extra, pv_c)

    def chain_a(qt, g):
        if True:
            p = qt % 2
            rdg, t_rdg = rd[p][g], t_rd[p][g]
            fw.op("dve", lambda e: e.tensor_scalar(out=rdg[:, :, 0], in0=pOc[:, :, 64], scalar1=1e-30, scalar2=None, op0=ALU.max),
                  reads=[t_pOc], writes=[t_rdg])
            fw.op("dve", lambda e: e.reciprocal(out=rdg[:, :, 0], in_=rdg[:, :, 0]), reads=[t_rdg], writes=[t_rdg])
            fw.op("dve", lambda e: e.tensor_scalar(out=imp[:], in0=pImp[:, 0, :], scalar1=rdg[:, 0, 0:1], scalar2=None, op0=ALU.mult),
                  reads=[t_pImp, t_rdg], writes=[t_tk])
            for r in range(1, 4):
                fw.op("dve", lambda e: e.scalar_tensor_tensor(out=imp[:], in0=pImp[:, r, :], scalar=rdg[:, r, 0:1], in1=imp[:],
                                                              op0=ALU.mult, op1=ALU.add), reads=[t_pImp, t_rdg, t_tk], writes=[t_tk])
            for r in range(4):
                fw.op("dve", lambda e: e.tensor_scalar(out=accc[p][g][:, r, :], in0=pOc[:, r, 0:64], scalar1=rdg[:, r, 0:1], scalar2=None, op0=ALU.mult),
                      reads=[t_pOc, t_rdg], writes=[t_ac[p][g]])
            fw.op("dve", lambda e: e.scalar_tensor_tensor(out=sc1[:], in0=imp[:], scalar=1.0, in1=elig[p][:], op0=ALU.add, op1=ALU.mult),
                  reads=[t_tk, t_el[p]], writes=[t_tk])
            fw.op("dve", lambda e: e.tensor_scalar(out=sc1[:], in0=sc1[:], scalar1=-1.0, scalar2=None, op0=ALU.add), reads=[t_tk], writes=[t_tk])
            fw.op("dve", lambda e: e.max(out=m8[:, 0:8], in_=sc1[:]), reads=[t_tk], writes=[t_tk])
            fw.op("dve", lambda e: e.match_replace(out=sc2[:], in_to_replace=m8[:, 0:8], in_values=sc1[:], imm_value=-2.0),
                  reads=[t_tk], writes=[t_tk])
            fw.op("dve", lambda e: e.max(out=m8[:, 8:16], in_=sc2[:]), reads=[t_tk], writes=[t_tk])
            fw.op("dve", lambda e: e.scalar_tensor_tensor(out=ns_both[p][:, (1 - g) * 64:(2 - g) * 64], in0=sc1[:], scalar=m8[:, 12:13], in1=elig[p][:],
                                                          op0=ALU.is_lt, op1=ALU.mult), reads=[t_tk, t_el[p]], writes=[t_ns[p]])
            fw.op("dve", lambda e: e.tensor_scalar(out=rdg[:, :, 0], in0=rdg[:, :, 0], scalar1=0.0, scalar2=1.0, op0=ALU.mult, op1=ALU.add),
                  reads=[t_rdg], writes=[t_rdg])

    def stage_b(qt, mid=None):
        p = qt % 2
        qsl = slice(qt * 128, (qt + 1) * 128)
        fw.op("pe", lambda e: e.transpose(out=pNsT, in_=ns_both[p][:], identity=identf[:]), reads=[t_ns[p], t_tab], writes=[t_pOw])
        for g in range(2):
            own = slice(g * 64, (g + 1) * 64)
            spare = slice((1 - g) * 64, (2 - g) * 64)
            fw.op("dve", lambda e: e.tensor_copy(out=Qaug[p][g][spare, :, :], in_=pNsT[spare, :].unsqueeze(1).to_broadcast([64, 4, 128])),
                  reads=[t_pOw], writes=[t_qa[p][g]])
            fw.op("pool", lambda e: e.tensor_copy(out=Qaug[p][g][own, :, :], in_=QT[own, :, qsl]), reads=[t_QT], writes=[t_qa[p][g]])
        for g in range(2):
            if g == 1 and mid is not None:
                mid()
            kts = list(range(max(0, qt - 4), qt + 1))
            for ki, kt in enumerate(kts):
                extra = []
                if kt == qt:
                    extra.append((ident[:], causb4[:], [t_const, t_tab]))
                elif kt == qt - 4:
                    extra.append((ident[:], winb4[:], [t_const, t_tab]))

                def pv_w(j, ki=ki, kt=kt, g=g, last=(ki == len(kts) - 1)):
                    for r in range(4):
                        fw.op("pe", lambda e: mm(e, pOw[:, r, :], lhsT=PT[j][:, r * 128:(r + 1) * 128], rhs=Vw[:, kt, g, :],
                                                 start=(ki == 0 and r == 0), stop=last), reads=[t_PT[j], t_V], writes=[t_pOw])
                tile(Kpad[:, 2 + g, kt * 128:(kt + 1) * 128], [t_K], qsl, extra, pv_w)
            for kt in range(qt + 1):
                extra = []
                if kt == qt:
                    extra.append((ident[:], causb4[:], [t_const, t_tab]))

                def pv_s(j, kt=kt, g=g, last=(kt == qt)):
                    for r in range(4):
                        fw.op("pe", lambda e: mm(e, pOs[:, r, :], lhsT=PT[j][:, r * 128:(r + 1) * 128], rhs=Vs[:, kt, g, :],
                                                 start=(kt == 0 and r == 0), stop=last), reads=[t_PT[j], t_V], writes=[t_pOs])
                if kt == qt:
                    tile(Kpad[:, g, kt * 128:(kt + 1) * 128], [t_K], qsl, extra,
                         lambda j, pv_s=pv_s, g=g: (pv_s(j), combine_b(qt, g)), rhs=Qaug[p][g][:], rhs_reads=[t_qa[p][g]])
                else:
                    tile(Kpad[:, g, kt * 128:(kt + 1) * 128], [t_K], qsl, extra, pv_s, rhs=Qaug[p][g][:], rhs_reads=[t_qa[p][g]])

    def combine_b(qt, g):
        if True:
            p = qt % 2
            rdg, t_rdg = rd[p][g], t_rd[p][g]
            fw.op("dve", lambda e: e.reciprocal(out=rdg[:, :, 1], in_=pOs[:, :, 64]), reads=[t_pOs], writes=[t_rdg])
            fw.op("dve", lambda e: e.reciprocal(out=rdg[:, :, 2], in_=pOw[:, :, 64]), reads=[t_pOw], writes=[t_rdg])
            fw.op("dve", lambda e: e.tensor_tensor(out=coef[:], in0=ng[:, qt, g * 12:(g + 1) * 12], in1=rdg[:].rearrange("p r b -> p (r b)"),
                                                   op=ALU.mult), reads=[t_tab, t_rdg], writes=[t_cb])
            cf = coef[:].rearrange("p (r b) -> p r b", b=3)
            fw.op("dve", lambda e: e.tensor_tensor(out=bs_[:], in0=pOs[:, :, 0:64], in1=cf[:, :, 1:2].to_broadcast([128, 4, 64]), op=ALU.mult),
                  reads=[t_pOs, t_cb], writes=[t_bs])
            fw.op("dve", lambda e: e.tensor_tensor(out=bw_[:], in0=pOw[:, :, 0:64], in1=cf[:, :, 2:3].to_broadcast([128, 4, 64]), op=ALU.mult),
                  reads=[t_pOw, t_cb], writes=[t_bw])
            fw.op("pool", lambda e: e.tensor_tensor(out=bc_[:], in0=accc[p][g][:], in1=cf[:, :, 0:1].to_broadcast([128, 4, 64]), op=ALU.mult),
                  reads=[t_ac[p][g], t_cb], writes=[t_bc])
            fw.op("pool", lambda e: e.tensor_tensor(out=bc_[:], in0=bc_[:], in1=bs_[:], op=ALU.add), reads=[t_bs, t_bc], writes=[t_bc])
            fw.op("pool", lambda e: e.tensor_tensor(out=otok[p][:, g * 256:(g + 1) * 256].rearrange("p (r d) -> p r d", r=4), in0=bc_[:], in1=bw_[:],
                                                    op=ALU.add), reads=[t_bc, t_bw], writes=[t_ot[p]])

    def stage_t(qt):
        p = qt % 2
        sb_ = (qt // 4) % 2
        for c in range(4):
            fw.op("pe", lambda e: e.transpose(out=pTT[:, c, :], in_=otok[p][:, c * 128:(c + 1) * 128], identity=ident[:]),
                  reads=[t_ot[p], t_const], writes=[t_pT])
        fw.op("dve", lambda e: e.tensor_copy(out=oTs[sb_][:, :, (qt % 4) * 128:(qt % 4 + 1) * 128], in_=pTT[:]),
              reads=[t_pT], writes=[t_oTs[sb_]])
        if qt % 4 == 3:
            fw.dma("sp", onsa_v[:, :, (qt // 4) * 512:(qt // 4 + 1) * 512], oTs[sb_][:], reads=[t_oTs[sb_]])

    NQ = S // 128
    stage_a(0)
    for qt in range(NQ):
        if qt + 1 < NQ:
            stage_a(qt + 1)
        else:
            flush()
        stage_b(qt, (lambda q=qt - 1: stage_t(q)) if qt >= 1 else None)
    flush()
    stage_t(NQ - 1)


def p2_gla(nc, fw, pes, l, W, w_in, hT, t_hT, pm, t_pm, pa, t_pa, wt, t_w, DR, sb, psum):
    CT = DR["CT"]
    t_c = Tok()
    segmask = sb(pes, [128, 512])
    trim = sb(pes, [128, 128])
    wg17 = sb(pes, [17, 256], BF16)
    ngb = sb(pes, [128, 128])
    fw.dma("sp", segmask[:], CT["segmask"], writes=[t_c])
    fw.dma("sp", trim[:], CT["trim"], writes=[t_c])
    wload(fw, wg17[:], W["gla_wg"][l], t_c)
    fw.dma("sp", ngb[:], W["gla_norm"][l].partition_broadcast(128), writes=[t_c])
    glowT = sb(pes, [17, S], BF16)
    t_gl = Tok()
    fw.op("pool", lambda e: e.memset(glowT[:], 1.0), writes=[t_gl])
    wlow = sb(pes, [128, 8, 16], BF16)
    wload(fw, wlow[:], w_in[:, :, O_GLOW:O_GLOW + 16], t_c)
    for T in range(S // 512):
        i = T % 2
        tsl = slice(T * 512, (T + 1) * 512)
        for k in range(8):
            fw.op("pe", lambda e: e.matmul(pm[i][0:16, :], lhsT=wlow[:, k, :], rhs=hT[:, k, tsl], start=(k == 0), stop=(k == 7)),
                  reads=[t_c, t_hT], writes=[t_pm[i]])
        fw.op("act", lambda e: e.activation(out=glowT[0:16, tsl], in_=pm[i][0:16, :], func=AF.Copy), reads=[t_pm[i]], writes=[t_gl])
    la = [sb(pes, [128, 512]) for _ in range(2)]
    t_la = [Tok(), Tok()]
    cs = [sb(pes, [128, 512]) for _ in range(2)]
    t_cs_ = [Tok(), Tok()]
    eb = [sb(pes, [128, 512]) for _ in range(2)]
    t_eb = [Tok(), Tok()]
    enb = [sb(pes, [128, 512]) for _ in range(2)]
    t_enb = [Tok(), Tok()]
    qo = [sb(pes, [128, 512], BF16) for _ in range(2)]
    t_qo = [Tok(), Tok()]
    ko = [sb(pes, [128, 512], BF16) for _ in range(2)]
    t_ko = [Tok(), Tok()]
    edec = sb(pes, [128, 2, 32])
    t_ed = Tok()
    px = [psum(pes, [128, 512]) for _ in range(2)]
    t_px = [Tok(), Tok()]
    wq2 = sb(pes, [128, 8, 128], BF16)
    t_wq2 = Tok()
    it = 0
    for hc in range(2):
        wload(fw, wt[0][:], w_in[:, :, O_GQ + hc * 128:O_GQ + (hc + 1) * 128], t_w[0])
        wload(fw, wq2[:], w_in[:, :, O_GK + hc * 128:O_GK + (hc + 1) * 128], t_wq2)
        for T in range(S // 512):
            i = it % 2
            it += 1
            tsl = slice(T * 512, (T + 1) * 512)
            fw.op("pe", lambda e: e.matmul(px[i][:], lhsT=wg17[:, hc * 128:(hc + 1) * 128], rhs=glowT[:, tsl], start=True, stop=True),
                  reads=[t_c, t_gl], writes=[t_px[i]])
            fw.op("act", lambda e: e.activation(out=la[i][:], in_=px[i][:], func=AF.Exp, scale=-1.0), reads=[t_px[i]], writes=[t_la[i]])
            fw.op("act", lambda e: e.activation(out=la[i][:], in_=la[i][:], func=AF.Ln, bias=1.0), reads=[t_la[i]], writes=[t_la[i]])
            fw.op("dve", lambda e: e.tensor_tensor_scan(out=cs[i][:], data0=segmask[:], data1=la[i][:], initial=0.0, op0=ALU.mult, op1=ALU.add),
                  reads=[t_la[i], t_c], writes=[t_cs_[i]])
            fw.op("act", lambda e: e.activation(out=eb[i][:], in_=cs[i][:], func=AF.Exp, scale=-1.0 / 16), reads=[t_cs_[i]], writes=[t_eb[i]])
            fw.op("act", lambda e: e.activation(out=enb[i][:], in_=cs[i][:], func=AF.Exp, scale=1.0 / 16), reads=[t_cs_[i]], writes=[t_enb[i]])
            fw.op("dve", lambda e: e.tensor_copy(out=edec[:, hc, T * 4:(T + 1) * 4], in_=eb[i][:, 127:512:128]), reads=[t_eb[i]], writes=[t_ed])
            for k in range(8):
                fw.op("pe", lambda e: e.matmul(pm[i][:], lhsT=wt[0][:, k, :], rhs=hT[:, k, tsl], start=(k == 0), stop=(k == 7)),
                      reads=[t_w[0], t_hT], writes=[t_pm[i]])
            fw.op("dve", lambda e: e.scalar_tensor_tensor(out=qo[i][:], in0=pm[i][:], scalar=0.125, in1=eb[i][:], op0=ALU.mult, op1=ALU.mult),
                  reads=[t_pm[i], t_eb[i]], writes=[t_qo[i]])
            fw.dma("sp", DR["qeT_d"][hc][:, tsl], qo[i][:], reads=[t_qo[i]])
            for k in range(8):
                fw.op("pe", lambda e: e.matmul(pa[i][:], lhsT=wq2[:, k, :], rhs=hT[:, k, tsl], start=(k == 0), stop=(k == 7)),
                      reads=[t_wq2, t_hT], writes=[t_pa[i]])
            fw.op("dve", lambda e: e.tensor_tensor(out=ko[i][:], in0=pa[i][:], in1=enb[i][:], op=ALU.mult),
                  reads=[t_pa[i], t_enb[i]], writes=[t_ko[i]])
            fw.dma("sp", DR["keT_d"][hc][:, tsl], ko[i][:], reads=[t_ko[i]])
    fw.dma("sp", DR["edec_d"], edec[:], reads=[t_ed])
    wk = sb(pes, [128, 8, 256], BF16)
    wv = sb(pes, [128, 8, 512], BF16)
    wr = sb(pes, [128, 8, 512], BF16)
    t_wtm = Tok()
    wload(fw, wk[:], w_in[:, :, O_GK:O_GK + 256], t_wtm)
    for k in range(0, 8, 4):
        wload(fw, wv[:, k:k + 4, :], w_in[:, k:k + 4, O_GV:O_GV + 512], t_wtm)
        wload(fw, wr[:, k:k + 4, :], w_in[:, k:k + 4, O_GR:O_GR + 512], t_wtm)
    pd_ = psum(pes, [128, 256])
    t_pd = Tok()
    lat = [sb(pes, [128, 256]) for _ in range(2)]
    t_lat = [Tok(), Tok()]
    dex = [sb(pes, [128, 256]) for _ in range(2)]
    t_dex = [Tok(), Tok()]
    sg = [sb(pes, [128, 512]) for _ in range(2)]
    t_sg = [Tok(), Tok()]
    NST = 2
    klst = [sb(pes, [128, NST, 256], BF16) for _ in range(2)]
    vst = [sb(pes, [128, NST, 512], BF16) for _ in range(2)]
    grst = [sb(pes, [128, NST, 512], BF16) for _ in range(2)]
    t_st = [Tok(), Tok()]
    kl_v = DR["kl_d"].rearrange("(t p) c -> p t c", p=128)
    gv_v = DR["gv_d"].rearrange("(t p) c -> p t c", p=128)
    gr_v = DR["GR_d"].rearrange("(t p) c -> p t c", p=128)
    def gate_front(tt):
        i = tt % 2
        tok = slice(tt * 128, (tt + 1) * 128)
        fw.op("pe", lambda e: e.matmul(px[i][:, 0:256], lhsT=glowT[:, tok], rhs=wg17[:], start=True, stop=True),
              reads=[t_c, t_gl], writes=[t_px[i]])
        fw.op("act", lambda e: e.activation(out=lat[i][:], in_=px[i][:, 0:256], func=AF.Exp, scale=-1.0), reads=[t_px[i]], writes=[t_lat[i]])
        fw.op("act", lambda e: e.activation(out=lat[i][:], in_=lat[i][:], func=AF.Ln, bias=1.0), reads=[t_lat[i]], writes=[t_lat[i]])

    gate_front(0)
    for tt in range(S // 128):
        i = tt % 2
        st = (tt // NST) % 2
        si = tt % NST
        tok = slice(tt * 128, (tt + 1) * 128)
        for k in range(8):
            fw.op("pe", lambda e: e.matmul(pa[i][:, 0:256], lhsT=hT[:, k, tok], rhs=wk[:, k, :], start=(k == 0), stop=(k == 7)),
                  reads=[t_wtm, t_hT], writes=[t_pa[i]])
        if tt + 1 < S // 128:
            gate_front(tt + 1)
        fw.op("pe", lambda e: e.matmul(pd_[:], lhsT=trim[:], rhs=lat[i][:], start=True, stop=True), reads=[t_c, t_lat[i]], writes=[t_pd])
        fw.op("act", lambda e: e.activation(out=dex[i][:], in_=pd_[:], func=AF.Exp, scale=1.0 / 16), reads=[t_pd], writes=[t_dex[i]])
        fw.op("dve", lambda e: e.tensor_tensor(out=klst[st][:, si, :], in0=pa[i][:, 0:256], in1=dex[i][:], op=ALU.mult),
              reads=[t_pa[i], t_dex[i]], writes=[t_st[st]])
        for k in range(8):
            fw.op("pe", lambda e: e.matmul(pm[i][:], lhsT=hT[:, k, tok], rhs=wv[:, k, :], start=(k == 0), stop=(k == 7)),
                  reads=[t_wtm, t_hT], writes=[t_pm[i]])
        fw.op("act", lambda e: e.activation(out=vst[st][:, si, :], in_=pm[i][:], func=AF.Copy), reads=[t_pm[i]], writes=[t_st[st]])
        if si == NST - 1:
            g0 = tt - (NST - 1)
            fw.dma("sp", kl_v[:, g0:g0 + NST, :], klst[st][:], reads=[t_st[st]])
            fw.dma("sp", gv_v[:, g0:g0 + NST, :], vst[st][:], reads=[t_st[st]])
    t_st2 = [Tok(), Tok()]
    for tt in range(S // 128):
        i = tt % 2
        st = (tt // NST) % 2
        si = tt % NST
        tok = slice(tt * 128, (tt + 1) * 128)
        for k in range(8):
            fw.op("pe", lambda e: e.matmul(px[i][:], lhsT=hT[:, k, tok], rhs=wr[:, k, :], start=(k == 0), stop=(k == 7)),
                  reads=[t_wtm, t_hT], writes=[t_px[i]])
        fw.op("act", lambda e: e.activation(out=sg[i][:], in_=px[i][:], func=AF.Sigmoid), reads=[t_px[i]], writes=[t_sg[i]])
        fw.op("dve", lambda e: e.tensor_tensor(out=sg[i][:], in0=px[i][:], in1=sg[i][:], op=ALU.mult), reads=[t_px[i], t_sg[i]], writes=[t_sg[i]])
        fw.op("pool", lambda e: e.tensor_tensor(out=grst[st][:, si, :].rearrange("p (h v) -> p h v", h=4), in0=sg[i][:].rearrange("p (h v) -> p h v", h=4),
                                                in1=ngb[:].unsqueeze(1).to_broadcast([128, 4, 128]), op=ALU.mult),
              reads=[t_sg[i], t_c], writes=[t_st2[st]])
        if si == NST - 1:
            g0 = tt - (NST - 1)
            fw.dma("sp", gr_v[:, g0:g0 + NST, :], grst[st][:], reads=[t_st2[st]])


def p4_gla(nc, fw, pes, l, DR, CT, CS, sb, psum):
    ident, t_const = CS["ident"], CS["t_const"]
    mm = lambda e, *a, **k: e.matmul(*a, skip_group_check=True, **k)
    NB = 4
    t_blk = [Tok() for _ in range(NB)]
    t_misc = Tok()
    qeP = sb(pes, [128, 4, S], BF16)
    keP = sb(pes, [128, 4, S], BF16)
    klP = sb(pes, [128, 32, 4, 128], BF16)
    gv = sb(pes, [128, 32, 512], BF16)
    GR = sb(pes, [128, 32, 512], BF16)
    edec = sb(pes, [128, 2, 32])
    tril = sb(pes, [128, 128])
    fw.dma("sp", edec[:], DR["edec_d"], writes=[t_misc])
    fw.dma("sp", tril[:], CT["tril"], writes=[t_misc])
    kl_v = DR["kl_d"].rearrange("(t p) c -> p t c", p=128)
    gv_v = DR["gv_d"].rearrange("(t p) c -> p t c", p=128)
    gr_v = DR["GR_d"].rearrange("(t p) c -> p t c", p=128)
    for bk in range(NB):
        tk = slice(bk * 1024, (bk + 1) * 1024)
        t0 = bk * 8
        tb = t_blk[bk]
        for i in range(4):
            fw.op("dve", lambda e: e.memset(qeP[:, i, tk], 0.0), writes=[tb])
            fw.op("pool", lambda e: e.memset(keP[:, i, tk], 0.0), writes=[tb])
        fw.op("dve", lambda e: e.memset(klP[:, t0:t0 + 8, :, :], 0.0), writes=[tb])
        for hc in range(2):
            for hh in range(2):
                hr = slice(hh * 64, (hh + 1) * 64)
                fw.dma("sp", qeP[hr, hh * 2 + hc, tk], DR["qeT_d"][hc][hr, tk], writes=[tb])
                fw.dma("sp", keP[hr, hh * 2 + hc, tk], DR["keT_d"][hc][hr, tk], writes=[tb])
        for h in range(4):
            fw.dma("sp", klP[:, t0:t0 + 8, h, (h % 2) * 64:(h % 2 + 1) * 64], kl_v[:, t0:t0 + 8, h * 64:(h + 1) * 64], writes=[tb])
        fw.dma("sp", gv[:, t0:t0 + 8, :], gv_v[:, t0:t0 + 8, :], writes=[tb])
        fw.dma("sp", GR[:, t0:t0 + 8, :], gr_v[:, t0:t0 + 8, :], writes=[tb])
    stf = sb(pes, [128, 2, 128])
    stb = [sb(pes, [128, 2, 128], BF16) for _ in range(2)]
    t_stf = Tok()
    t_stb = [Tok(), Tok()]
    fw.op("pool", lambda e: e.memset(stf[:], 0.0), writes=[t_stf])
    fw.op("pool", lambda e: e.memset(stb[1][:], 0.0), writes=[t_stb[1]])
    pA = [psum(pes, [128, 4, 128]) for _ in range(2)]
    t_pA = [Tok(), Tok()]
    pO = [psum(pes, [128, 512]) for _ in range(2)]
    t_pO = [Tok(), Tok()]
    pSt = psum(pes, [128, 2, 128])
    t_pSt = Tok()
    pTo = psum(pes, [128, 4, 128], BF16)
    t_pTo = Tok()
    AT = [sb(pes, [128, 4, 128], BF16) for _ in range(2)]
    t_AT = [Tok(), Tok()]
    junk = sb(pes, [128, 128], BF16)
    t_j = Tok()
    ss = [sb(pes, [128, 4]) for _ in range(2)]
    t_ss = [Tok(), Tok()]
    tg = [sb(pes, [128, 512]) for _ in range(2)]
    t_tg = [Tok(), Tok()]
    otok = [sb(pes, [128, 512], BF16) for _ in range(2)]
    t_ot = [Tok(), Tok()]
    oTs = [sb(pes, [128, 4, 512], BF16) for _ in range(2)]
    t_oTs = [Tok(), Tok()]
    ogla_v = DR["oglaT_d"].rearrange("c p t -> p c t")
    sk = Skew()
    for c in range(S // 128):
        csl = slice(c * 128, (c + 1) * 128)
        a = c % 2
        t_in = t_blk[c // 8]

        def s0(c=c, csl=csl, a=a, t_in=t_in):
            for h in range(4):
                hc, hh = h // 2, h % 2
                fw.op("pe", lambda e: mm(e, pA[a][:, h, :], lhsT=keP[:, hh * 2 + hc, csl], rhs=qeP[:, hh * 2 + hc, csl], start=(h == 0), stop=True),
                      reads=[t_in], writes=[t_pA[a]])
            fw.op("dve", lambda e: e.tensor_tensor(out=AT[a][:], in0=pA[a][:], in1=tril[:].unsqueeze(1).to_broadcast([128, 4, 128]), op=ALU.mult),
                  reads=[t_pA[a], t_misc], writes=[t_AT[a]])

        def s1(c=c, csl=csl, a=a, t_in=t_in):
            for h in range(4):
                hc, hh = h // 2, h % 2
                fw.op("pe", lambda e: mm(e, pSt[:, hc, :], lhsT=klP[:, c, h, :], rhs=gv[:, c, h * 128:(h + 1) * 128],
                                         start=(h == 0), stop=(hh == 1)), reads=[t_in], writes=[t_pSt])
            sprev, t_sprev = stb[(c - 1) % 2], t_stb[(c - 1) % 2]
            for h in range(4):
                fw.op("pe", lambda e: mm(e, pO[a][:, h * 128:(h + 1) * 128], lhsT=AT[a][:, h, :], rhs=gv[:, c, h * 128:(h + 1) * 128],
                                         start=(h == 0), stop=False), reads=[t_AT[a], t_in], writes=[t_pO[a]])
            for h in range(4):
                hc, hh = h // 2, h % 2
                fw.op("pe", lambda e: mm(e, pO[a][:, h * 128:(h + 1) * 128], lhsT=qeP[:, hh * 2 + hc, csl], rhs=sprev[:, hc, :],
                                         start=False, stop=True), reads=[t_in, t_sprev], writes=[t_pO[a]])
            for hc in range(2):
                fw.op("dve", lambda e: e.scalar_tensor_tensor(out=stf[:, hc, :], in0=stf[:, hc, :], scalar=edec[:, hc, c:c + 1], in1=pSt[:, hc, :],
                                                              op0=ALU.mult, op1=ALU.add), reads=[t_pSt, t_misc, t_stf], writes=[t_stf])
            fw.op("dve", lambda e: e.tensor_copy(out=stb[c % 2][:], in_=stf[:]), reads=[t_stf], writes=[t_stb[c % 2]])

        def s2(c=c, a=a, t_in=t_in):
            for h in range(4):
                fw.op("act", lambda e: e.activation(out=junk[:], in_=pO[a][:, h * 128:(h + 1) * 128], func=AF.Square, accum_out=ss[a][:, h:h + 1]),
                      reads=[t_pO[a]], writes=[t_j, t_ss[a]])
            fw.op("act", lambda e: e.activation(out=ss[a][:], in_=ss[a][:], func=AF.Sqrt, scale=1.0 / 128, bias=EPS), reads=[t_ss[a]], writes=[t_ss[a]])
            fw.op("dve", lambda e: e.reciprocal(out=ss[a][:], in_=ss[a][:]), reads=[t_ss[a]], writes=[t_ss[a]])
            fw.op("dve", lambda e: e.tensor_tensor(out=tg[a][:], in0=pO[a][:], in1=GR[:, c, :], op=ALU.mult), reads=[t_pO[a], t_in], writes=[t_tg[a]])
            fw.op("pool", lambda e: e.tensor_tensor(out=otok[a][:].rearrange("p (h v) -> p h v", h=4), in0=tg[a][:].rearrange("p (h v) -> p h v", h=4),
                                                    in1=ss[a][:].unsqueeze(2).to_broadcast([128, 4, 128]), op=ALU.mult),
                  reads=[t_tg[a], t_ss[a]], writes=[t_ot[a]])

        def s3(c=c, a=a):
            sb_ = (c // 4) % 2
            for cc in range(4):
                fw.op("pe", lambda e: e.transpose(out=pTo[:, cc, :], in_=otok[a][:, cc * 128:(cc + 1) * 128], identity=ident[:]),
                      reads=[t_ot[a], t_const], writes=[t_pTo])
            fw.op("act", lambda e: e.activation(out=oTs[sb_][:, :, (c % 4) * 128:(c % 4 + 1) * 128], in_=pTo[:], func=AF.Copy),
                  reads=[t_pTo], writes=[t_oTs[sb_]])
            if c % 4 == 3:
                fw.dma("sp", ogla_v[:, :, (c // 4) * 512:(c // 4 + 1) * 512], oTs[sb_][:], reads=[t_oTs[sb_]])

        sk.push([s0, s1, s2, s3])
    sk.drain()


def p5_mem(nc, fw, pes, l, W, mem_in, mqT_d, omemT_d, CS, sb, psum, norm_to_hT, after_w=None):
    ident, ones128, t_const = CS["ident"], CS["ones128"], CS["t_const"]
    gT = sb(pes, [128, 8])
    kn = sb(pes, [128, 1])
    t_g = Tok()
    fw.dma("sp", gT[:], W["mem_normT"][l], writes=[t_g])
    fw.dma("sp", kn[:], W["mem_kn"][l], writes=[t_g])
    mT = sb(pes, [128, 8, MEM], BF16)
    with ExitStack() as p1:
        t_mT = norm_to_hT(p1, mem_in, MEM // 128, gT, t_g, mT)
        fw.barrier()
    wkv = W["mem_w_kv"][l].rearrange("(c p) n -> p c n", p=128)
    kT = sb(pes, [128, 4, MEM], BF16)
    vaug = sb(pes, [128, 2, 4, 130], BF16)
    t_kT = Tok()
    t_v = Tok()
    fw.op("pool", lambda e: e.memset(vaug[:], 1.0), writes=[t_v])
    with ExitStack() as p2:
        wt = [sb(p2, [128, 8, 128], BF16) for _ in range(2)]
        t_w = [Tok(), Tok()]
        wv = sb(p2, [128, 8, 512], BF16)
        t_wv = Tok()
        pk = [psum(p2, [128, MEM]) for _ in range(2)]
        t_pk = [Tok(), Tok()]
        pa = [psum(p2, [128, MEM]) for _ in range(2)]
        t_pa = [Tok(), Tok()]
        pv = [psum(p2, [128, 512]) for _ in range(2)]
        t_pv = [Tok(), Tok()]
        sq = [sb(p2, [128, MEM], BF16) for _ in range(2)]
        t_sq = [Tok(), Tok()]
        rstd = [sb(p2, [128, MEM]) for _ in range(2)]
        t_rs = [Tok(), Tok()]
        wload(fw, wv[:], wkv[:, :, 512:1024], t_wv)
        for h in range(4):
            b = h % 2
            wload(fw, wt[b][:], wkv[:, :, h * 128:(h + 1) * 128], t_w[b])
            for k in range(8):
                fw.op("pe", lambda e: e.matmul(pk[b][:], lhsT=wt[b][:, k, :], rhs=mT[:, k, :], start=(k == 0), stop=(k == 7)),
                      reads=[t_w[b], t_mT], writes=[t_pk[b]])
            fw.op("act", lambda e: e.activation(out=sq[b][:], in_=pk[b][:], func=AF.Square), reads=[t_pk[b]], writes=[t_sq[b]])
            fw.op("pe", lambda e: e.matmul(pa[b][:], lhsT=ones128[:], rhs=sq[b][:], start=True, stop=True),
                  reads=[t_sq[b], t_const], writes=[t_pa[b]])
            fw.op("act", lambda e: e.activation(out=rstd[b][:], in_=pa[b][:], func=AF.Sqrt, scale=1.0 / 128, bias=EPS),
                  reads=[t_pa[b]], writes=[t_rs[b]])
            fw.op("dve", lambda e: e.reciprocal(out=rstd[b][:], in_=rstd[b][:]), reads=[t_rs[b]], writes=[t_rs[b]])
            fw.op("dve", lambda e: e.scalar_tensor_tensor(out=kT[:, h, :], in0=pk[b][:], scalar=kn[:, 0:1], in1=rstd[b][:],
                                                          op0=ALU.mult, op1=ALU.mult),
                  reads=[t_pk[b], t_rs[b], t_g], writes=[t_kT])
        for mt in range(2):
            for k in range(8):
                fw.op("pe", lambda e: e.matmul(pv[mt][:], lhsT=mT[:, k, mt * 128:(mt + 1) * 128], rhs=wv[:, k, :],
                                               start=(k == 0), stop=(k == 7)),
                      reads=[t_wv, t_mT], writes=[t_pv[mt]])
            fw.op("act", lambda e: e.activation(out=vaug[:, mt, :, 0:128], in_=pv[mt][:].rearrange("p (h d) -> p h d", h=4),
                                                func=AF.Copy), reads=[t_pv[mt]], writes=[t_v])
        fw.barrier()
    if after_w is not None:
        after_w()
    qt = [sb(pes, [128, 4, 512], BF16) for _ in range(2)]
    t_q = [Tok(), Tok()]
    ps_s = [psum(pes, [128, 2, 512]) for _ in range(2)]
    t_ps = [Tok(), Tok()]
    pT = [sb(pes, [128, 2, 512], BF16) for _ in range(2)]
    t_pT = [Tok(), Tok()]
    ps_o = [psum(pes, [128, 130]) for _ in range(2)]
    t_po = [Tok(), Tok()]
    rden = [sb(pes, [128, 1]) for _ in range(2)]
    t_rd = [Tok(), Tok()]
    otok = [[sb(pes, [128, 512], BF16) for _ in range(4)] for _ in range(2)]
    t_ot = [[Tok() for _ in range(4)] for _ in range(2)]
    ps_t = [psum(pes, [128, 4, 128], BF16) for _ in range(2)]
    t_pt = [Tok(), Tok()]
    oT = [sb(pes, [128, 4, 512], BF16) for _ in range(2)]
    t_oT = [Tok(), Tok()]
    omem_v = omemT_d.rearrange("c p t -> p c t")
    cnt = dict(o=0, j=0)
    sk = Skew()
    NT = S // 512
    fw.dma("sp", qt[0][:], mqT_d[:, :, 0:512], writes=[t_q[0]])
    for T in range(NT):
        b = T % 2
        tsl = slice(T * 512, (T + 1) * 512)
        if T + 1 < NT:
            fw.dma("sp", qt[1 - b][:], mqT_d[:, :, (T + 1) * 512:(T + 2) * 512], writes=[t_q[1 - b]])
        for h in range(4):
            j = cnt["j"] % 2
            cnt["j"] += 1

            def s0(h=h, j=j, b=b):
                for mt in range(2):
                    fw.op("pe", lambda e: e.matmul(ps_s[j][:, mt, :], lhsT=kT[:, h, mt * 128:(mt + 1) * 128], rhs=qt[b][:, h, :],
                                                   start=True, stop=True), reads=[t_kT, t_q[b]], writes=[t_ps[j]])
                fw.op("act", lambda e: e.activation(out=pT[j][:], in_=ps_s[j][:], func=AF.Exp), reads=[t_ps[j]], writes=[t_pT[j]])

            def s1(h=h, j=j, b=b):
                for qs in range(4):
                    o2 = cnt["o"] % 2
                    cnt["o"] += 1
                    for mt in range(2):
                        fw.op("pe", lambda e: e.matmul(ps_o[o2][:], lhsT=pT[j][:, mt, qs * 128:(qs + 1) * 128], rhs=vaug[:, mt, h, :],
                                                       start=(mt == 0), stop=(mt == 1)), reads=[t_pT[j], t_v], writes=[t_po[o2]])
                    fw.op("dve", lambda e: e.reciprocal(out=rden[o2][:], in_=ps_o[o2][:, 128:129]), reads=[t_po[o2]], writes=[t_rd[o2]])
                    fw.op("dve", lambda e: e.tensor_scalar(out=otok[b][qs][:, h * 128:(h + 1) * 128], in0=ps_o[o2][:, 0:128],
                                                           scalar1=rden[o2][:, 0:1], scalar2=None, op0=ALU.mult),
                          reads=[t_po[o2], t_rd[o2]], writes=[t_ot[b][qs]])

            def s2(b=b, tsl=tsl):
                for qs in range(4):
                    jj = qs % 2
                    for c in range(4):
                        fw.op("pe", lambda e: e.transpose(out=ps_t[jj][:, c, :], in_=otok[b][qs][:, c * 128:(c + 1) * 128], identity=ident[:]),
                              reads=[t_ot[b][qs], t_const], writes=[t_pt[jj]])
                    fw.op("act", lambda e: e.activation(out=oT[b][:, :, qs * 128:(qs + 1) * 128], in_=ps_t[jj][:], func=AF.Copy),
                          reads=[t_pt[jj]], writes=[t_oT[b]])
                fw.dma("sp", omem_v[:, :, tsl], oT[b][:], reads=[t_oT[b]])

            sk.push([s0, s1, s2] if h == 3 else [s0, s1])
    sk.drain()


def p6_load(nc, fw, pes, l, W, sb):
    w_in = W["w_in"][l].rearrange("(c p) n -> p c n", p=128)
    Wb = sb(pes, [128, 3, 4, D], BF16)
    Wm = sb(pes, [128, 8, 3 * D], BF16)
    Wo = sb(pes, [128, 8, D], BF16)
    bm = sb(pes, [128, 24])
    t_w = Tok()
    t_wm = [Tok() for _ in range(8)]
    t_wo = Tok()

    def issue():
        fw.dma("sp", bm[:], W["b_mergeT"][l], writes=[t_w])
        wbv = W["w_branch"][l].rearrange("b (k p) n -> p b k n", p=128)
        for d2 in range(4):
            for br in range(3):
                wload(fw, Wb[:, br, :, d2 * 256:(d2 + 1) * 256], wbv[:, br, :, d2 * 256:(d2 + 1) * 256], t_wm[d2])
                c0 = br * D + d2 * 256
                wload(fw, Wm[:, :, c0:c0 + 256], w_in[:, :, O_MERGE + c0:O_MERGE + c0 + 256], t_wm[d2])
        wov = W["w_out"][l].rearrange("(k p) n -> p k n", p=128)
        for k in range(0, 8, 4):
            wload(fw, Wo[:, k:k + 4, :], wov[:, k:k + 4, :], t_wo)

    return (Wb, Wm, Wo, bm, t_w, t_wm, t_wo), issue


def p6_merge(nc, fw, pes, l, W, hT_d, onsaT_d, oglaT_d, omemT_d, x_src, x1_d, sb, psum, w6):
    Wb, Wm, Wo, bm, t_w, t_wm, t_wo = w6
    hTt = [sb(pes, [128, 8, 512], BF16) for _ in range(2)]
    oTt = [sb(pes, [128, 3, 4, 512], BF16) for _ in range(2)]
    xt = [sb(pes, [128, 4, D]) for _ in range(2)]
    t_in = [Tok(), Tok()]
    t_x = [Tok(), Tok()]
    psB = [psum(pes, [128, 512]) for _ in range(3)]
    psG = [psum(pes, [128, 512]) for _ in range(3)]
    t_pB = [Tok() for _ in range(3)]
    t_pG = [Tok() for _ in range(3)]
    pso = [psum(pes, [128, 512]) for _ in range(2)]
    t_po = [Tok(), Tok()]
    gate = [sb(pes, [128, 512]) for _ in range(3)]
    t_gt = [Tok() for _ in range(3)]
    mg = [sb(pes, [128, 512]) for _ in range(3)]
    t_mg = [Tok() for _ in range(3)]
    mT = [sb(pes, [128, 8, 512], BF16) for _ in range(2)]
    t_mT = [Tok(), Tok()]
    xo = [sb(pes, [128, 512]) for _ in range(2)]
    t_xo = [Tok(), Tok()]
    hv = hT_d.rearrange("c p t -> p c t")
    ovs = [o.rearrange("c p t -> p c t") for o in (onsaT_d, oglaT_d, omemT_d)]
    oc = 0
    def load_in(T):
        b = T % 2
        tsl = slice(T * 512, (T + 1) * 512)
        fw.dma("sp", hTt[b][:], hv[:, :, tsl], writes=[t_in[b]])
        for br in range(3):
            fw.dma("sp", oTt[b][:, br, :, :], ovs[br][:, :, tsl], writes=[t_in[b]])
        fw.dma("sp", xt[b][:], x_src[tsl, :].rearrange("(s p) d -> p s d", p=128), writes=[t_x[b]])

    load_in(0)
    for T in range(S // 512):
        b = T % 2
        tsl = slice(T * 512, (T + 1) * 512)
        if T + 1 < S // 512:
            load_in(T + 1)
        for dc in range(8):
            dsl = slice(dc * 128, (dc + 1) * 128)
            for br in range(3):
                for k in range(4):
                    fw.op("pe", lambda e: e.matmul(psB[br][:], lhsT=Wb[:, br, k, dsl], rhs=oTt[b][:, br, k, :], start=(k == 0), stop=(k == 3)),
                          reads=[t_wm[dc // 2], t_in[b]], writes=[t_pB[br]])
                for k in range(8):
                    fw.op("pe", lambda e: e.matmul(psG[br][:], lhsT=Wm[:, k, br * D + dc * 128:br * D + (dc + 1) * 128], rhs=hTt[b][:, k, :],
                                                   start=(k == 0), stop=(k == 7)), reads=[t_wm[dc // 2], t_in[b]], writes=[t_pG[br]])
                fw.op("act", lambda e: e.activation(out=gate[br][:], in_=psG[br][:], func=AF.Sigmoid, bias=bm[:, br * 8 + dc:br * 8 + dc + 1]),
                      reads=[t_pG[br], t_w], writes=[t_gt[br]])
                fw.op("dve", lambda e: e.tensor_tensor(out=mg[br][:], in0=psB[br][:], in1=gate[br][:], op=ALU.mult),
                      reads=[t_pB[br], t_gt[br]], writes=[t_mg[br]])
            fw.op("pool", lambda e: e.tensor_tensor(out=mg[0][:], in0=mg[0][:], in1=mg[1][:], op=ALU.add),
                  reads=[t_mg[1]], writes=[t_mg[0]])
            fw.op("pool", lambda e: e.tensor_tensor(out=mT[b][:, dc, :], in0=mg[0][:], in1=mg[2][:], op=ALU.add),
                  reads=[t_mg[0], t_mg[2]], writes=[t_mT[b]])
        for ts_ in range(4):
            for dh in range(2):
                o2 = oc % 2
                oc += 1
                for k in range(8):
                    fw.op("pe", lambda e: e.matmul(pso[o2][:], lhsT=mT[b][:, k, ts_ * 128:(ts_ + 1) * 128], rhs=Wo[:, k, dh * 512:(dh + 1) * 512],
                                                   start=(k == 0), stop=(k == 7)), reads=[t_mT[b], t_wo], writes=[t_po[o2]])
                fw.op("dve", lambda e: e.tensor_tensor(out=xo[o2][:], in0=pso[o2][:], in1=xt[b][:, ts_, dh * 512:(dh + 1) * 512], op=ALU.add),
                      reads=[t_po[o2], t_x[b]], writes=[t_xo[o2]])
                fw.dma("sp", x1_d[T * 512 + ts_ * 128:T * 512 + (ts_ + 1) * 128, dh * 512:(dh + 1) * 512], xo[o2][:], reads=[t_xo[o2]])


def p7_mlp(nc, fw, pes, l, W, x1_d, x_dst, CS, sb, psum):
    ident, t_const = CS["ident"], CS["t_const"]
    Wu = sb(pes, [128, 8, 4 * D], BF16)
    Wd = sb(pes, [128, 32, D], BF16)
    gT = sb(pes, [128, 8])
    t_w = Tok()
    t_wu = [Tok() for _ in range(4)]
    t_wd = [Tok() for _ in range(8)]
    wuv = W["w_up"][l].rearrange("(k p) n -> p k n", p=128)
    wdv = W["w_down"][l].rearrange("(k p) n -> p k n", p=128)
    fw.dma("sp", gT[:], W["ln_mlpT"][l], writes=[t_w])
    for cb in range(4):
        for kh in range(0, 8, 4):
            wload(fw, Wu[:, kh:kh + 4, cb * 1024:(cb + 1) * 1024], wuv[:, kh:kh + 4, cb * 1024:(cb + 1) * 1024], t_wu[cb])
    for kb in range(8):
        wload(fw, Wd[:, kb * 4:kb * 4 + 4, :], wdv[:, kb * 4:kb * 4 + 4, :], t_wd[kb])
    xs = [sb(pes, [128, D]) for _ in range(2)]
    t_x = [Tok(), Tok()]
    xr = [sb(pes, [128, 512]) for _ in range(2)]
    t_xr = [Tok(), Tok()]
    junk = sb(pes, [128, D], BF16)
    t_j = Tok()
    ss = [sb(pes, [128, 1]) for _ in range(2)]
    t_s = [Tok(), Tok()]
    hn = [sb(pes, [128, D], BF16) for _ in range(2)]
    t_h = [Tok(), Tok()]
    pst = [psum(pes, [128, 8, 128], BF16) for _ in range(2)]
    t_p = [Tok(), Tok()]
    h2T = [sb(pes, [128, 8, 512], BF16) for _ in range(2)]
    t_h2 = [Tok(), Tok()]
    pu = [psum(pes, [128, 512]) for _ in range(2)]
    t_pu = [Tok(), Tok()]
    rl = [sb(pes, [128, 512]) for _ in range(2)]
    t_rl = [Tok(), Tok()]
    aT = sb(pes, [128, 32, 512], BF16)
    t_a = Tok()
    pd = [psum(pes, [128, 512]) for _ in range(2)]
    t_pd = [Tok(), Tok()]
    xo = [sb(pes, [128, 512]) for _ in range(2)]
    t_xo = [Tok(), Tok()]
    def norm_front(T, s_):
        i = (T * 4 + s_) % 2
        fw.dma("sp", xs[i][:], x1_d[T * 512 + s_ * 128:T * 512 + (s_ + 1) * 128, :], writes=[t_x[i]])
        fw.op("act", lambda e: e.activation(out=junk[:], in_=xs[i][:], func=AF.Square, accum_out=ss[i][:, 0:1]),
              reads=[t_x[i]], writes=[t_j, t_s[i]])
        fw.op("act", lambda e: e.activation(out=ss[i][:], in_=ss[i][:], func=AF.Sqrt, scale=1.0 / D, bias=EPS),
              reads=[t_s[i]], writes=[t_s[i]])
        fw.op("dve", lambda e: e.reciprocal(out=ss[i][:], in_=ss[i][:]), reads=[t_s[i]], writes=[t_s[i]])
        fw.op("dve", lambda e: e.tensor_scalar(out=hn[i][:], in0=xs[i][:], scalar1=ss[i][:, 0:1], scalar2=None, op0=ALU.mult),
              reads=[t_x[i], t_s[i]], writes=[t_h[i]])

    def norm_back(T, s_):
        i = (T * 4 + s_) % 2
        hb = T % 2
        for c in range(8):
            fw.op("pe", lambda e: e.transpose(out=pst[i][:, c, :], in_=hn[i][:, c * 128:(c + 1) * 128], identity=ident[:]),
                  reads=[t_h[i], t_const], writes=[t_p[i]])
        fw.op("dve", lambda e: e.tensor_tensor(out=h2T[hb][:, :, s_ * 128:(s_ + 1) * 128], in0=pst[i][:],
                                               in1=gT[:].unsqueeze(2).to_broadcast([128, 8, 128]), op=ALU.mult),
              reads=[t_p[i], t_w], writes=[t_h2[hb]])

    NT = S // 512
    for s_ in range(4):
        norm_front(0, s_)
        norm_back(0, s_)
    for T in range(NT):
        hb = T % 2
        for hc in range(32):
            i = hc % 2
            if T + 1 < NT and hc >= 6 and (hc - 6) % 6 == 0 and (hc - 6) // 6 < 4:
                norm_front(T + 1, (hc - 6) // 6)
            if T + 1 < NT and hc >= 9 and (hc - 9) % 6 == 0 and (hc - 9) // 6 < 4:
                norm_back(T + 1, (hc - 9) // 6)
            for k in range(8):
                fw.op("pe", lambda e: e.matmul(pu[i][:], lhsT=Wu[:, k, hc * 128:(hc + 1) * 128], rhs=h2T[hb][:, k, :], start=(k == 0), stop=(k == 7)),
                      reads=[t_wu[hc // 8], t_h2[hb]], writes=[t_pu[i]])
            fw.op("act", lambda e: e.activation(out=rl[i][:], in_=pu[i][:], func=AF.Relu), reads=[t_pu[i]], writes=[t_rl[i]])
            eng = "dve" if hc % 2 == 0 else "pool"
            fw.op(eng, lambda e: e.tensor_tensor(out=aT[:, hc, :], in0=rl[i][:], in1=rl[i][:], op=ALU.mult),
                  reads=[t_rl[i]], writes=[t_a])
        for s_ in range(4):
            for dh in range(2):
                i = (s_ * 2 + dh) % 2
                fw.dma("sp", xr[i][:], x1_d[T * 512 + s_ * 128:T * 512 + (s_ + 1) * 128, dh * 512:(dh + 1) * 512], writes=[t_xr[i]])
                for k in range(32):
                    fw.op("pe", lambda e: e.matmul(pd[i][:], lhsT=aT[:, k, s_ * 128:(s_ + 1) * 128], rhs=Wd[:, k, dh * 512:(dh + 1) * 512],
                                                   start=(k == 0), stop=(k == 31)), reads=[t_a, t_wd[k // 4]], writes=[t_pd[i]])
                fw.op("dve", lambda e: e.tensor_tensor(out=xo[i][:], in0=pd[i][:], in1=xr[i][:], op=ALU.add),
                      reads=[t_pd[i], t_xr[i]], writes=[t_xo[i]])
                fw.dma("sp", x_dst[T * 512 + s_ * 128:T * 512 + (s_ + 1) * 128, dh * 512:(dh + 1) * 512], xo[i][:], reads=[t_xo[i]])


def host_inputs(inp, b, nlayers=DEPTH, l0=0, x=None):
    f = np.float32
    L = nlayers
    sl = slice(l0, l0 + L)
    P = {k: np.asarray(v)[sl] for k, v in inp.items() if k not in ("x", "mem", "positions")}
    m = {
        "x": np.ascontiguousarray(inp["x"][b] if x is None else x, dtype=f),
        "mem": np.ascontiguousarray(inp["mem"][b], dtype=f),
        "positions": np.ascontiguousarray(np.asarray(inp["positions"][b]).reshape(1, S).astype(np.int32)),
        "ln_mixT": np.ascontiguousarray(P["ln_mix"].reshape(L, 8, 128).transpose(0, 2, 1), dtype=f),
        "w_in": np.ascontiguousarray(P["w_in"], dtype=f),
        "b_mergeT": np.ascontiguousarray(P["b_merge"].reshape(L, 3, 8, 128).transpose(0, 3, 1, 2).reshape(L, 128, 24), dtype=f),
        "gq": np.ascontiguousarray(np.tile(P["nsa_q_norm"], (1, 2)).reshape(L, 128, 1), dtype=f),
        "gk": np.ascontiguousarray(np.tile(P["nsa_k_norm"], (1, 1, 2)).transpose(0, 2, 1), dtype=f),
        "cmp_peT": np.ascontiguousarray(np.tile(P["cmp_pe"].transpose(0, 1, 3, 2), (1, 1, 2, 1)), dtype=f),
        "cmp_w1": np.ascontiguousarray(P["cmp_w1"], dtype=f),
        "cmp_w2": np.ascontiguousarray(P["cmp_w2"], dtype=f),
        "gla_wg": np.ascontiguousarray(np.concatenate([P["gla_w_gate"], P["gla_b_gate"][:, None, :]], axis=1), dtype=f),
        "gla_negb": np.ascontiguousarray(P["gla_b_gate"].reshape(L, 2, 128).transpose(0, 2, 1), dtype=f),
        "gla_norm": np.ascontiguousarray(P["gla_norm"].reshape(L, 1, 128), dtype=f),
        "mem_normT": np.ascontiguousarray(P["mem_norm"].reshape(L, 8, 128).transpose(0, 2, 1), dtype=f),
        "mem_w_kv": np.ascontiguousarray(P["mem_w_kv"], dtype=f),
        "mem_qn": np.ascontiguousarray(P["mem_q_norm"].reshape(L, 128, 1), dtype=f),
        "mem_kn": np.ascontiguousarray(P["mem_k_norm"].reshape(L, 128, 1), dtype=f),
        "w_branch": np.ascontiguousarray(P["w_branch"], dtype=f),
        "w_out": np.ascontiguousarray(P["w_out"], dtype=f),
        "ln_mlpT": np.ascontiguousarray(P["ln_mlp"].reshape(L, 8, 128).transpose(0, 2, 1), dtype=f),
        "w_up": np.ascontiguousarray(P["w_up"], dtype=f),
        "w_down": np.ascontiguousarray(P["w_down"], dtype=f),
    }
    for k, v in _consts().items():
        m["c_" + k] = v
    return m


FUSED = True
_CACHE = {}


def kernel(**inputs):
    nb = 8
    if FUSED:
        if "nc" not in _CACHE:
            _CACHE["nc"] = build(DEPTH)[0]
        nc = _CACHE["nc"]
        in_maps = [host_inputs(inputs, b) for b in range(nb)]
        res = run_bass_kernel_spmd(nc, in_maps, core_ids=list(range(nb)))
        return np.stack([np.asarray(r["y"], dtype=np.float32) for r in res.results], axis=0)
    if "nc1" not in _CACHE:
        _CACHE["nc1"] = build(1)[0]
    nc = _CACHE["nc1"]
    xs = [None] * nb
    for l in range(DEPTH):
        in_maps = [host_inputs(inputs, b, 1, l, xs[b]) for b in range(nb)]
        res = run_bass_kernel_spmd(nc, in_maps, core_ids=list(range(nb)))
        xs = [np.asarray(r["y"], dtype=np.float32) for r in res.results]
    return np.stack(xs, axis=0)
```
